# Optimizing a Trainium2 kernel written in Bass

```python
import jax, jax.numpy as jnp
from jax import lax
import numpy as np

D_MODEL = 1024
BATCH = 8
SEQ = 2048
DEPTH = 4
DEC_BATCH = 128
DEC_SEQ = 8
PAST_LEN = 8192
PAGE_SIZE = 128

N_A = DEPTH // 2
N_B = DEPTH - N_A
MEM_WIDTH = D_MODEL // 4
MIX_A = D_MODEL - MEM_WIDTH
HEAD_A = 64
H_A = MIX_A // HEAD_A
DECAY_LORA = 64
A_LORA = 64
VRES_LORA = 32
GATE_LORA = 160
GN_EPS = 64e-5
C_RWKV = 3 * MIX_A + DECAY_LORA + A_LORA + GATE_LORA
C_A = C_RWKV + MEM_WIDTH
N_MEM = 256
MEM_HEADS = 4
MEM_HEAD_DIM = MEM_WIDTH // MEM_HEADS
MEM_SCALE = MEM_HEAD_DIM ** -0.5
NOPE_DIM = 128
ROPE_DIM = 64
V_DIM = 128
H_B = MIX_A // V_DIM
Q_LORA = 256
KV_LORA = 256
C_B = Q_LORA + MEM_WIDTH
ROPE_BASE = 10000.0
Q_BLOCK = 128
ATTN_SCALE = (NOPE_DIM + ROPE_DIM) ** -0.5
D_FF = 2816
CONV_W = 3
RMS_EPS = 1e-6

kernel_name = 'yoco_rwkv7_mla_memxattn_convglu_step'


def rms_norm(x, g):
    xf = x.astype(jnp.float32)
    y = xf * lax.rsqrt(jnp.mean(xf * xf, axis=-1, keepdims=True) + RMS_EPS)
    return (y * g.astype(jnp.float32)).astype(x.dtype)


def rope(x, pos):
    half = ROPE_DIM // 2
    inv = ROPE_BASE ** (-jnp.arange(half, dtype=jnp.float32) / half)
    ang = pos.astype(jnp.float32)[:, None] * inv[None, :]
    cos = jnp.cos(ang)[None, :, None, :]
    sin = jnp.sin(ang)[None, :, None, :]
    xf = x.astype(jnp.float32)
    x1, x2 = xf[..., :half], xf[..., half:]
    return jnp.concatenate([x1 * cos - x2 * sin, x1 * sin + x2 * cos], axis=-1).astype(x.dtype)


def wkv7_scan(s0, r, decay, k, v, kk, a):
    def step(s, inp):
        r_t, w_t, k_t, v_t, kk_t, a_t = inp
        s_kk = jnp.einsum('bhvk,bhk->bhv', s, kk_t)
        s = (s * w_t[:, :, None, :] - s_kk[..., None] * (kk_t * a_t)[:, :, None, :]
             + v_t[..., None] * k_t[:, :, None, :])
        return s, jnp.einsum('bhvk,bhk->bhv', s, r_t)
    seq = (r, decay, k, v, kk, a)
    seq = tuple(jnp.swapaxes(t, 0, 1) for t in seq)
    s, y = lax.scan(step, s0, seq)
    return s, jnp.swapaxes(y, 0, 1)


def rwkv7_mix(xn, shift_prev, wkv_prev, v_first, w_in, mu, w_decay_up, w0, w_a_up, a0,
              w_g_up, k_k, k_a, r_k, lnx_w, lnx_b, vres):
    B, T, _ = xn.shape
    f32 = jnp.float32
    xs = jnp.concatenate([shift_prev[:, None, :].astype(xn.dtype), xn], axis=1)
    proj = xs @ w_in
    cur, prev = proj[:, 1:], proj[:, :-1]
    mixed = cur[..., :C_RWKV] + (prev[..., :C_RWKV] - cur[..., :C_RWKV]) * mu
    q_mem = cur[..., C_RWKV:C_A]
    cuts = [MIX_A, MIX_A + DECAY_LORA, 2 * MIX_A + DECAY_LORA, 3 * MIX_A + DECAY_LORA,
            3 * MIX_A + DECAY_LORA + A_LORA]
    r, wl, k, v, al, gl = jnp.split(mixed, cuts, axis=-1)
    w_log = -jax.nn.softplus(-(w0 + jnp.tanh(wl) @ w_decay_up).astype(f32)) - 0.5
    decay = jnp.exp(-jnp.exp(w_log))
    a = jax.nn.sigmoid((a0 + al @ w_a_up).astype(f32))
    g = jax.nn.sigmoid(gl) @ w_g_up
    if vres is not None:
        mu_v, w_v_up, v0 = vres
        vl = cur[..., C_A:] + (prev[..., C_A:] - cur[..., C_A:]) * mu_v
        v = v + (v_first - v) * jax.nn.sigmoid(v0 + vl @ w_v_up)
    heads = lambda t: t.reshape(B, T, H_A, HEAD_A)
    rf, kf, vf, af = heads(r.astype(f32)), heads(k.astype(f32)), heads(v.astype(f32)), heads(a)
    kk = kf * k_k.astype(f32).reshape(H_A, HEAD_A)
    kk = kk * lax.rsqrt(jnp.maximum(jnp.sum(kk * kk, axis=-1, keepdims=True), 1e-24))
    kf = kf * (1.0 + (af - 1.0) * k_a.astype(f32).reshape(H_A, HEAD_A))
    s_new, y = wkv7_scan(wkv_prev.astype(f32), rf, heads(decay), kf, vf, kk, af)
    mean = jnp.mean(y, axis=-1, keepdims=True)
    var = jnp.mean(jnp.square(y - mean), axis=-1, keepdims=True)
    y = ((y - mean) * lax.rsqrt(var + GN_EPS)).reshape(B, T, MIX_A)
    y = y * lnx_w.astype(f32) + lnx_b.astype(f32)
    bonus = jnp.sum(rf * kf * r_k.astype(f32), axis=-1, keepdims=True) * vf
    o = (y + bonus.reshape(B, T, MIX_A)) * g.astype(f32)
    return o.astype(xn.dtype), q_mem, xn[:, -1], s_new.astype(wkv_prev.dtype), v


def memory_kv(mem, norm_mem, w_mem_kv):
    B, M, _ = mem.shape
    mf = mem.astype(jnp.float32)
    mhat = (mf * lax.rsqrt(jnp.mean(mf * mf, axis=-1, keepdims=True) + RMS_EPS)).astype(mem.dtype)
    kv = jnp.einsum('bmd,ld,ldc->lbmc', mhat, norm_mem, w_mem_kv)
    kv = kv.reshape(DEPTH, B, M, 2, MEM_HEADS, MEM_HEAD_DIM)
    return kv[:, :, :, 0], kv[:, :, :, 1]


def memory_attention(q, mem_k, mem_v):
    B, T, _ = q.shape
    qh = q.reshape(B, T, MEM_HEADS, MEM_HEAD_DIM)
    s = jnp.einsum('bshd,bmhd->bhsm', qh, mem_k).astype(jnp.float32) * MEM_SCALE
    p = jax.nn.softmax(s, axis=-1).astype(mem_v.dtype)
    return jnp.einsum('bhsm,bmhd->bshd', p, mem_v).reshape(B, T, MEM_WIDTH)


def shared_latent(x, pos, norm_kv, w_kv_a, norm_ckv):
    h = rms_norm(x, norm_kv) @ w_kv_a
    ckv = rms_norm(h[..., :KV_LORA], norm_ckv)
    kpe = rope(h[..., None, KV_LORA:], pos)[:, :, 0]
    return ckv, kpe


def mla_queries(xn, pos, w_in, norm_q, w_q_b):
    B, T, _ = xn.shape
    proj = xn @ w_in
    q_a, q_mem = proj[..., :Q_LORA], proj[..., Q_LORA:]
    q = (rms_norm(q_a, norm_q) @ w_q_b).reshape(B, T, H_B, NOPE_DIM + ROPE_DIM)
    return q[..., :NOPE_DIM], rope(q[..., NOPE_DIM:], pos), q_mem


def causal_block_attention(q, k, v):
    T = q.shape[1]
    blk = min(Q_BLOCK, T)
    outs = []
    for i in range(T // blk):
        lo, hi = i * blk, (i + 1) * blk
        s = jnp.einsum('bqhd,bkhd->bhqk', q[:, lo:hi], k[:, :hi]).astype(jnp.float32) * ATTN_SCALE
        mask = jnp.arange(hi)[None, :] <= jnp.arange(lo, hi)[:, None]
        p = jax.nn.softmax(jnp.where(mask, s, -jnp.inf), axis=-1).astype(v.dtype)
        outs.append(jnp.einsum('bhqk,bkhd->bqhd', p, v[:, :hi]))
    return jnp.concatenate(outs, axis=1)


def latent_decode_attention(q_nope, q_pe, w_uk, w_uv, ckv_past, kpe_past, ckv_new, kpe_new):
    T = q_nope.shape[1]
    Tp = ckv_past.shape[1]
    q_lat = jnp.einsum('bshd,chd->bshc', q_nope, w_uk)
    s_past = jnp.einsum('bshc,btc->bhst', q_lat, ckv_past) + jnp.einsum('bshr,btr->bhst', q_pe, kpe_past)
    s_new = jnp.einsum('bshc,btc->bhst', q_lat, ckv_new) + jnp.einsum('bshr,btr->bhst', q_pe, kpe_new)
    causal = jnp.tril(jnp.ones((T, T), dtype=bool))
    s_new = jnp.where(causal, s_new.astype(jnp.float32) * ATTN_SCALE, -jnp.inf)
    s = jnp.concatenate([s_past.astype(jnp.float32) * ATTN_SCALE, s_new], axis=-1)
    p = jax.nn.softmax(s, axis=-1).astype(ckv_new.dtype)
    o_lat = (jnp.einsum('bhst,btc->bshc', p[..., :Tp], ckv_past)
             + jnp.einsum('bhst,btc->bshc', p[..., Tp:], ckv_new))
    return jnp.einsum('bshc,chd->bshd', o_lat, w_uv)


def conv_glu(xn, conv_prev, w_up, conv_w, conv_b, w_down):
    T = xn.shape[1]
    u = xn @ w_up
    gate, val = u[..., :D_FF], u[..., D_FF:]
    ext = jnp.concatenate([conv_prev.astype(gate.dtype), gate], axis=1)
    conv = conv_b
    for j in range(CONV_W):
        conv = conv + ext[:, j:j + T] * conv_w[j]
    h = jax.nn.silu(conv) * val
    return h @ w_down, ext[:, T:]


def trunk(x, pos, mem_k, mem_v, wkv0, shift0, conv0, past, P):
    B, T, _ = x.shape
    new_wkv, new_shift, new_conv = [], [], []
    v_first = None
    for l in range(DEPTH):
        xn = rms_norm(x, P['norm_mix'][l])
        if l < N_A:
            if l == 0:
                w_in, vres = P['w_in_a'][0], None
            else:
                w_in = jnp.concatenate([P['w_in_a'][l], P['w_vres_in'][l - 1]], axis=1)
                vres = (P['mu_vres'][l - 1], P['w_vres_up'][l - 1], P['v0'][l - 1])
            o_tok, q_mem, s_shift, s_wkv, v = rwkv7_mix(
                xn, shift0[l], wkv0[l], v_first, w_in, P['mu_a'][l], P['w_decay_up'][l], P['w0'][l],
                P['w_a_up'][l], P['a0'][l], P['w_g_up'][l], P['k_k'][l], P['k_a'][l], P['r_k'][l],
                P['lnx_w'][l], P['lnx_b'][l], vres)
            if l == 0:
                v_first = v
            new_shift.append(s_shift)
            new_wkv.append(s_wkv)
        else:
            if l == N_A:
                ckv, kpe = shared_latent(x, pos, P['norm_kv'], P['w_kv_a'], P['norm_ckv'])
                w_uk, w_uv = P['w_kv_b'][..., :NOPE_DIM], P['w_kv_b'][..., NOPE_DIM:]
                if past is None:
                    k_full = jnp.concatenate(
                        [jnp.einsum('btc,chd->bthd', ckv, w_uk),
                         jnp.broadcast_to(kpe[:, :, None, :], (B, T, H_B, ROPE_DIM))], axis=-1)
                    v_full = jnp.einsum('btc,chd->bthd', ckv, w_uv)
            j = l - N_A
            q_nope, q_pe, q_mem = mla_queries(xn, pos, P['w_in_b'][j], P['norm_q'][j], P['w_q_b'][j])
            if past is None:
                o = causal_block_attention(jnp.concatenate([q_nope, q_pe], axis=-1), k_full, v_full)
            else:
                o = latent_decode_attention(q_nope, q_pe, w_uk, w_uv, past[0], past[1], ckv, kpe)
            o_tok = o.reshape(B, T, H_B * V_DIM)
        o_mem = memory_attention(q_mem, mem_k[l], mem_v[l])
        x = x + jnp.concatenate([o_tok, o_mem], axis=-1) @ P['w_o'][l]
        f, c = conv_glu(rms_norm(x, P['norm_ffn'][l]), conv0[l], P['w_ffn_up'][l],
                        P['conv_w'][l], P['conv_b'][l], P['w_ffn_down'][l])
        new_conv.append(c)
        x = x + f
    y = rms_norm(x, P['final_norm'])
    return y, ckv, kpe, jnp.stack(new_wkv), jnp.stack(new_shift), jnp.stack(new_conv)


def setup_inputs(seed: int = 0) -> dict:
    key = jax.random.key(seed)
    ks = iter(jax.random.split(key, 64))
    f32 = jnp.float32
    def nrm(shape, scale):
        return jax.random.normal(next(ks), shape, f32) * scale
    def uni(shape):
        return jax.random.uniform(next(ks), shape, f32)
    def gain(shape):
        return 1.0 + nrm(shape, 0.05)
    n_pages = PAST_LEN // PAGE_SIZE
    n_used = DEC_BATCH * n_pages
    n_phys = (n_used * 5 + 3) // 4
    perm = jax.random.permutation(next(ks), n_phys)
    page_table = perm[:n_used].reshape(DEC_BATCH, n_pages).astype(jnp.int32)
    na1 = max(N_A - 1, 0)
    return {
        'x_prompt': nrm((BATCH, SEQ, D_MODEL), 1.0),
        'x_sample': nrm((DEC_BATCH, DEC_SEQ, D_MODEL), 1.0),
        'cache_ckv': nrm((n_phys, PAGE_SIZE, KV_LORA), 1.0),
        'cache_kpe': nrm((n_phys, PAGE_SIZE, ROPE_DIM), 1.0),
        'cache_mem_k': nrm((DEPTH, DEC_BATCH, N_MEM, MEM_HEADS, MEM_HEAD_DIM), 1.0),
        'cache_mem_v': nrm((DEPTH, DEC_BATCH, N_MEM, MEM_HEADS, MEM_HEAD_DIM), 1.0),
        'state_wkv': nrm((N_A, DEC_BATCH, H_A, HEAD_A, HEAD_A), 0.3),
        'state_shift': nrm((N_A, DEC_BATCH, D_MODEL), 1.0),
        'state_conv': nrm((DEPTH, DEC_BATCH, CONV_W - 1, D_FF), 1.0),
        'page_table': page_table,
        'mem_prompt': nrm((BATCH, N_MEM, D_MODEL), 1.0),
        'norm_mix': gain((DEPTH, D_MODEL)),
        'norm_ffn': gain((DEPTH, D_MODEL)),
        'norm_mem': gain((DEPTH, D_MODEL)),
        'w_mem_kv': nrm((DEPTH, D_MODEL, 2 * MEM_WIDTH), D_MODEL ** -0.5),
        'w_o': nrm((DEPTH, MIX_A + MEM_WIDTH, D_MODEL), (MIX_A + MEM_WIDTH) ** -0.5),
        'w_in_a': nrm((N_A, D_MODEL, C_A), D_MODEL ** -0.5),
        'mu_a': uni((N_A, C_RWKV)),
        'w_vres_in': nrm((na1, D_MODEL, VRES_LORA), D_MODEL ** -0.5),
        'mu_vres': uni((na1, VRES_LORA)),
        'w_decay_up': nrm((N_A, DECAY_LORA, MIX_A), 0.5 * DECAY_LORA ** -0.5),
        'w0': -2.5 + nrm((N_A, MIX_A), 0.5),
        'w_a_up': nrm((N_A, A_LORA, MIX_A), A_LORA ** -0.5),
        'a0': nrm((N_A, MIX_A), 0.5),
        'w_g_up': nrm((N_A, GATE_LORA, MIX_A), GATE_LORA ** -0.5),
        'w_vres_up': nrm((na1, VRES_LORA, MIX_A), VRES_LORA ** -0.5),
        'v0': nrm((na1, MIX_A), 0.5),
        'k_k': 0.85 + nrm((N_A, MIX_A), 0.1),
        'k_a': 1.0 + nrm((N_A, MIX_A), 0.1),
        'r_k': nrm((N_A, H_A, HEAD_A), 0.1),
        'lnx_w': gain((N_A, MIX_A)),
        'lnx_b': nrm((N_A, MIX_A), 0.02),
        'norm_kv': gain((D_MODEL,)),
        'w_kv_a': nrm((D_MODEL, KV_LORA + ROPE_DIM), D_MODEL ** -0.5),
        'norm_ckv': gain((KV_LORA,)),
        'w_kv_b': nrm((KV_LORA, H_B, NOPE_DIM + V_DIM), KV_LORA ** -0.5),
        'w_in_b': nrm((N_B, D_MODEL, C_B), D_MODEL ** -0.5),
        'norm_q': gain((N_B, Q_LORA)),
        'w_q_b': nrm((N_B, Q_LORA, H_B * (NOPE_DIM + ROPE_DIM)), Q_LORA ** -0.5),
        'w_ffn_up': nrm((DEPTH, D_MODEL, 2 * D_FF), D_MODEL ** -0.5),
        'conv_w': nrm((DEPTH, CONV_W, D_FF), 0.5),
        'conv_b': nrm((DEPTH, D_FF), 0.02),
        'w_ffn_down': nrm((DEPTH, D_FF, D_MODEL), D_FF ** -0.5),
        'final_norm': gain((D_MODEL,)),
    }


def reference(x_prompt, x_sample, cache_ckv, cache_kpe, cache_mem_k, cache_mem_v,
              state_wkv, state_shift, state_conv, page_table, mem_prompt,
              norm_mix, norm_ffn, norm_mem, w_mem_kv, w_o,
              w_in_a, mu_a, w_vres_in, mu_vres, w_decay_up, w0, w_a_up, a0,
              w_g_up, w_vres_up, v0, k_k, k_a, r_k, lnx_w, lnx_b,
              norm_kv, w_kv_a, norm_ckv, w_kv_b, w_in_b, norm_q, w_q_b,
              w_ffn_up, conv_w, conv_b, w_ffn_down, final_norm):
    P = dict(norm_mix=norm_mix, norm_ffn=norm_ffn, w_o=w_o,
             w_in_a=w_in_a, mu_a=mu_a, w_vres_in=w_vres_in, mu_vres=mu_vres,
             w_decay_up=w_decay_up, w0=w0, w_a_up=w_a_up, a0=a0, w_g_up=w_g_up,
             w_vres_up=w_vres_up, v0=v0, k_k=k_k, k_a=k_a, r_k=r_k, lnx_w=lnx_w, lnx_b=lnx_b,
             norm_kv=norm_kv, w_kv_a=w_kv_a, norm_ckv=norm_ckv, w_kv_b=w_kv_b,
             w_in_b=w_in_b, norm_q=norm_q, w_q_b=w_q_b,
             w_ffn_up=w_ffn_up, conv_w=conv_w, conv_b=conv_b, w_ffn_down=w_ffn_down,
             final_norm=final_norm)
    dt = x_prompt.dtype
    B, T = x_prompt.shape[:2]
    mem_k_p, mem_v_p = memory_kv(mem_prompt, norm_mem, w_mem_kv)
    y_p, ckv_p, kpe_p, wkv_p, shift_p, conv_p = trunk(
        x_prompt, jnp.arange(T), mem_k_p, mem_v_p,
        jnp.zeros((N_A, B, H_A, HEAD_A, HEAD_A), dt), jnp.zeros((N_A, B, D_MODEL), dt),
        jnp.zeros((DEPTH, B, CONV_W - 1, D_FF), dt), None, P)
    DB, TS = x_sample.shape[:2]
    past_len = page_table.shape[1] * PAGE_SIZE
    ckv_past = cache_ckv[page_table].reshape(DB, past_len, KV_LORA)
    kpe_past = cache_kpe[page_table].reshape(DB, past_len, ROPE_DIM)
    y_s, ckv_s, kpe_s, wkv_s, shift_s, conv_s = trunk(
        x_sample, past_len + jnp.arange(TS), cache_mem_k, cache_mem_v,
        state_wkv, state_shift, state_conv, (ckv_past, kpe_past), P)
    return (y_p, y_s, ckv_p, kpe_p, mem_k_p, mem_v_p, wkv_p, shift_p, conv_p,
            ckv_s, kpe_s, wkv_s, shift_s, conv_s)
```

```python
import contextlib
import numpy as np
import ml_dtypes
import concourse.bass as bass
import concourse.mybir as mybir
from concourse.bass_utils import run_bass_kernel_spmd

F32 = mybir.dt.float32
BF16 = mybir.dt.bfloat16
I32 = mybir.dt.int32
AF = mybir.ActivationFunctionType
ALU = mybir.AluOpType
AX = mybir.AxisListType

D = 1024
NT = 17
NPT = 16
DFF = 2816
NMEM = 256
H_A = 12
C_RWKV = 2592
C_A = 2848
ATTN_SCALE = 192 ** -0.5
MEM_SCALE = 0.125
RMS_EPS = 1e-6
GN_EPS = 64e-5
DECAY_C = float(np.exp(-0.5))


class Buf:
    __slots__ = ("name", "w", "r", "excl")

    def __init__(self, name, excl=False):
        self.name = name
        self.w = None
        self.r = []
        self.excl = excl


class V:
    __slots__ = ("ap", "buf")

    def __init__(self, ap, buf):
        self.ap = ap
        self.buf = buf

    def __getitem__(self, k):
        return V(self.ap[k], self.buf)

    def re(self, s, **kw):
        return V(self.ap.rearrange(s, **kw), self.buf)

    def bc(self, shape):
        return V(self.ap.broadcast_to(shape), self.buf)


class Op:
    __slots__ = ("eng", "fn", "deps", "signal", "cnt", "dma", "sem", "semval", "idx")


ENGS = ["pe", "act", "dve", "pool", "sp"]
NDMA_SEMS = 24
NSEM_Q = {"sp": 24, "pool": 4, "act": 4}


class KB:
    def __init__(self, nc):
        self.nc = nc
        self.ops = []
        self.last = {e: None for e in ENGS}
        self.dmas_since = []

    def op(self, eng, fn, reads=(), writes=(), dma=False, extra_deps=(), pseudo=False):
        o = Op()
        o.eng = eng
        o.fn = fn
        o.dma = dma
        o.signal = False
        o.cnt = None
        o.sem = None
        o.semval = None
        o.idx = len(self.ops)
        deps = set()
        rb = []
        wb = []
        for v in reads:
            b = v.buf if isinstance(v, V) else v
            if b is not None and b not in rb:
                rb.append(b)
        for v in writes:
            b = v.buf if isinstance(v, V) else v
            if b is not None and b not in wb:
                wb.append(b)
        for b in rb:
            if b.w is not None:
                deps.add((b.w, "raw"))
            if b.excl:
                for r in b.r:
                    if self.ops[r].eng != eng:
                        deps.add((r, "raw"))
        for b in wb:
            if b.w is not None:
                deps.add((b.w, "waw"))
            for r in b.r:
                deps.add((r, "war"))
        for j in extra_deps:
            deps.add((j, "raw"))
        final = {}
        for (j, kind) in deps:
            if j == o.idx:
                continue
            pj = self.ops[j]
            need = True
            if pj.eng == eng and not pj.dma and not dma:
                if eng == "pe":
                    need = False
                elif kind != "raw":
                    need = False
            if need:
                final[j] = True
        o.deps = sorted(final.keys())
        for j in o.deps:
            self.ops[j].signal = True
        for b in rb:
            b.r.append(o.idx)
        for b in wb:
            b.w = o.idx
            b.r = []
        self.ops.append(o)
        if not pseudo:
            self.last[eng] = o.idx
        if dma:
            self.dmas_since.append(o.idx)
        return o

    def barrier(self):
        lasts = [j for j in self.last.values() if j is not None]
        dm = list(self.dmas_since)
        self.dmas_since = []
        for e in ENGS:
            self.op(e, lambda eng: None, extra_deps=lasts + dm, pseudo=True)

    def finalize(self):
        nc = self.nc
        ops = self.ops
        per = {e: [o for o in ops if o.eng == e] for e in ENGS}
        for e in ENGS:
            c = 0
            for o in per[e]:
                if o.dma:
                    continue
                if o.signal:
                    c += 1
                    o.cnt = c
        with contextlib.ExitStack() as st:
            esem = {e: st.enter_context(nc.semaphore("s_" + e)) for e in ENGS}
            dsem = {e: [st.enter_context(nc.semaphore("d_%s_%d" % (e, i))) for i in range(NSEM_Q[e])]
                    for e in ("sp", "pool", "act")}
            dstate = {e: {"i": 0, "tot": [0] * NSEM_Q[e]} for e in dsem}
            for o in ops:
                if o.dma:
                    s = dstate[o.eng]
                    k = s["i"] % NSEM_Q[o.eng]
                    s["i"] += 1
                    o.sem = (o.eng, k)
                    o.semval = (s["tot"][k], s["tot"][k] + 16)
                    s["tot"][k] += 16
            block = st.enter_context(nc.Block())

            def mk(e):
                def body(engobj):
                    seen = {}
                    for o in per[e]:
                        waits = {}
                        if o.dma:
                            key = ("d",) + o.sem
                            prev = o.semval[0]
                            if prev > 0:
                                waits[key] = (dsem[o.sem[0]][o.sem[1]], prev)
                        for j in o.deps:
                            pj = ops[j]
                            if pj.dma:
                                key = ("d",) + pj.sem
                                val = pj.semval[1]
                                semh = dsem[pj.sem[0]][pj.sem[1]]
                            else:
                                key = ("e", pj.eng)
                                val = pj.cnt
                                semh = esem[pj.eng]
                            if key not in waits or waits[key][1] < val:
                                waits[key] = (semh, val)
                        for key, (semh, val) in waits.items():
                            if seen.get(key, 0) >= val:
                                continue
                            engobj.wait_ge(semh, val)
                            seen[key] = val
                        ins = o.fn(engobj)
                        if ins is None:
                            continue
                        if o.dma:
                            ins.then_inc(dsem[o.sem[0]][o.sem[1]], 16)
                        elif o.signal:
                            ins.then_inc(esem[e], 1)
                    if e == "sp":
                        for qe in dsem:
                            for k in range(NSEM_Q[qe]):
                                tot = dstate[qe]["tot"][k]
                                if tot > 0:
                                    engobj.wait_ge(dsem[qe][k], tot)
                return body

            block.tensor(mk("pe"))
            block.scalar(mk("act"))
            block.vector(mk("dve"))
            block.gpsimd(mk("pool"))
            block.sync(mk("sp"))


class G:
    pass


def _aps(*vs):
    return [v for v in vs if isinstance(v, V)]


def mm(g, out, lhsT, rhs, start=True, stop=True):
    g.kb.op("pe", lambda e: e.matmul(out.ap, lhsT=lhsT.ap, rhs=rhs.ap, start=start, stop=stop),
            reads=[lhsT, rhs], writes=[out])


def tr(g, out, in_, ident):
    g.kb.op("pe", lambda e: e.transpose(out=out.ap, in_=in_.ap, identity=ident.ap),
            reads=[in_, ident], writes=[out])


def act(g, out, in_, func, scale=1.0, bias=0.0, accum=None):
    sc = scale.ap if isinstance(scale, V) else scale
    bi = bias.ap if isinstance(bias, V) else bias
    kw = {}
    if accum is not None:
        kw["accum_out"] = accum.ap
    g.kb.op("act", lambda e: e.activation(out=out.ap, in_=in_.ap, func=func, bias=bi, scale=sc, **kw),
            reads=_aps(in_, scale, bias), writes=_aps(out, accum))


def ts(g, eng, out, in0, s1, op0, s2=None, op1=None):
    a1 = s1.ap if isinstance(s1, V) else s1
    a2 = s2.ap if isinstance(s2, V) else s2
    if op1 is None:
        g.kb.op(eng, lambda e: e.tensor_scalar(out=out.ap, in0=in0.ap, scalar1=a1, scalar2=None, op0=op0),
                reads=_aps(in0, s1), writes=[out])
    else:
        g.kb.op(eng, lambda e: e.tensor_scalar(out=out.ap, in0=in0.ap, scalar1=a1, scalar2=a2, op0=op0, op1=op1),
                reads=_aps(in0, s1, s2), writes=[out])


def tt(g, eng, out, in0, in1, op):
    g.kb.op(eng, lambda e: e.tensor_tensor(out=out.ap, in0=in0.ap, in1=in1.ap, op=op),
            reads=[in0, in1], writes=[out])


def stt(g, out, in0, scalar, in1, op0, op1):
    sc = scalar.ap if isinstance(scalar, V) else scalar
    g.kb.op("dve", lambda e: e.scalar_tensor_tensor(out=out.ap, in0=in0.ap, scalar=sc, in1=in1.ap, op0=op0, op1=op1),
            reads=_aps(in0, scalar, in1), writes=[out])


def cp(g, eng, out, in_):
    if eng == "act":
        g.kb.op("act", lambda e: e.activation(out=out.ap, in_=in_.ap, func=AF.Copy), reads=[in_], writes=[out])
    else:
        g.kb.op(eng, lambda e: e.tensor_copy(out=out.ap, in_=in_.ap), reads=[in_], writes=[out])


def red(g, out, in_, op=ALU.add):
    g.kb.op("dve", lambda e: e.tensor_reduce(out=out.ap, in_=in_.ap, axis=AX.X, op=op), reads=[in_], writes=[out])


def recip(g, out, in_):
    g.kb.op("dve", lambda e: e.reciprocal(out=out.ap, in_=in_.ap), reads=[in_], writes=[out])


def memset(g, eng, out, val):
    g.kb.op(eng, lambda e: e.memset(out.ap, val), writes=[out])


def dma(g, out, in_, eng="sp", slow=False):
    kw = {"allow_slow_non_contiguous": True} if slow else {}
    g.kb.op(eng, lambda e: e.dma_start(out=out.ap, in_=in_.ap, **kw), reads=[in_], writes=[out], dma=True)


def dv(ap):
    return V(ap, None)


class Scope:
    def __init__(self, g, tag):
        self.g = g
        self.tag = tag
        self.st = contextlib.ExitStack()
        self.n = 0

    def __enter__(self):
        self.st.__enter__()
        return self

    def __exit__(self, *a):
        return self.st.__exit__(*a)

    def sb(self, name, shape, dt):
        self.n += 1
        h = self.st.enter_context(self.g.nc.sbuf_tensor("%s_%s_%d" % (self.tag, name, self.n), list(shape), dt))
        return V(h[:], Buf(name))


def psum_hold(g, k, shape, dt=F32):
    base = g.ps_f[6 + k] if dt == F32 else g.ps_b[6 + k]
    n = int(np.prod(shape[1:]))
    v = V(base.ap[0:shape[0], 0:n], base.buf)
    if len(shape) == 3:
        v = v.re("p (a b) -> p a b", a=shape[1])
    elif len(shape) == 4:
        v = v.re("p (a b c) -> p a b c", a=shape[1], b=shape[2])
    return v


def psum(g, shape, dt=F32):
    k = g.ps_i % 6
    g.ps_i += 1
    base = g.ps_f[k] if dt == F32 else g.ps_b[k]
    n = int(np.prod(shape[1:]))
    v = V(base.ap[0:shape[0], 0:n], base.buf)
    if len(shape) == 3:
        v = v.re("p (a b) -> p a b", a=shape[1])
    elif len(shape) == 4:
        v = v.re("p (a b c) -> p a b c", a=shape[1], b=shape[2])
    return v


def rmsnorm_T(g, S, x_tile, gT, out_T, tag):
    sq = S.tmp_sq
    ss = S.tmp_ss
    xn = S.tmp_xn
    act(g, sq, x_tile, AF.Square, accum=ss)
    ts(g, "dve", ss, ss, 1.0 / D, ALU.mult, RMS_EPS, ALU.add)
    act(g, ss, ss, AF.Sqrt)
    recip(g, ss, ss)
    ts(g, "dve", xn, x_tile, ss, ALU.mult)
    for half in range(2):
        pt = psum(g, [128, 4, 128], BF16)
        for c in range(4):
            tr(g, pt[:, c, :], xn[:, (half * 4 + c) * 128:(half * 4 + c + 1) * 128], g.ident_b)
        for c in range(4):
            cc = half * 4 + c
            ts(g, "dve", out_T[:, cc, :], pt[:, c, :], gT[:, cc:cc + 1], ALU.mult)
    return xn, ss


def load_vec_T(g, S, name, src_ap, n):
    k = n // 128
    t = S.sb(name, [128, k], F32)
    dma(g, t, dv(src_ap.rearrange("(c p) -> p c", p=128)), slow=True)
    return t


def load_w_into(g, S, t, src_ap, kdim, ncols, col_off=0, cast_eng="pool"):
    kc = (kdim + 127) // 128
    if not hasattr(S, "stg"):
        S.stg = [S.sb("stg%d" % i, [128, 1440], F32) for i in range(2)]
        S.stg_i = 0
    SW = S.stg[0].ap.shape[1]
    for c in range(kc):
        r = min(128, kdim - c * 128)
        for n0 in range(0, ncols, SW):
            n = min(SW, ncols - n0)
            stg = S.stg[S.stg_i % 2]
            S.stg_i += 1
            dma(g, stg[0:r, 0:n], dv(src_ap[c * 128:c * 128 + r, n0:n0 + n]))
            ce = ("pool", "act", "dve")[S.stg_i % 3] if n >= 512 else cast_eng
            cp(g, ce, t[0:r, c, col_off + n0:col_off + n0 + n], stg[0:r, 0:n])


def load_w_bf16(g, S, name, src_ap, kdim, ncols, cast_eng="pool"):
    kc = (kdim + 127) // 128
    t = S.sb(name, [128, kc, ncols], BF16)
    load_w_into(g, S, t, src_ap, kdim, ncols, 0, cast_eng)
    return t


def load_bcast(g, S, name, src_ap, n):
    t = S.sb(name, [128, n], F32)
    dma(g, t, dv(src_ap.partition_broadcast(128)))
    return t


def ffn_phase(g, l, half):
    I = g.I
    nch = 11
    c0 = half * nch
    with Scope(g, "f%d%d" % (l, half)) as S:
        wup_g = load_w_bf16(g, S, "wupg", I["w_ffn_up"][l][:, c0 * 128:(c0 + nch) * 128], D, nch * 128)
        wup_v = load_w_bf16(g, S, "wupv", I["w_ffn_up"][l][:, DFF + c0 * 128:DFF + (c0 + nch) * 128], D, nch * 128)
        wdn = load_w_bf16(g, S, "wdn", I["w_ffn_down"][l][c0 * 128:(c0 + nch) * 128, :], nch * 128, D)
        cw = S.sb("cw", [128, 3, nch], F32)
        for j in range(3):
            dma(g, cw[:, j, :], dv(I["conv_w"][l][j, c0 * 128:(c0 + nch) * 128].rearrange("(c p) -> p c", p=128)), slow=True)
        cb = S.sb("cb", [128, nch], F32)
        dma(g, cb, dv(I["conv_b"][l][c0 * 128:(c0 + nch) * 128].rearrange("(c p) -> p c", p=128)), slow=True)
        gT = load_vec_T(g, S, "gffn", I["norm_ffn"][l], D)
        S.tmp_sq = S.sb("sq", [128, D], BF16)
        S.tmp_ss = S.sb("ss", [128, 1], F32)
        S.tmp_xn = S.sb("xn", [128, D], BF16)
        nset = 2 if l < 2 else 1
        xnT2 = [S.sb("xnT%d" % k, [128, 8, 512], BF16) for k in range(nset)] * (3 - nset)
        gext = S.sb("gext", [128, nch, 514], F32)
        conv = S.sb("conv", [128, nch, 512], F32)
        hT = S.sb("hT", [128, nch, 512], BF16)
        cst = S.sb("cst", [32, nch, 128], F32)
        cstT = S.sb("cstT", [128, nch, 32], F32)
        xb2 = [[S.sb("xb%d_%d" % (s_, k), [128, D], F32) for k in range(4)] for s_ in range(nset)] * (3 - nset)
        memset(g, "dve", gext[:, :, 0:2], 0.0)
        blocks = [(0, 4), (4, 4), (8, 4), (12, 4), (NPT, 1)]

        def pro_f(bi):
            t0_, nt_ = blocks[bi]
            xb_, xnT_ = xb2[bi % 2], xnT2[bi % 2]
            for q in range(nt_):
                i = t0_ + q
                sl = slice(q * 128, (q + 1) * 128)
                dma(g, xb_[q], g.x_src[i])
                if half == 0:
                    rmsnorm_T(g, S, xb_[q], gT, xnT_[:, :, sl], "f")
                    dma(g, g.xn2_scr[i], xnT_[:, :, sl])
                else:
                    dma(g, xnT_[:, :, sl], g.xn2_scr[i])

        if nset == 2:
            pro_f(0)
        for bi, (t0, nt) in enumerate(blocks):
            if nset == 1:
                pro_f(bi)
            samp = (t0 == NPT)
            W = nt * 128
            xb, xnT = xb2[bi % 2], xnT2[bi % 2]
            gev = gext[:, :, 0:160].re("p c (j t) -> p c j t", t=10)
            if samp:
                dma(g, cst.re("p c f -> p (c f)"), dv(I["state_conv"][l].rearrange("b j f -> (b j) f")[:, c0 * 128:(c0 + nch) * 128]))
                for c in range(nch):
                    pt = psum(g, [128, 32], F32)
                    tr(g, pt, cst[:, c, :], g.ident_f[0:32, 0:32])
                    cp(g, "act", cstT[:, c, :], pt)
                cp(g, "dve", gev[:, :, :, 0:2], cstT.re("p c (j t) -> p c j t", t=2))
            for c in range(nch):
                pg = psum(g, [128, 512])
                for k in range(8):
                    mm(g, pg[:, 0:W], wup_g[:, k, c * 128:(c + 1) * 128], xnT[:, k, 0:W], start=(k == 0), stop=(k == 7))
                if samp:
                    gcur = gev[:, c, :, 2:10]
                    src = pg[:, 0:128].re("p (j t) -> p j t", t=8)
                    cv = conv[:, c, 0:128].re("p (j t) -> p j t", t=8)
                else:
                    gcur = gext[:, c, 2:2 + W]
                    src = pg[:, 0:W]
                    cv = conv[:, c, 0:W]
                cp(g, "act", gcur, src)
                act(g, cv, src, AF.Identity, scale=cw[:, 2, c:c + 1], bias=cb[:, c:c + 1])
            for c in range(nch):
                if samp:
                    g1 = gev[:, c, :, 1:9]
                    g0 = gev[:, c, :, 0:8]
                    cv = conv[:, c, 0:128].re("p (j t) -> p j t", t=8)
                else:
                    g1 = gext[:, c, 1:1 + W]
                    g0 = gext[:, c, 0:W]
                    cv = conv[:, c, 0:W]
                stt(g, cv, g1, cw[:, 1, c:c + 1], cv, ALU.mult, ALU.add)
                stt(g, cv, g0, cw[:, 0, c:c + 1], cv, ALU.mult, ALU.add)
            if samp or t0 == NPT - 4:
                nrow = 32 if samp else 2
                for c in range(nch):
                    pt = psum(g, [32, 128], F32)
                    if samp:
                        cp(g, "pool", cstT[:, c, :].re("p (j t) -> p j t", t=2), gev[:, c, :, 8:10])
                    else:
                        cp(g, "pool", cstT[:, c, 0:2], gext[:, c, W:W + 2])
                    tr(g, pt[0:nrow, :], cstT[:, c, 0:nrow], g.ident_f)
                    cp(g, "act", cst[0:nrow, c, :], pt[0:nrow, :])
                if samp:
                    dma(g, dv(g.O["conv_s"][l].rearrange("b j f -> (b j) f")[:, c0 * 128:(c0 + nch) * 128]), cst.re("p c f -> p (c f)"))
                else:
                    dma(g, dv(g.O["conv_p"][l][:, c0 * 128:(c0 + nch) * 128]), cst[0:2].re("p c f -> p (c f)"))
            if not samp and t0 < NPT - 4:
                cp(g, "pool", gext[:, :, 0:2], gext[:, :, W:W + 2])
            act(g, conv[:, :, 0:W], conv[:, :, 0:W], AF.Silu)
            for c in range(nch):
                pv = psum(g, [128, 512])
                for k in range(8):
                    mm(g, pv[:, 0:W], wup_v[:, k, c * 128:(c + 1) * 128], xnT[:, k, 0:W], start=(k == 0), stop=(k == 7))
                tt(g, "dve", hT[:, c, 0:W], conv[:, c, 0:W], pv[:, 0:W], ALU.mult)
            if nset == 2 and bi + 1 < len(blocks):
                pro_f(bi + 1)
            for q in range(nt):
                xt = xb[q]
                for nb in range(2):
                    po = psum(g, [128, 512], F32)
                    for c in range(nch):
                        mm(g, po, hT[:, c, q * 128:(q + 1) * 128], wdn[:, c, nb * 512:(nb + 1) * 512], start=(c == 0), stop=(c == nch - 1))
                    tt(g, "dve", xt[:, nb * 512:(nb + 1) * 512], xt[:, nb * 512:(nb + 1) * 512], po, ALU.add)
                x_store(g, t0 + q, xt)
    g.kb.barrier()


def x_load(g, S, i):
    if not hasattr(S, "xbufs"):
        S.xbufs = [S.sb("xb%d" % k, [128, D], F32) for k in range(getattr(S, "nxb", 2))]
        S.xb_i = 0
    t = S.xbufs[S.xb_i % len(S.xbufs)]
    S.xb_i += 1
    dma(g, t, g.x_src[i])
    return t


def x_store(g, i, t):
    dma(g, g.x_scr[i], t)
    g.x_src[i] = g.x_scr[i]


def final_phase(g):
    I = g.I
    with Scope(g, "fin") as S:
        gb = S.sb("gfin", [128, D], F32)
        dma(g, gb, dv(I["final_norm"].partition_broadcast(128)))
        sq = S.sb("sq", [128, D], F32)
        ss = S.sb("ss", [128, 1], F32)
        yo = [S.sb("yo%d" % i, [128, D], F32) for i in range(2)]
        for i in range(NT):
            xt = x_load(g, S, i)
            y = yo[i % 2]
            act(g, sq, xt, AF.Square, accum=ss)
            ts(g, "dve", ss, ss, 1.0 / D, ALU.mult, RMS_EPS, ALU.add)
            act(g, ss, ss, AF.Sqrt)
            recip(g, ss, ss)
            stt(g, y, xt, ss, gb, ALU.mult, ALU.mult)
            dma(g, dv(g.O["y"][i]), y)
    g.kb.barrier()


def mem_prep(g, S, l):
    I, O = g.I, g.O
    kTm = S.sb("kTm", [128, 2, 2, 256], BF16)
    memset(g, "pool", kTm, 0.0)
    Vpm = S.sb("Vpm", [128, 2, 4, 128], BF16)
    memset(g, "pool", Vpm, 0.0)
    with Scope(g, "mp%d" % l) as T:
        wm = load_w_bf16(g, T, "wmem", I["w_mem_kv"][l], D, 512)
        gT = load_vec_T(g, T, "gmem", I["norm_mem"][l], D)
        T.tmp_sq = T.sb("sq", [128, D], F32)
        T.tmp_ss = T.sb("ss", [128, 1], F32)
        T.tmp_xn = T.sb("xn", [128, D], BF16)
        mT = T.sb("mT", [128, 8, 256], BF16)
        mi = [T.sb("min%d" % k, [128, D], F32) for k in range(2)]
        kv = [T.sb("kv%d" % k, [128, 512], F32) for k in range(2)]
        for mt in range(2):
            dma(g, mi[mt], dv(I["mem_prompt"][mt * 128:(mt + 1) * 128, :]))
            rmsnorm_T(g, T, mi[mt], gT, mT[:, :, mt * 128:(mt + 1) * 128], "m")
        for mt in range(2):
            po = psum(g, [128, 512])
            for k in range(8):
                mm(g, po, mT[:, k, mt * 128:(mt + 1) * 128], wm[:, k, :], start=(k == 0), stop=(k == 7))
            cp(g, "act", kv[mt], po)
            dma(g, dv(O["memk"][l][mt * 128:(mt + 1) * 128, :]), kv[mt][:, 0:256])
            dma(g, dv(O["memv"][l][mt * 128:(mt + 1) * 128, :]), kv[mt][:, 256:512])
            for h in range(4):
                hb = (h % 2) * 64
                cp(g, "dve", Vpm[:, mt, h, hb:hb + 64], kv[mt][:, 256 + h * 64:256 + (h + 1) * 64])
        for pair in range(2):
            pk = psum(g, [128, 256])
            for k in range(8):
                mm(g, pk, wm[:, k, pair * 128:(pair + 1) * 128], mT[:, k, :], start=(k == 0), stop=(k == 7))
            for hh in range(2):
                cp(g, "act", kTm[hh * 64:hh * 64 + 64, pair, hh, :], pk[hh * 64:hh * 64 + 64, :])
    g.kb.barrier()
    return kTm, Vpm


def mem_attn_alloc(g, S):
    S.mem_PT = S.sb("memPT", [128, 4, 128], BF16)
    S.mem_rs = S.sb("memrs", [128, 128], F32)
    S.smk = [S.sb("smk%d" % k, [128, 2, 256], F32) for k in range(1)] * 2
    S.smv = [S.sb("smv%d" % k, [128, 2, 256], F32) for k in range(1)] * 2
    S.smkb = [S.sb("smkb%d" % k, [128, 2, 256], BF16) for k in range(1)] * 2
    S.kTs = [S.sb("kTs%d" % k, [128, 2, 2, 256], BF16) for k in range(1)] * 2
    memset(g, "pool", S.kTs[0], 0.0)
    S.Vps = [S.sb("Vps%d" % k, [128, 2, 4, 128], BF16) for k in range(1)] * 2
    S.sPT = [S.sb("sPT%d" % k, [128, 2, 4, 8], BF16) for k in range(2)]
    memset(g, "pool", S.Vps[0], 0.0)


def mem_attn_prompt(g, S, qmT, kTm, Vpm, omT):
    for pair in range(2):
        ps_s = psum(g, [128, 4, 128])
        for hh in range(2):
            pb = hh * 64
            for mt in range(2):
                mm(g, ps_s[:, hh * 2 + mt, :], kTm[:, pair, hh, mt * 128:(mt + 1) * 128], qmT[:, pair, :])
        PT = S.mem_PT
        act(g, PT, ps_s, AF.Exp, scale=MEM_SCALE)
        po = psum(g, [128, 2, 128])
        n = 0
        for hh in range(2):
            h = pair * 2 + hh
            for mt in range(2):
                mm(g, po[:, 0, :], Vpm[:, mt, h, :], PT[:, hh * 2 + mt, :], start=(n == 0), stop=(n == 3))
                n += 1
        n = 0
        for hh in range(2):
            for mt in range(2):
                mm(g, po[:, 1, :], g.ones_half[:, hh, :], PT[:, hh * 2 + mt, :], start=(n == 0), stop=(n == 3))
                n += 1
        recip(g, S.mem_rs, po[:, 1, :])
        tt(g, "dve", omT[:, pair, :], po[:, 0, :], S.mem_rs, ALU.mult)


def mem_attn_sample(g, S, l, qmT, omT):
    I = g.I
    acc = psum_hold(g, 0, [128, 4, 128])
    for j in range(16):
        kin, vin, kb16, kTs, Vps, sPT = S.smk[j % 2], S.smv[j % 2], S.smkb[j % 2], S.kTs[j % 2], S.Vps[j % 2], S.sPT[j % 2]
        dma(g, kin, dv(I["cache_mem_k"][l, j].rearrange("(mt p) f -> p mt f", p=128)))
        dma(g, vin, dv(I["cache_mem_v"][l, j].rearrange("(mt p) f -> p mt f", p=128)))
        cp(g, "pool", kb16, kin)
        ptk = psum(g, [128, 4, 128], BF16)
        for pair in range(2):
            for mt in range(2):
                tr(g, ptk[:, pair * 2 + mt, :], kb16[:, mt, pair * 128:(pair + 1) * 128], g.ident_b)
        for pair in range(2):
            for hh in range(2):
                cp(g, "act", kTs[hh * 64:hh * 64 + 64, pair, hh, :].re("p (m k) -> p m k", m=2), ptk[hh * 64:hh * 64 + 64, pair * 2:pair * 2 + 2, :])
        vv = vin.re("p m (h d) -> p m h d", d=64)
        for hh in range(2):
            cp(g, "pool", Vps[:, :, hh::2, hh * 64:hh * 64 + 64], vv[:, :, hh::2, :])
        ps_s = psum(g, [128, 2, 4, 8])
        for pair in range(2):
            for hh in range(2):
                pb = hh * 64
                for mt in range(2):
                    mm(g, ps_s[:, pair, hh * 2 + mt, :], kTs[:, pair, hh, mt * 128:(mt + 1) * 128], qmT[:, pair, 8 * j:8 * j + 8])
        act(g, sPT, ps_s, AF.Exp, scale=MEM_SCALE)
        for pair in range(2):
            n = 0
            for hh in range(2):
                h = pair * 2 + hh
                for mt in range(2):
                    mm(g, acc[:, pair * 2, 8 * j:8 * j + 8], Vps[:, mt, h, :], sPT[:, pair, hh * 2 + mt, :], start=(n == 0), stop=(n == 3))
                    n += 1
            n = 0
            for hh in range(2):
                for mt in range(2):
                    mm(g, acc[:, pair * 2 + 1, 8 * j:8 * j + 8], g.ones_half[:, hh, :], sPT[:, pair, hh * 2 + mt, :], start=(n == 0), stop=(n == 3))
                    n += 1
    for pair in range(2):
        recip(g, S.mem_rs, acc[:, pair * 2 + 1, :])
        tt(g, "dve", omT[:, pair, :], acc[:, pair * 2, :], S.mem_rs, ALU.mult)

def a_chunks(l):
    ch = []
    for i in range(6):
        ch.append(("r", i, 128 * i, 128))
    ch.append(("wl", 0, 768, 64))
    for i in range(6):
        ch.append(("k", i, 832 + 128 * i, 128))
    for i in range(6):
        ch.append(("v", i, 1600 + 128 * i, 128))
    ch.append(("al", 0, 2368, 64))
    ch.append(("gl", 0, 2432, 128))
    ch.append(("gl", 1, 2560, 32))
    if l == 1:
        ch.append(("vl", 0, 2848, 32))
    return ch


def mixer_a_phase(g, l):
    I, O = g.I, g.O
    import os
    tiles = [int(t) for t in os.environ["DBG_TILES"].split(",") if t] if "DBG_TILES" in os.environ else list(range(NT))
    LIM = int(os.environ.get("DBG_STAGE", "99"))
    SLIM = int(os.environ.get("DBG_SCAN", "99"))
    with Scope(g, "a%d" % l) as S:
        kTm, Vpm = mem_prep(g, S, l)
        S.nxb = 2
        T1 = S.sb("T1", [128, 6, 128], F32)
        Al = S.sb("Al", [128, 6, 128], F32)
        S.stg = [T1.re("p c k -> p (c k)"), Al.re("p c k -> p (c k)")]
        S.stg_i = 0
        w_in = S.sb("w_in", [128, 8, 2880], BF16)
        load_w_into(g, S, w_in, I["w_in_a"][l], D, C_A, 0)
        if l == 1:
            load_w_into(g, S, w_in, I["w_vres_in"][0], D, 32, C_A)
        w_o = load_w_bf16(g, S, "w_o", I["w_o"][l], D, D)
        wd_up = load_w_bf16(g, S, "wd_up", I["w_decay_up"][l], 64, 768)
        wa_up = load_w_bf16(g, S, "wa_up", I["w_a_up"][l], 64, 768)
        wg_up = load_w_bf16(g, S, "wg_up", I["w_g_up"][l], 160, 768)
        if l == 1:
            wv_up = load_w_bf16(g, S, "wv_up", I["w_vres_up"][0], 32, 768)
            v0T = load_vec_T(g, S, "v0T", I["v0"][0], 768)
        gT = load_vec_T(g, S, "gmix", I["norm_mix"][l], D)
        a0T = load_vec_T(g, S, "a0T", I["a0"][l], 768)
        kkT = load_vec_T(g, S, "kkT", I["k_k"][l], 768)
        kaT = load_vec_T(g, S, "kaT", I["k_a"][l], 768)
        rkT = load_vec_T(g, S, "rkT", I["r_k"][l], 768)
        omka = S.sb("omka", [128, 6], F32)
        ts(g, "dve", omka, kaT, -1.0, ALU.mult, 1.0, ALU.add)
        w0b = load_bcast(g, S, "w0b", I["w0"][l], 768)
        lnw = S.sb("lnw", [128, 768], BF16)
        lnb = S.sb("lnb", [128, 768], BF16)
        for (dst_, nm_) in ((lnw, "lnx_w"), (lnb, "lnx_b")):
            stg_ = S.stg[S.stg_i % 2]
            S.stg_i += 1
            dma(g, stg_, dv(I[nm_][l].partition_broadcast(128)))
            cp(g, "pool", dst_, stg_)
        shf = S.sb("shf", [128, 8, 16], F32)
        chunks = a_chunks(l)
        nsh = len(chunks)
        MU = S.sb("MU", [128, nsh], F32)
        memset(g, "dve", MU, 0.0)
        for ci, (nm, idx, col0, w) in enumerate(chunks):
            src = I["mu_vres"][0] if nm == "vl" else I["mu_a"][l][col0:col0 + w]
            dma(g, MU[0:w, ci:ci + 1], dv(src.rearrange("(p o) -> p o", o=1)), slow=True)
        OMU = S.sb("OMU", [128, nsh], F32)
        ts(g, "dve", OMU, MU, -1.0, ALU.mult, 1.0, ALU.add)
        CARRY = S.sb("CARRY", [128, nsh], F32)
        memset(g, "dve", CARRY, 0.0)
        S.tmp_sq = S.sb("sq", [128, D], BF16)
        S.tmp_ss = S.sb("ss", [128, 1], F32)
        S.tmp_xn = S.sb("xn", [128, D], BF16)
        xnT2 = [S.sb("xnT%d" % k, [128, 8, 128], BF16) for k in range(2)]
        E = [S.sb("E%d" % k, [128, 4, 144], F32) for k in range(1)] * 2
        tmpm = [S.sb("tmpm%d" % k, [128, 128], F32) for k in range(1)] * 2
        tmpb = S.sb("tmpb", [128, 128], F32)
        Rm = S.sb("Rm", [128, 6, 128], F32)
        Km = S.sb("Km", [128, 6, 128], F32)
        Vm = S.sb("Vm", [128, 6, 128], F32)
        WL = S.sb("WL", [64, 128], F32)
        AL = S.sb("AL", [64, 128], F32)
        GL = S.sb("GL", [128, 2, 128], F32)
        VL = S.sb("VL", [32, 128], F32)
        WLb = S.sb("WLb", [64, 128], BF16)
        ALb = S.sb("ALb", [64, 128], BF16)
        GLb = S.sb("GLb", [128, 2, 128], BF16)
        VLb = S.sb("VLb", [32, 128], BF16)
        qmT = S.sb("qmT", [128, 2, 128], BF16)
        LW = S.sb("LW", [128, 768], F32)
        Gt = S.sb("Gt", [128, 768], BF16)
        Aa = S.sb("Aa", [128, 6, 128], F32)
        Kd = S.sb("Kd", [128, 6, 128], F32)
        T1b = S.sb("T1b", [128, 6, 128], BF16)
        vfb = T1b
        EX = [S.sb("EX%d" % k, [128, 4, 128], F32) for k in range(1)] * 2
        PEND = S.sb("PEND", [128, 6, 16], F32)
        ART = S.sb("ART", [128, 6, 2, 128], BF16)
        BtT = S.sb("BtT", [128, 6, 128], BF16)
        BtT2 = S.sb("BtT2", [128, 6, 2, 128], BF16)
        KtT2 = S.sb("KtT2", [128, 6, 2, 128], BF16)
        AtT2 = S.sb("AtT2", [128, 6, 2, 128], BF16)
        for t_ in (BtT2, KtT2, AtT2):
            memset(g, "pool", t_, 0.0)
        BhT = S.sb("BhT", [128, 6, 128], BF16)
        KhT = S.sb("KhT", [128, 6, 128], BF16)
        BoT = S.sb("BoT", [128, 6, 128], BF16)
        Atk = S.sb("Atk", [128, 12, 64], BF16)
        Vtk = S.sb("Vtk", [128, 12, 64], BF16)
        Vpad = S.sb("Vpad", [128, 12, 128], BF16)
        Bhk = S.sb("Bhk", [128, 12, 64], BF16)
        Khk = S.sb("Khk", [128, 12, 64], BF16)
        Botk = T1b.re("p c k -> p (c k)")
        memset(g, "pool", Vpad, 0.0)
        CDT = BF16 if os.environ.get("DBG_CHAIN_BF16", "0") == "1" else F32
        PT_ = [S.sb("P%d" % k, [128, 2, 128], CDT) for k in range(2)]
        PTT = [S.sb("PT%d" % k, [128, 2, 128], CDT) for k in range(2)]
        Gm = [S.sb("G%d" % k, [128, 2, 128], CDT) for k in range(2)]
        nDrb = S.sb("nDrb", [128, 2, 128], BF16)
        LkT = S.sb("LkT", [128, 2, 128], BF16)
        DrkT = S.sb("DrkT", [128, 2, 128], BF16)
        AW = S.sb("AW", [128, 2, 128], F32)
        Ab = S.sb("Ab", [128, 2, 64], BF16)
        Apad = S.sb("Apad", [128, 2, 128], BF16)
        nUb = S.sb("nUb", [128, 2, 64], BF16)
        nUpad = S.sb("nUpad", [128, 2, 128], BF16)
        memset(g, "pool", Apad, 0.0)
        memset(g, "pool", nUpad, 0.0)
        RbT = S.sb("RbT", [128, 128], F32)
        YbT = S.sb("YbT", [128, 128], F32)
        YT = S.sb("YT", [128, 6, 128], F32)
        Nst = S.sb("Nst", [128, 6, 128], F32)
        memset(g, "dve", Nst, 0.0)
        PhT = S.sb("PhT", [128, 128], F32)
        Gam = S.sb("Gam", [128, 128], F32)
        SG = 2
        shin = T1.re("p c k -> p (c k)")[0:16, :]
        shT = S.sb("shT", [128, 8, 16], BF16)
        Bbd = S.sb("Bbd", [128, SG, 128], BF16)
        Kbd = S.sb("Kbd", [128, SG, 128], BF16)
        PhT2 = S.sb("PhT2", [128, 2, 128], F32)
        PhTs = S.sb("PhTs", [128, SG, 128], F32)
        GamTs = S.sb("GamTs", [128, SG, 128], F32)
        S0in = S.sb("S0in", [128, SG, 64], F32)
        S0bd = S.sb("S0bd", [128, SG, 128], F32)
        N0bd = S.sb("N0bd", [128, SG, 128], F32)
        RS = S.sb("RS", [128, SG, 128], F32)
        memset(g, "pool", S0bd, 0.0)
        Ytk = LW
        st1 = S.sb("st1", [128, 12], F32)
        st2 = S.sb("st2", [128, 12], F32)
        otk = T1b.re("p c k -> p (c k)")
        oT = S.sb("oT", [128, 8, 128], BF16)
        mem_attn_alloc(g, S)
        print("A-phase SBUF remaining:", g.nc.sbuf_bytes_remaining)

        def pro_a(i_, k_):
            xt_ = x_load(g, S, i_)
            rmsnorm_T(g, S, xt_, gT, xnT2[k_], "a")
            return xt_

        xt_next = pro_a(tiles[0], 0)
        for n_, i in enumerate(tiles):
            samp = (i == NPT)
            nblk = 16 if samp else 2
            blk = 8 if samp else 64
            MS = g.ms_s if samp else g.ms_p
            MI = g.mi_s if samp else g.mi_p
            MST = g.mst_s if samp else g.mst_p
            TRI = g.tri_s if samp else g.tri_p
            xt = xt_next
            xnT = xnT2[n_ % 2]
            if samp:
                cp(g, "dve", shf, V(xnT.ap[:, :, 7:128:8], xnT.buf))
                for c in range(8):
                    dma(g, dv(O["shift_s"][l][:, c * 128:(c + 1) * 128].rearrange("j p -> p j")), shf[:, c, :], slow=True)
            elif i == NPT - 1:
                cp(g, "dve", shf[:, :, 0:1], xnT[:, :, 127:128])
                dma(g, dv(O["shift_p"][l].rearrange("(c p) -> p c", p=128)), shf[:, :, 0], slow=True)
            if samp:
                pts = psum(g, [128, 8, 16], F32)
                dma(g, shin[:, 0:768], dv(I["state_shift"][l][:, 0:768]))
                for k in range(6):
                    tr(g, pts[:, k, :], shin[:, k * 128:(k + 1) * 128], g.ident_f[0:16, 0:16])
                dma(g, shin[:, 0:256], dv(I["state_shift"][l][:, 768:1024]))
                for k in range(2):
                    tr(g, pts[:, 6 + k, :], shin[:, k * 128:(k + 1) * 128], g.ident_f[0:16, 0:16])
                cp(g, "act", shT, pts)
            dest = {"r": Rm, "k": Km, "v": Vm, "gl": GL}
            for gi, g0 in enumerate(range(0, nsh, 4)):
                grp = chunks[g0:g0 + 4]
                n = len(grp)
                Eb = E[gi % 2]
                pg = psum(g, [128, 4, 128])
                for c, (nm, idx, col0, w) in enumerate(grp):
                    for k in range(8):
                        mm(g, pg[0:w, c, :], w_in[:, k, col0:col0 + w], xnT[:, k, :], start=(k == 0), stop=(k == 7))
                if samp:
                    pg2 = psum(g, [128, 4, 16])
                    for c, (nm, idx, col0, w) in enumerate(grp):
                        for k in range(8):
                            mm(g, pg2[0:w, c, :], w_in[:, k, col0:col0 + w], shT[:, k, :], start=(k == 0), stop=(k == 7))
                    Ev = Eb.re("p c (j t) -> p c j t", t=9)
                    for c, (nm, idx, col0, w) in enumerate(grp):
                        cp(g, "act", Ev[0:w, c, :, 1:9], pg[0:w, c, :].re("p (j t) -> p j t", t=8))
                        cp(g, "dve", Ev[0:w, c, :, 0:1], pg2[0:w, c, :].re("p (j o) -> p j o", o=1))
                else:
                    cp(g, "dve", Eb[:, 0:n, 0:1], CARRY[:, g0:g0 + n].re("p (c o) -> p c o", o=1))
                    for c, (nm, idx, col0, w) in enumerate(grp):
                        cp(g, "act", Eb[0:w, c, 1:129], pg[0:w, c, :])
                        cp(g, "dve", CARRY[0:w, g0 + c:g0 + c + 1], Eb[0:w, c, 128:129])
                for c, (nm, idx, col0, w) in enumerate(grp):
                    ci = g0 + c
                    if nm in dest:
                        dst = dest[nm][0:w, idx, :]
                    else:
                        dst = {"wl": WL, "al": AL, "vl": VL}[nm][0:w, :]
                    tm = tmpm[ci % 2]
                    if samp:
                        Ev = Eb.re("p c (j t) -> p c j t", t=9)
                        cur = Ev[0:w, c, :, 1:9]
                        prev = Ev[0:w, c, :, 0:8]
                        dst = dst.re("p (j t) -> p j t", t=8)
                        tmv = tm[0:w, :].re("p (j t) -> p j t", t=8)
                    else:
                        cur = Eb[0:w, c, 1:129]
                        prev = Eb[0:w, c, 0:128]
                        tmv = tm[0:w, :]
                    act(g, tmv, prev, AF.Identity, scale=MU[0:w, ci:ci + 1])
                    stt(g, dst, cur, OMU[0:w, ci:ci + 1], tmv, ALU.mult, ALU.add)
            pg = psum(g, [128, 2, 128])
            for c in range(2):
                for k in range(8):
                    mm(g, pg[:, c, :], w_in[:, k, C_RWKV + c * 128:C_RWKV + (c + 1) * 128], xnT[:, k, :], start=(k == 0), stop=(k == 7))
            cp(g, "act", qmT, pg)
            if LIM < 1:
                x_store(g, i, xt)
                continue
            act(g, WLb, WL, AF.Tanh)
            cp(g, "act", ALb, AL)
            act(g, GLb[:, 0, :], GL[:, 0, :], AF.Sigmoid)
            act(g, GLb[0:32, 1, :], GL[0:32, 1, :], AF.Sigmoid)
            for nb, (c0, cn) in enumerate([(0, 512), (512, 256)]):
                pz = psum(g, [128, cn])
                mm(g, pz, WLb, wd_up[0:64, 0, c0:c0 + cn])
                tt(g, "dve", LW[:, c0:c0 + cn], pz, w0b[:, c0:c0 + cn], ALU.add)
            act(g, LW, LW, AF.Sigmoid)
            ts(g, "dve", LW, LW, -DECAY_C, ALU.mult)
            for nb, (c0, cn) in enumerate([(0, 512), (512, 256)]):
                pz = psum(g, [128, cn])
                mm(g, pz, GLb[:, 0, :], wg_up[:, 0, c0:c0 + cn], start=True, stop=False)
                mm(g, pz, GLb[0:32, 1, :], wg_up[0:32, 1, c0:c0 + cn], start=False, stop=True)
                cp(g, "act", Gt[:, c0:c0 + cn], pz)
            for half in range(2):
                pa = psum(g, [128, 3, 128])
                for c in range(3):
                    cc = half * 3 + c
                    mm(g, pa[:, c, :], wa_up[0:64, 0, cc * 128:(cc + 1) * 128], ALb)
                for c in range(3):
                    cc = half * 3 + c
                    act(g, Aa[:, cc, :], pa[:, c, :], AF.Sigmoid, bias=a0T[:, cc:cc + 1])
            if l == 0:
                cp(g, "act", vfb, Vm)
                dma(g, g.vf_scr[i], vfb)
            else:
                cp(g, "act", VLb, VL)
                dma(g, vfb, g.vf_scr[i])
                for half in range(2):
                    pa = psum(g, [128, 3, 128])
                    for c in range(3):
                        cc = half * 3 + c
                        mm(g, pa[:, c, :], wv_up[0:32, 0, cc * 128:(cc + 1) * 128], VLb)
                    for c in range(3):
                        cc = half * 3 + c
                        act(g, T1[:, cc, :], pa[:, c, :], AF.Sigmoid, bias=v0T[:, cc:cc + 1])
                tt(g, "dve", Al, vfb, Vm, ALU.subtract)
                tt(g, "dve", Al, Al, T1, ALU.mult)
                tt(g, "dve", Vm, Vm, Al, ALU.add)
            if LIM < 2:
                x_store(g, i, xt)
                continue
            for cc in range(6):
                act(g, Al[:, cc, :], Km[:, cc, :], AF.Identity, scale=kkT[:, cc:cc + 1])
                ts(g, "dve", Kd[:, cc, :], Aa[:, cc, :], kaT[:, cc:cc + 1], ALU.mult, omka[:, cc:cc + 1], ALU.add)
            tt(g, "dve", T1b, Al, Al, ALU.mult)
            for nb, (c0, cn) in enumerate([(0, 4), (4, 2)]):
                pss = psum(g, [128, cn, 128])
                for c in range(cn):
                    mm(g, pss[:, c, :], g.bd2_b, T1b[:, c0 + c, :])
                ts(g, "dve", T1[:, c0:c0 + cn, :], pss, 1e-24, ALU.max)
            act(g, T1, T1, AF.Sqrt)
            recip(g, T1, T1)
            tt(g, "dve", Al, Al, T1, ALU.mult)
            tt(g, "dve", Kd, Kd, Km, ALU.mult)
            tt(g, "dve", T1, Rm, Kd, ALU.mult)
            for cc in range(6):
                ts(g, "dve", T1b[:, cc, :], T1[:, cc, :], rkT[:, cc:cc + 1], ALU.mult)
            for nb, (c0, cn) in enumerate([(0, 4), (4, 2)]):
                pss = psum(g, [128, cn, 128])
                for c in range(cn):
                    mm(g, pss[:, c, :], g.bd2_b, T1b[:, c0 + c, :])
                tt(g, "dve", BoT[:, c0:c0 + cn, :], pss, Vm[:, c0:c0 + cn, :], ALU.mult)
            if LIM < 3:
                x_store(g, i, xt)
                continue
            for cc in range(6):
                pc = psum(g, [128, 3, 128])
                mm(g, pc.re("p a t -> p (a t)"), LW[:, cc * 128:(cc + 1) * 128], TRI.re("p a t -> p (a t)"))
                ex = EX[cc % 2]
                act(g, ex[:, 0:2, :], pc[:, 0:2, :], AF.Exp)
                act(g, ex[:, 2, :], pc[:, 0, :], AF.Exp, scale=-1.0)
                act(g, ex[:, 3, :], pc[:, 2, :], AF.Exp)
                tt(g, "dve", ART[:, cc, 0, :], Al[:, cc, :], ex[:, 1, :], ALU.mult)
                tt(g, "dve", ART[:, cc, 1, :], Rm[:, cc, :], ex[:, 0, :], ALU.mult)
                tt(g, "dve", tmpb, Al[:, cc, :], Aa[:, cc, :], ALU.mult)
                tt(g, "dve", BtT[:, cc, :], tmpb, ex[:, 2, :], ALU.mult)
                tt(g, "pool", BhT[:, cc, :], tmpb, ex[:, 3, :], ALU.mult)
                tt(g, "dve", KhT[:, cc, :], Kd[:, cc, :], ex[:, 3, :], ALU.mult)
                for hh in range(2):
                    pb = hh * 64
                    cp(g, "act", AtT2[pb:pb + 64, cc, hh, :], ART[pb:pb + 64, cc, 0, :])
                    cp(g, "pool", BtT2[pb:pb + 64, cc, hh, :], BtT[pb:pb + 64, cc, :])
                    tt(g, "dve", KtT2[pb:pb + 64, cc, hh, :], Kd[pb:pb + 64, cc, :], ex[pb:pb + 64, 2, :], ALU.mult)
                cp(g, "act", PEND[:, cc, 0:nblk], V(ex.ap[:, 0, blk - 1:128:blk], ex.buf))
            if LIM < 4:
                x_store(g, i, xt)
                continue
            cp(g, "act", T1b, Vm)
            for (srcT, sel, dstk) in [(ART, 0, Atk), (T1b, None, Vtk), (BhT, None, Bhk), (KhT, None, Khk)]:
                for half in range(2):
                    ptt = psum(g, [128, 3, 128], BF16)
                    for c in range(3):
                        cc = half * 3 + c
                        src = srcT[:, cc, sel, :] if sel is not None else srcT[:, cc, :]
                        tr(g, ptt[:, c, :], src, g.ident_b)
                    cp(g, "act", dstk[:, half * 6:half * 6 + 6, :].re("p h k -> p (h k)"), ptt.re("p c k -> p (c k)"))
            for hh in range(2):
                cp(g, "pool", Vpad[:, hh::2, hh * 64:hh * 64 + 64], Vtk[:, hh::2, :])
            for half in range(2):
                ptt = psum(g, [128, 3, 128], BF16)
                for c in range(3):
                    tr(g, ptt[:, c, :], BoT[:, half * 3 + c, :], g.ident_b)
                cp(g, "act", Botk[:, half * 384:(half + 1) * 384], ptt.re("p c k -> p (c k)"))
            if LIM < 5:
                x_store(g, i, xt)
                continue
            for cc in range(6):
                h0 = 2 * cc
                if cc == 2 and n_ + 1 < len(tiles):
                    xt_next = pro_a(tiles[n_ + 1], (n_ + 1) % 2)
                pg1 = psum(g, [128, 2, 256])
                pg2 = psum(g, [128, 2, 256])
                ppt = psum(g, [128, 2, 128])
                ar = ART[:, cc, :, :].re("p a t -> p (a t)")
                for hh in range(2):
                    mm(g, pg1[:, hh, :], BtT2[:, cc, hh, :], ar)
                    mm(g, pg2[:, hh, :], KtT2[:, cc, hh, :], ar)
                    mm(g, ppt[:, hh, :], AtT2[:, cc, hh, :], BtT[:, cc, :])
                P, PT = PT_[0], PTT[0]
                msb = MS.re("p (o t) -> p o t", o=1).bc([128, 2, 128])
                mib = MI.re("p (o t) -> p o t", o=1).bc([128, 2, 128])
                mstb = MST.re("p (o t) -> p o t", o=1).bc([128, 2, 128])
                stt(g, P, pg1[:, :, 0:128], -1.0, msb, ALU.mult, ALU.mult)
                stt(g, nDrb, pg1[:, :, 128:256], -1.0, mib, ALU.mult, ALU.mult)
                tt(g, "dve", LkT, pg2[:, :, 0:128], msb, ALU.mult)
                tt(g, "dve", DrkT, pg2[:, :, 128:256], mib, ALU.mult)
                stt(g, PT, ppt, -1.0, mstb, ALU.mult, ALU.mult)
                if SLIM < 1:
                    continue
                G0 = Gm[0]
                tt(g, "dve", G0, P, g.ident_f.re("p (o t) -> p o t", o=1).bc([128, 2, 128]), ALU.add)
                nit = 2 if samp else 5
                nit = min(nit, int(os.environ.get("DBG_NIT", "9")))
                cur = 0
                for it in range(nit):
                    Q, QT, Gc = PT_[cur], PTT[cur], Gm[cur]
                    Qn, QTn, Gn = PT_[1 - cur], PTT[1 - cur], Gm[1 - cur]
                    last = (it == nit - 1)
                    pq2t = psum(g, [128, 2, 128])
                    for hh in range(2):
                        mm(g, pq2t[:, hh, :], Q[:, hh, :], QT[:, hh, :])
                    if os.environ.get("DBG_SQ", "0") != "1":
                        cp(g, "act", QTn, pq2t)
                    if not last:
                        pq2 = psum(g, [128, 2, 128])
                        for hh in range(2):
                            mm(g, pq2[:, hh, :], QT[:, hh, :], Q[:, hh, :])
                        cp(g, "act", Qn, pq2)
                    if os.environ.get("DBG_SKIPG", "0") == "1":
                        continue
                    pgn = psum(g, [128, 2, 128])
                    for hh in range(2):
                        mm(g, pgn[:, hh, :], QTn[:, hh, :], Gc[:, hh, :])
                    tt(g, "dve", Gn, pgn, Gc, ALU.add)
                    cur = 1 - cur
                TTf = Gm[cur]
                if SLIM < 2:
                    continue
                pw = psum(g, [128, 2, 64])
                for hh in range(2):
                    mm(g, pw[:, hh, :], LkT[:, hh, :], Vtk[:, h0 + hh, :])
                cp(g, "act", AW[:, :, 64:128], pw)
                cp(g, "act", AW[:, :, 0:64], Atk[:, h0:h0 + 2, :])
                pau = psum(g, [128, 2, 128])
                for hh in range(2):
                    mm(g, pau[:, hh, :], TTf[:, hh, :], AW[:, hh, :])
                cp(g, "dve", Ab, pau[:, :, 0:64])
                ts(g, "dve", nUb, pau[:, :, 64:128], -1.0, ALU.mult)
                for hh in range(2):
                    cp(g, "dve", Apad[:, hh, hh * 64:hh * 64 + 64], pau[:, hh, 0:64])
                    cp(g, "dve", nUpad[:, hh, hh * 64:hh * 64 + 64], pau[:, hh, 64:128])
                if SLIM < 3:
                    continue
                pr = psum(g, [128, 2, 128])
                for hh in range(2):
                    mm(g, pr[:, 0, :], Apad[:, hh, :], nDrb[:, hh, :], start=(hh == 0), stop=(hh == 1))
                n = 0
                for hh in range(2):
                    mm(g, pr[:, 1, :], Vpad[:, h0 + hh, :], DrkT[:, hh, :], start=(n == 0), stop=False)
                    n += 1
                for hh in range(2):
                    mm(g, pr[:, 1, :], nUpad[:, hh, :], nDrb[:, hh, :], start=False, stop=(hh == 1))
                tt(g, "dve", RbT, pr[:, 0, :], ART[:, cc, 1, :], ALU.add)
                cp(g, "act", YbT, pr[:, 1, :])
                if SLIM < 4:
                    continue
                if not samp:
                    bpair = Bhk[:, h0:h0 + 2, :].re("p h k -> p (h k)").re("p (o k) -> p o k", o=1).bc([128, 2, 128])
                    kpair = Khk[:, h0:h0 + 2, :].re("p h k -> p (h k)").re("p (o k) -> p o k", o=1).bc([128, 2, 128])
                    bmsk = g.blkmask.re("p (j o) -> p j o", o=1).bc([128, 2, 128])
                    tt(g, "dve", Bbd, bpair, bmsk, ALU.mult)
                    tt(g, "pool", Kbd, kpair, bmsk, ALU.mult)
                    pe1 = psum(g, [128, 2, 128])
                    mm(g, pe1.re("p j k -> p (j k)"), Ab.re("p h k -> p (h k)"), Bbd.re("p j k -> p (j k)"))
                    tt(g, "dve", PhT2, pe1, g.bd2_f.re("p (o k) -> p o k", o=1).bc([128, 2, 128]), ALU.mult)
                    for c in range(2):
                        r0 = c * 64
                        stt(g, PhT, g.ident_f, PEND[:, cc, c:c + 1], PhT2[:, c, :], ALU.mult, ALU.subtract)
                        pe2 = psum(g, [128, 128])
                        mm(g, pe2, Kbd[:, c, :], Vtk[:, h0:h0 + 2, :].re("p h k -> p (h k)"), start=True, stop=False)
                        mm(g, pe2, Bbd[:, c, :], nUb.re("p h k -> p (h k)"), start=False, stop=True)
                        tt(g, "dve", Gam, pe2, g.bd2_f, ALU.mult)
                        py = psum(g, [128, 64])
                        mm(g, py, Nst[:, cc, :], RbT[:, r0:r0 + 64])
                        tt(g, "dve", YT[:, cc, r0:r0 + 64], py, YbT[:, r0:r0 + 64], ALU.add)
                        pn = psum(g, [128, 128])
                        mm(g, pn, PhT, Nst[:, cc, :])
                        tt(g, "dve", Nst[:, cc, :], pn, Gam, ALU.add)
                    if i == NPT - 1:
                        pst = psum(g, [128, 128])
                        tr(g, pst, Nst[:, cc, :], g.ident_f)
                        cp(g, "act", PhT, pst)
                        for hh in range(2):
                            dma(g, dv(O["wkv_p"][l][h0 + hh]), PhT[hh * 64:hh * 64 + 64, hh * 64:hh * 64 + 64])
                else:
                    py = psum_hold(g, 1, [128, 128])
                    smb = g.seqmask.re("p (j o) -> p j o", o=1)
                    bpair = Bhk[:, h0:h0 + 2, :].re("p h k -> p (h k)").re("p (o k) -> p o k", o=1).bc([128, SG, 128])
                    kpair = Khk[:, h0:h0 + 2, :].re("p h k -> p (h k)").re("p (o k) -> p o k", o=1).bc([128, SG, 128])
                    bd2b = g.bd2_f.re("p (o k) -> p o k", o=1).bc([128, SG, 128])
                    for sg in range(16 // SG):
                        j0 = sg * SG
                        dma(g, S0in, dv(I["state_wkv"][l][j0:j0 + SG, h0:h0 + 2].rearrange("j h v k -> (h v) j k")))
                        for hh in range(2):
                            cp(g, "pool", S0bd[hh * 64:hh * 64 + 64, :, hh * 64:hh * 64 + 64], S0in[hh * 64:hh * 64 + 64, :, :])
                        pst = psum(g, [128, SG, 128])
                        for jj in range(SG):
                            tr(g, pst[:, jj, :], S0bd[:, jj, :], g.ident_f)
                        cp(g, "act", N0bd, pst)
                        msk = smb[:, j0:j0 + SG, :].bc([128, SG, 128])
                        tt(g, "dve", Bbd, bpair, msk, ALU.mult)
                        tt(g, "dve", Kbd, kpair, msk, ALU.mult)
                        pe1 = psum(g, [128, SG, 128])
                        mm(g, pe1.re("p j k -> p (j k)"), Ab.re("p h k -> p (h k)"), Bbd.re("p j k -> p (j k)"))
                        tt(g, "dve", PhTs, pe1, bd2b, ALU.mult)
                        for jj in range(SG):
                            stt(g, PhTs[:, jj, :], g.ident_f, PEND[:, cc, j0 + jj:j0 + jj + 1], PhTs[:, jj, :], ALU.mult, ALU.subtract)
                        pe2 = psum(g, [128, SG, 128])
                        mm(g, pe2.re("p j k -> p (j k)"), Vtk[:, h0:h0 + 2, :].re("p h k -> p (h k)"), Kbd.re("p j k -> p (j k)"), start=True, stop=False)
                        mm(g, pe2.re("p j k -> p (j k)"), nUb.re("p h k -> p (h k)"), Bbd.re("p j k -> p (j k)"), start=False, stop=True)
                        tt(g, "dve", GamTs, pe2, bd2b, ALU.mult)
                        for jj in range(SG):
                            j = j0 + jj
                            mm(g, py[:, 8 * j:8 * j + 8], N0bd[:, jj, :], RbT[:, 8 * j:8 * j + 8])
                        pn = psum(g, [128, SG, 128])
                        for jj in range(SG):
                            mm(g, pn[:, jj, :], N0bd[:, jj, :], PhTs[:, jj, :])
                        tt(g, "dve", RS, pn, GamTs, ALU.add)
                        for hh in range(2):
                            dma(g, dv(O["wkv_s"][l][j0:j0 + SG, h0 + hh].rearrange("j v k -> v j k")), RS[hh * 64:hh * 64 + 64, :, hh * 64:hh * 64 + 64])
                    tt(g, "dve", YT[:, cc, :], py, YbT, ALU.add)
            if LIM < 6:
                x_store(g, i, xt)
                continue
            for half in range(2):
                pty = psum(g, [128, 3, 128])
                for c in range(3):
                    tr(g, pty[:, c, :], YT[:, half * 3 + c, :], g.ident_f)
                cp(g, "act", Ytk[:, half * 384:(half + 1) * 384], pty.re("p c k -> p (c k)"))
            y3 = Ytk.re("p (h k) -> p h k", k=64)
            red(g, st1, y3)
            ts(g, "dve", st1, st1, 1.0 / 64, ALU.mult)
            tt(g, "dve", y3, y3, st1.re("p (h o) -> p h o", o=1).bc([128, 12, 64]), ALU.subtract)
            t13 = T1.re("p c k -> p (c k)").re("p (h k) -> p h k", k=64)
            tt(g, "dve", t13, y3, y3, ALU.mult)
            red(g, st2, t13)
            ts(g, "dve", st2, st2, 1.0 / 64, ALU.mult, GN_EPS, ALU.add)
            act(g, st2, st2, AF.Sqrt)
            recip(g, st2, st2)
            tt(g, "dve", y3, y3, st2.re("p (h o) -> p h o", o=1).bc([128, 12, 64]), ALU.mult)
            tt(g, "dve", Ytk, Ytk, lnw, ALU.mult)
            tt(g, "dve", Ytk, Ytk, lnb, ALU.add)
            tt(g, "dve", Ytk, Ytk, Botk, ALU.add)
            tt(g, "dve", otk, Ytk, Gt, ALU.mult)
            if g.taps is not None and "otok" in g.taps:
                dma(g, dv(O["dbg_otok"][l, i]), otk_f(g, S, otk))
            for half in range(2):
                pto = psum(g, [128, 3, 128], BF16)
                for c in range(3):
                    tr(g, pto[:, c, :], otk[:, (half * 3 + c) * 128:(half * 3 + c + 1) * 128], g.ident_b)
                cp(g, "act", oT[:, half * 3:half * 3 + 3, :], pto)
            if LIM < 7:
                x_store(g, i, xt)
                continue
            if samp:
                mem_attn_sample(g, S, l, qmT, oT[:, 6:8, :])
            else:
                mem_attn_prompt(g, S, qmT, kTm, Vpm, oT[:, 6:8, :])
            if LIM < 8:
                x_store(g, i, xt)
                continue
            for nb in range(2):
                po = psum(g, [128, 512])
                for k in range(8):
                    mm(g, po, oT[:, k, :], w_o[:, k, nb * 512:(nb + 1) * 512], start=(k == 0), stop=(k == 7))
                tt(g, "dve", xt[:, nb * 512:(nb + 1) * 512], xt[:, nb * 512:(nb + 1) * 512], po, ALU.add)
            x_store(g, i, xt)
    g.kb.barrier()


def otk_f(g, S, otk):
    if not hasattr(S, "otkf"):
        S.otkf = S.sb("otkf", [128, 768], F32)
    cp(g, "dve", S.otkf, otk)
    return S.otkf

def gather(g, out, rows_ap, idx_view):
    g.kb.op("pool", lambda e: e.indirect_dma_start(out=out.ap, out_offset=None, in_=rows_ap,
                                                    in_offset=bass.IndirectOffsetOnAxis(ap=idx_view.ap, axis=0)),
            reads=[idx_view], writes=[out], dma=True)


def latent_phase(g, SB):
    I, O = g.I, g.O
    SB.ckv_tok = SB.sb("ckv_tok", [128, NT, 256], BF16)
    SB.ckvT = SB.sb("ckvT", [128, 2, NT * 128], BF16)
    SB.kpeT = SB.sb("kpeT", [64, NT * 128], BF16)
    with Scope(g, "lat") as S:
        wkva = load_w_bf16(g, S, "wkva", I["w_kv_a"], D, 384)
        gT = load_vec_T(g, S, "gkv", I["norm_kv"], D)
        gck = load_bcast(g, S, "gck", I["norm_ckv"], 256)
        rt = S.sb("rt", [128, NT, 2, 64], F32)
        dma(g, rt.re("p i a d -> p i (a d)"), dv(I["c_rope_tok"].rearrange("i p a d -> p i (a d)")))
        S.tmp_sq = S.sb("sq", [128, D], BF16)
        S.tmp_ss = S.sb("ss", [128, 1], F32)
        S.tmp_xn = S.sb("xn", [128, D], BF16)
        xkT = S.sb("xkT", [128, 8, 128], BF16)
        hh_ = S.sb("h", [128, 384], F32)
        ck = [S.sb("ck%d" % k, [128, 256], F32) for k in range(2)]
        kp = [S.sb("kp%d" % k, [128, 64], F32) for k in range(2)]
        kpb = S.sb("kpb", [128, 64], BF16)
        t64 = S.sb("t64", [128, 64], F32)
        ss2 = S.sb("ss2", [128, 1], F32)
        junk = S.sb("junk", [128, 256], BF16)
        for i in range(NT):
            xt = x_load(g, S, i)
            rmsnorm_T(g, S, xt, gT, xkT, "k")
            ph = psum(g, [128, 384])
            for k in range(8):
                mm(g, ph, xkT[:, k, :], wkva[:, k, :], start=(k == 0), stop=(k == 7))
            cp(g, "act", hh_, ph)
            act(g, junk, hh_[:, 0:256], AF.Square, accum=ss2)
            ts(g, "dve", ss2, ss2, 1.0 / 256, ALU.mult, RMS_EPS, ALU.add)
            act(g, ss2, ss2, AF.Sqrt)
            recip(g, ss2, ss2)
            c_ = ck[i % 2]
            stt(g, c_, hh_[:, 0:256], ss2, gck, ALU.mult, ALU.mult)
            dma(g, dv(O["ckv"][i]), c_)
            cp(g, "pool", SB.ckv_tok[:, i, :], c_)
            k_ = kp[i % 2]
            tt(g, "dve", k_, hh_[:, 256:320], rt[:, i, 0, :], ALU.mult)
            tt(g, "dve", t64, hh_[:, 320:384], rt[:, i, 1, :], ALU.mult)
            tt(g, "dve", k_, k_, t64, ALU.add)
            dma(g, dv(O["kpe"][i]), k_)
            cp(g, "pool", kpb, k_)
            pt = psum(g, [128, 3, 128], BF16)
            for cc in range(2):
                tr(g, pt[:, cc, :], SB.ckv_tok[:, i, cc * 128:(cc + 1) * 128], g.ident_b)
            tr(g, pt[0:64, 2, :], kpb, g.ident_b)
            cp(g, "act", SB.ckvT[:, :, i * 128:(i + 1) * 128], pt[:, 0:2, :])
            cp(g, "act", SB.kpeT[:, i * 128:(i + 1) * 128], pt[0:64, 2, :])
    g.kb.barrier()


def mixer_b_phase(g, l, SB):
    I, O = g.I, g.O
    j_ = l - 2
    import os
    tiles = [int(t) for t in os.environ["DBG_TILES"].split(",") if t] if "DBG_TILES" in os.environ else list(range(NT))
    with Scope(g, "b%d" % l) as S:
        kTm, Vpm = mem_prep(g, S, l)
        S.nxb = 2
        w_in = load_w_bf16(g, S, "w_inb", I["w_in_b"][j_], D, 512)
        wq = load_w_bf16(g, S, "wq", I["w_q_b"][j_], 256, 1536)
        wuk = load_w_bf16(g, S, "wuk", I["wukT"].rearrange("d h c -> d (h c)"), 128, 1536)
        wuv = load_w_bf16(g, S, "wuv", I["wuv"].rearrange("c h v -> c (h v)"), 256, 768)
        w_o = load_w_bf16(g, S, "w_o", I["w_o"][l], D, D)
        gT = load_vec_T(g, S, "gmix", I["norm_mix"][l], D)
        gq = load_vec_T(g, S, "gq", I["norm_q"][j_], 256)
        rT = S.sb("rT", [64, NT, 2, 128], BF16)
        dma(g, rT.re("p i a t -> p i (a t)"), dv(I["c_rope_T"].rearrange("p i a t -> p i (a t)")))
        S.tmp_sq = S.sb("sq", [128, D], BF16)
        S.tmp_ss = S.sb("ss", [128, 1], F32)
        S.tmp_xn = S.sb("xn", [128, D], BF16)
        t1 = S.sb("t1", [64, 128], F32)
        t2 = S.sb("t2", [64, 128], F32)
        PTb = [S.sb("PTb%d" % k, [128, 4, 128], BF16) for k in range(2)]
        rs2 = [S.sb("rs%d" % k, [128, 128], F32) for k in range(2)]
        olat2 = [S.sb("olat%d" % k, [128, 2, 128], BF16) for k in range(2)]
        oT = S.sb("oT", [128, 8, 128], BF16)
        ss2 = S.sb("ss2", [128, 1], F32)
        junk = S.sb("junk", [128, 256], BF16)
        ones_b = S.sb("ones_b", [128, 128], BF16)
        memset(g, "pool", ones_b, 1.0)
        qlat_s = S.sb("qlat_s", [128, 2, 16, 48], BF16)
        qpe_s = S.sb("qpe_s", [64, 16, 48], BF16)
        GK = [S.sb("GK%d" % k, [128, 8, 256], BF16) for k in range(2)]
        GP = [S.sb("GP%d" % k, [128, 8, 64], BF16) for k in range(2)]
        KT = [S.sb("KT%d" % k, [128, 2, 3, 128], BF16) for k in range(2)]
        PTs = [S.sb("PTs%d" % k, [128, 8, 48], BF16) for k in range(2)]
        OL = S.sb("OL", [128, 3, 16, 48], F32)
        rsS = S.sb("rsS", [128, 16, 48], F32)
        olat_s = S.sb("olat_s", [128, 2, 16, 48], BF16)
        idx = S.sb("idx", [128, 16, 8], I32)
        ptb = S.sb("ptb", [128, 16, 8], I32)
        sub16 = S.sb("sub16", [128, 1], F32)
        dma(g, ptb, dv(I["ptb"]))
        dma(g, sub16, dv(I["c_sub16"]))
        ts(g, "dve", idx, ptb, 16.0, ALU.mult, sub16[:, 0:1], ALU.add)
        mem_attn_alloc(g, S)
        rows_ckv = I["cache_ckv"].rearrange("n (s t) d -> (n s) (t d)", s=16)
        rows_kpe = I["cache_kpe"].rearrange("n (s t) d -> (n s) (t d)", s=16)

        class QS:
            pass
        Qs = []
        for k_ in range(2):
            Q_ = QS()
            Q_.xnT = S.sb("xnT%d" % k_, [128, 8, 128], BF16)
            Q_.qa = S.sb("qa%d" % k_, [128, 256], F32)
            Q_.qn = S.sb("qn%d" % k_, [128, 256], BF16)
            Q_.qnT = S.sb("qnT%d" % k_, [128, 2, 128], BF16)
            Q_.qmT = S.sb("qmT%d" % k_, [128, 2, 128], BF16)
            Q_.qnope = S.sb("qnope%d" % k_, [128, 6, 128], BF16)
            Q_.qpe = S.sb("qpe%d" % k_, [64, 6, 128], BF16)
            Q_.qlat = S.sb("qlat%d" % k_, [128, 6, 2, 128], BF16)
            Qs.append(Q_)

        def pro(i, Q):
            xnT, qa, qn, qnT, qmT, qnope, qpe, qlat = Q.xnT, Q.qa, Q.qn, Q.qnT, Q.qmT, Q.qnope, Q.qpe, Q.qlat
            Q.xt = x_load(g, S, i)
            xt = Q.xt
            rmsnorm_T(g, S, xt, gT, xnT, "b")
            yield
            pq = psum(g, [128, 256])
            for k in range(8):
                mm(g, pq, xnT[:, k, :], w_in[:, k, 0:256], start=(k == 0), stop=(k == 7))
            cp(g, "act", qa, pq)
            act(g, junk, qa, AF.Square, accum=ss2)
            ts(g, "dve", ss2, ss2, 1.0 / 256, ALU.mult, RMS_EPS, ALU.add)
            act(g, ss2, ss2, AF.Sqrt)
            recip(g, ss2, ss2)
            ts(g, "dve", qn, qa, ss2, ALU.mult)
            pt = psum(g, [128, 2, 128], BF16)
            for c in range(2):
                tr(g, pt[:, c, :], qn[:, c * 128:(c + 1) * 128], g.ident_b)
            for c in range(2):
                ts(g, "dve", qnT[:, c, :], pt[:, c, :], gq[:, c:c + 1], ALU.mult)
            pm = psum(g, [128, 2, 128])
            for c in range(2):
                for k in range(8):
                    mm(g, pm[:, c, :], w_in[:, k, 256 + c * 128:256 + (c + 1) * 128], xnT[:, k, :], start=(k == 0), stop=(k == 7))
            cp(g, "act", qmT, pm)
            yield
            for h in range(6):
                ph = psum(g, [128, 3, 128])
                for kc in range(2):
                    mm(g, ph[:, 0, :], wq[:, kc, h * 192:h * 192 + 128], qnT[:, kc, :], start=(kc == 0), stop=(kc == 1))
                for kc in range(2):
                    mm(g, ph[0:64, 1, :], wq[:, kc, h * 192 + 128:h * 192 + 192], qnT[:, kc, :], start=(kc == 0), stop=(kc == 1))
                for kc in range(2):
                    mm(g, ph[0:64, 2, :], wq[:, kc, 1152 + h * 64:1152 + (h + 1) * 64], qnT[:, kc, :], start=(kc == 0), stop=(kc == 1))
                cp(g, "act", qnope[:, h, :], ph[:, 0, :])
                tt(g, "dve", t1, ph[0:64, 1, :], rT[:, i, 0, :], ALU.mult)
                tt(g, "dve", t2, ph[0:64, 2, :], rT[:, i, 1, :], ALU.mult)
                tt(g, "dve", qpe[:, h, :], t1, t2, ALU.add)
                if h % 2 == 1:
                    yield
            for h in range(6):
                pl = psum(g, [128, 2, 128])
                for cc in range(2):
                    mm(g, pl[:, cc, :], wuk[:, 0, h * 256 + cc * 128:h * 256 + (cc + 1) * 128], qnope[:, h, :])
                cp(g, "act", qlat[:, h, :, :], pl)
            yield

        def drain(gen):
            if gen is not None:
                for _ in gen:
                    pass

        def step(gen):
            if gen is not None:
                next(gen, None)

        drain(pro(tiles[0], Qs[0]))
        for n_, i in enumerate(tiles):
            samp = (i == NPT)
            Q = Qs[n_ % 2]
            xt, qmT, qpe, qlat = Q.xt, Q.qmT, Q.qpe, Q.qlat
            nxt = pro(tiles[n_ + 1], Qs[(n_ + 1) % 2]) if n_ + 1 < len(tiles) else None
            if not samp:
                for h in range(6):
                    acc = psum_hold(g, h % 2, [128, 3, 128])
                    rs, olat = rs2[h % 2], olat2[h % 2]
                    nkt = i + 1
                    for k0 in range(0, nkt, 4):
                        kn = min(4, nkt - k0)
                        pss = psum(g, [128, 4, 128])
                        for kk in range(kn):
                            kt = k0 + kk
                            ks = slice(kt * 128, (kt + 1) * 128)
                            mm(g, pss[:, kk, :], SB.ckvT[:, 0, ks], qlat[:, h, 0, :], start=True, stop=False)
                            mm(g, pss[:, kk, :], SB.ckvT[:, 1, ks], qlat[:, h, 1, :], start=False, stop=False)
                            mm(g, pss[:, kk, :], SB.kpeT[:, ks], qpe[:, h, :], start=False, stop=True)
                        P_ = PTb[(k0 // 4) % 2]
                        act(g, P_[:, 0:kn, :], pss[:, 0:kn, :], AF.Exp, scale=ATTN_SCALE)
                        if k0 + kn == nkt:
                            tt(g, "dve", P_[:, kn - 1, :], P_[:, kn - 1, :], g.caus, ALU.mult)
                        for a in range(3):
                            for kk in range(kn):
                                kt = k0 + kk
                                first = (kt == 0)
                                lastk = (kt == nkt - 1)
                                st_ = first and a == 0
                                if a < 2:
                                    mm(g, acc[:, a, :], SB.ckv_tok[:, kt, a * 128:(a + 1) * 128], P_[:, kk, :], start=st_, stop=lastk)
                                else:
                                    mm(g, acc[:, 2, :], ones_b, P_[:, kk, :], start=st_, stop=lastk)
                    recip(g, rs, acc[:, 2, :])
                    tt(g, "dve", olat, acc[:, 0:2, :], rs.re("p (o q) -> p o q", o=1).bc([128, 2, 128]), ALU.mult)
                    po_ = psum(g, [128, 128])
                    for cc in range(2):
                        mm(g, po_, wuv[:, cc, h * 128:(h + 1) * 128], olat[:, cc, :], start=(cc == 0), stop=(cc == 1))
                    cp(g, "act", oT[:, h, :], po_)
                    step(nxt)
            else:
                for h in range(6):
                    cp(g, "pool", qlat_s[:, :, :, h * 8:(h + 1) * 8], qlat[:, h, :, :].re("p c (j t) -> p c j t", t=8))
                    cp(g, "pool", qpe_s[:, :, h * 8:(h + 1) * 8], qpe[:, h, :].re("p (j t) -> p j t", t=8))
                for j in range(16):
                    acc = psum_hold(g, j % 2, [128, 3, 48])
                    for gi in range(8):
                        gk, gp = GK[gi % 2], GP[gi % 2]
                        gather(g, gk.re("p u d -> p (u d)"), rows_ckv, idx[:, j, gi:gi + 1])
                        gather(g, gp.re("p u d -> p (u d)"), rows_kpe, idx[:, j, gi:gi + 1])
                        pss = psum(g, [128, 8, 48])
                        for u2 in range(4):
                            kt_ = KT[u2 % 2]
                            ptk = psum(g, [128, 2, 3, 128], BF16)
                            for uu in range(2):
                                u = u2 * 2 + uu
                                for cc in range(2):
                                    tr(g, ptk[:, uu, cc, :], gk[:, u, cc * 128:(cc + 1) * 128], g.ident_b)
                                tr(g, ptk[0:64, uu, 2, :], gp[:, u, :], g.ident_b)
                            cp(g, "act", kt_[:, :, 0:2, :], ptk[:, :, 0:2, :])
                            cp(g, "act", kt_[0:64, :, 2, :], ptk[0:64, :, 2, :])
                            for uu in range(2):
                                u = u2 * 2 + uu
                                mm(g, pss[:, u, :], kt_[:, uu, 0, :], qlat_s[:, 0, j, :], start=True, stop=False)
                                mm(g, pss[:, u, :], kt_[:, uu, 1, :], qlat_s[:, 1, j, :], start=False, stop=False)
                                mm(g, pss[:, u, :], kt_[0:64, uu, 2, :], qpe_s[:, j, :], start=False, stop=True)
                        P_ = PTs[gi % 2]
                        act(g, P_, pss, AF.Exp, scale=ATTN_SCALE)
                        for u in range(8):
                            first = (gi == 0 and u == 0)
                            lastk = (gi == 7 and u == 7)
                            for cc in range(2):
                                mm(g, acc[:, cc, :], gk[:, u, cc * 128:(cc + 1) * 128], P_[:, u, :], start=(first and cc == 0), stop=lastk)
                            mm(g, acc[:, 2, :], ones_b, P_[:, u, :], start=False, stop=lastk)
                    cp(g, "act", OL[:, :, j, :], acc)
                ks = slice(NPT * 128, NT * 128)
                for h in range(6):
                    pss = psum(g, [128, 128])
                    mm(g, pss, SB.ckvT[:, 0, ks], qlat[:, h, 0, :], start=True, stop=False)
                    mm(g, pss, SB.ckvT[:, 1, ks], qlat[:, h, 1, :], start=False, stop=False)
                    mm(g, pss, SB.kpeT[:, ks], qpe[:, h, :], start=False, stop=True)
                    P_ = PTb[h % 2]
                    act(g, P_[:, 0, :], pss, AF.Exp, scale=ATTN_SCALE)
                    tt(g, "dve", P_[:, 0, :], P_[:, 0, :], g.mi_s, ALU.mult)
                    pn = psum(g, [128, 3, 128])
                    for cc in range(2):
                        mm(g, pn[:, cc, :], SB.ckv_tok[:, NPT, cc * 128:(cc + 1) * 128], P_[:, 0, :])
                    mm(g, pn[:, 2, :], ones_b, P_[:, 0, :])
                    olv = OL[:, :, :, h * 8:(h + 1) * 8]
                    for a in range(3):
                        tt(g, "dve", OL[:, a, :, h * 8:(h + 1) * 8], OL[:, a, :, h * 8:(h + 1) * 8], pn[:, a, :].re("p (j t) -> p j t", t=8), ALU.add)
                recip(g, rsS, OL[:, 2, :, :])
                for cc in range(2):
                    tt(g, "dve", olat_s[:, cc, :, :], OL[:, cc, :, :], rsS, ALU.mult)
                for h in range(6):
                    po_ = psum(g, [128, 16, 8])
                    for cc in range(2):
                        mm(g, po_, wuv[:, cc, h * 128:(h + 1) * 128], olat_s[:, cc, :, h * 8:(h + 1) * 8], start=(cc == 0), stop=(cc == 1))
                    cp(g, "act", oT[:, h, :], po_.re("p j t -> p (j t)"))
            if samp:
                mem_attn_sample(g, S, l, qmT, oT[:, 6:8, :])
            else:
                mem_attn_prompt(g, S, qmT, kTm, Vpm, oT[:, 6:8, :])
            for nb in range(2):
                po = psum(g, [128, 512])
                for k in range(8):
                    mm(g, po, oT[:, k, :], w_o[:, k, nb * 512:(nb + 1) * 512], start=(k == 0), stop=(k == 7))
                tt(g, "dve", xt[:, nb * 512:(nb + 1) * 512], xt[:, nb * 512:(nb + 1) * 512], po, ALU.add)
            x_store(g, i, xt)
            drain(nxt)
    g.kb.barrier()


IN_SPECS = None
USE_CACHE = True


def build_program(n_phys, mode="full", dbg=None):
    nc = bass.Bass("TRN2", target_bir_lowering=False)
    g = G()
    g.nc = nc
    g.kb = KB(nc)
    g.dbg = dbg or {}
    I = {}
    O = {}

    def din(name, shape, dt=F32):
        I[name] = nc.dram_tensor(name, list(shape), dt, kind="ExternalInput").ap()

    def dout(name, shape, dt=F32):
        O[name] = nc.dram_tensor("o_" + name, list(shape), dt, kind="ExternalOutput").ap()

    din("xin", [NT, 128, D])
    if USE_CACHE:
        din("cache_ckv", [n_phys, 128, 256])
        din("cache_kpe", [n_phys, 128, 64])
        din("ptb", [128, 16, 8], I32)
    din("cache_mem_k", [4, 16, NMEM, 256])
    din("cache_mem_v", [4, 16, NMEM, 256])
    din("state_wkv", [2, 16, 12, 64, 64])
    din("state_shift", [2, 16, D])
    din("state_conv", [4, 16, 2, DFF])
    din("mem_prompt", [NMEM, D])
    for nm, shp in [("norm_mix", [4, D]), ("norm_ffn", [4, D]), ("norm_mem", [4, D]), ("w_mem_kv", [4, D, 512]),
                    ("w_o", [4, D, D]), ("w_in_a", [2, D, C_A]), ("mu_a", [2, C_RWKV]), ("w_vres_in", [1, D, 32]),
                    ("mu_vres", [1, 32]), ("w_decay_up", [2, 64, 768]), ("w0", [2, 768]), ("w_a_up", [2, 64, 768]),
                    ("a0", [2, 768]), ("w_g_up", [2, 160, 768]), ("w_vres_up", [1, 32, 768]), ("v0", [1, 768]),
                    ("k_k", [2, 768]), ("k_a", [2, 768]), ("r_k", [2, 768]), ("lnx_w", [2, 768]), ("lnx_b", [2, 768]),
                    ("norm_kv", [D]), ("w_kv_a", [D, 384]), ("norm_ckv", [256]), ("wukT", [128, 6, 256]),
                    ("wuv", [256, 6, 128]), ("w_in_b", [2, D, 512]), ("norm_q", [2, 256]), ("w_q_b", [2, 256, 1536]),
                    ("w_ffn_up", [4, D, 2 * DFF]), ("conv_w", [4, 3, DFF]), ("conv_b", [4, DFF]),
                    ("w_ffn_down", [4, DFF, D]), ("final_norm", [D])]:
        din(nm, shp)
    din("c_ident_b", [128, 128], BF16)
    din("c_ident_f", [128, 128])
    din("c_masks", [128, 6, 128])
    din("c_tri", [128, 2, 3, 128])
    din("c_bd2", [128, 128])
    din("c_ones_half", [128, 2, 128], BF16)
    din("c_seqmask", [128, 18])
    din("c_caus", [128, 128])
    din("c_sub16", [128, 1])
    din("c_rope_tok", [NT, 128, 2, 64])
    din("c_rope_T", [64, NT, 2, 128], BF16)
    dout("y", [NT, 128, D])
    dout("ckv", [NT, 128, 256])
    dout("kpe", [NT, 128, 64])
    dout("memk", [4, NMEM, 256])
    dout("memv", [4, NMEM, 256])
    dout("wkv_p", [2, 12, 64, 64])
    dout("shift_p", [2, D])
    dout("conv_p", [4, 2, DFF])
    dout("wkv_s", [2, 16, 12, 64, 64])
    dout("shift_s", [2, 16, D])
    dout("conv_s", [4, 16, 2, DFF])
    g.taps = g.dbg
    for k, shp in g.dbg.items():
        dout("dbg_" + k, shp)
    vf = nc.dram_tensor("scr_vf", [NT, 128, 6, 128], BF16, kind="Internal").ap()
    g.vf_scr = [V(vf[i], Buf("vf_%d" % i)) for i in range(NT)]
    g.I = I
    g.O = O
    xn2 = nc.dram_tensor("scr_xn2", [NT, 128, 8, 128], BF16, kind="Internal").ap()
    g.xn2_scr = [V(xn2[i], Buf("xn2_%d" % i)) for i in range(NT)]

    with contextlib.ExitStack() as st:
        xs_d = nc.dram_tensor("scr_x", [NT, 128, D], F32, kind="Internal").ap()
        g.x_scr = [V(xs_d[i], Buf("x%d" % i)) for i in range(NT)]
        g.x_src = [V(I["xin"][i], None) for i in range(NT)]
        ib = st.enter_context(nc.sbuf_tensor("identb", [128, 128], BF16))
        g.ident_b = V(ib[:], Buf("identb"))
        idf = st.enter_context(nc.sbuf_tensor("identf", [128, 128], F32))
        g.ident_f = V(idf[:], Buf("identf"))
        g.ps_f = []
        g.ps_b = []
        for k in range(8):
            ph = st.enter_context(nc.psum_tensor("psf%d" % k, [128, 512], F32))
            b = Buf("ps%d" % k, excl=True)
            g.ps_f.append(V(ph[:], b))
            g.ps_b.append(V(ph[:].bitcast(BF16), b))
        g.ps_i = 0
        dma(g, g.ident_b, dv(I["c_ident_b"]))
        dma(g, g.ident_f, dv(I["c_ident_f"]))

        def const(name, shape, dt, src):
            h = st.enter_context(nc.sbuf_tensor("k_" + name, list(shape), dt))
            v = V(h[:], Buf(name))
            dma(g, v, dv(src))
            return v
        mk = const("masks", [128, 6, 128], F32, I["c_masks"])
        g.ms_p, g.mi_p, g.mst_p, g.ms_s, g.mi_s, g.mst_s = [mk[:, k, :] for k in range(6)]
        tri = const("tri", [128, 2, 3, 128], F32, I["c_tri"])
        g.tri_p, g.tri_s = tri[:, 0], tri[:, 1]
        g.bd2_f = const("bd2f", [128, 128], F32, I["c_bd2"])
        g.bd2_b = const("bd2b", [128, 128], BF16, I["c_ident_b"])
        cp(g, "pool", g.bd2_b, g.bd2_f)
        g.ones_half = const("onesh", [128, 2, 128], BF16, I["c_ones_half"])
        sm_ = const("seqm", [128, 18], F32, I["c_seqmask"])
        g.seqmask = sm_[:, 0:16]
        g.blkmask = sm_[:, 16:18]
        g.caus = const("caus", [128, 128], F32, I["c_caus"])
        stages = g.dbg.get("_stages", None)
        import os
        if mode == "ffn":
            for l in range(int(os.environ.get("DBG_NL", "4"))):
                ffn_phase(g, l, 0)
                ffn_phase(g, l, 1)
        elif mode == "a0":
            mixer_a_phase(g, 0)
        elif mode in ("full", "b2"):
            nl = int(os.environ.get("DBG_NL", "4"))
            if mode == "full":
                for l in range(min(nl, 2)):
                    mixer_a_phase(g, l)
                    ffn_phase(g, l, 0)
                    ffn_phase(g, l, 1)
            if nl > 2 or mode == "b2":
                with Scope(g, "B") as SB:
                    latent_phase(g, SB)
                    for l in range(2, nl if mode == "full" else 3):
                        mixer_b_phase(g, l, SB)
                        if mode == "full":
                            ffn_phase(g, l, 0)
                            ffn_phase(g, l, 1)
        final_phase(g)
        g.kb.finalize()
    return nc, g


def _bf(a):
    return np.ascontiguousarray(a).astype(ml_dtypes.bfloat16)


def core_inputs(a, c_unused, n_phys):
    f = lambda v: np.ascontiguousarray(v, dtype=np.float32)
    m = {}
    xin = np.concatenate([a["x_prompt"].reshape(NPT, 128, D), a["x_sample"].reshape(1, 128, D)], 0)
    m["xin"] = f(xin)
    if USE_CACHE:
        m["cache_ckv"] = f(a["cache_ckv"])
        m["cache_kpe"] = f(a["cache_kpe"])
        pt = np.asarray(a["page_table"]).astype(np.int32)
        ptb = np.zeros((128, 16, 8), np.int32)
        for p in range(128):
            ptb[p] = pt.reshape(16, 8, 8)[:, :, p // 16]
        m["ptb"] = ptb
    m["cache_mem_k"] = f(a["cache_mem_k"]).reshape(4, 16, NMEM, 256)
    m["cache_mem_v"] = f(a["cache_mem_v"]).reshape(4, 16, NMEM, 256)
    m["state_wkv"] = f(a["state_wkv"])
    m["state_shift"] = f(a["state_shift"])
    m["state_conv"] = f(a["state_conv"])
    m["mem_prompt"] = f(a["mem_prompt"]).reshape(NMEM, D)
    for nm in ["norm_mix", "norm_ffn", "norm_mem", "w_mem_kv", "w_o", "w_in_a", "mu_a", "w_vres_in", "mu_vres",
               "w_decay_up", "w0", "w_a_up", "a0", "w_g_up", "w_vres_up", "v0", "k_k", "k_a", "lnx_w", "lnx_b",
               "norm_kv", "norm_ckv", "w_in_b", "norm_q", "w_ffn_up", "conv_w", "conv_b", "w_ffn_down", "final_norm"]:
        m[nm] = f(a[nm])
    m["r_k"] = f(a["r_k"]).reshape(2, 768)
    wkvb = f(a["w_kv_b"])
    m["wukT"] = np.ascontiguousarray(wkvb[:, :, :128].transpose(2, 1, 0))
    m["wuv"] = np.ascontiguousarray(wkvb[:, :, 128:])
    wkva = f(a["w_kv_a"])
    sw = np.concatenate([wkva[:, 288:320], wkva[:, 256:288]], 1)
    m["w_kv_a"] = np.ascontiguousarray(np.concatenate([wkva, sw], 1))
    wq = f(a["w_q_b"]).reshape(2, 256, 6, 192)
    wq_sw = np.concatenate([wq[..., 160:192], wq[..., 128:160]], -1)
    m["w_q_b"] = np.ascontiguousarray(np.concatenate([wq.reshape(2, 256, 1152), wq_sw.reshape(2, 256, 384)], -1))
    m["c_ident_b"] = _bf(np.eye(128, dtype=np.float32))
    m["c_ident_f"] = np.eye(128, dtype=np.float32)
    idx = np.arange(128)
    masks = np.zeros((128, 6, 128), np.float32)
    tri = np.zeros((128, 2, 3, 128), np.float32)
    for gi, blk in enumerate([64, 8]):
        same = (idx[:, None] // blk) == (idx[None, :] // blk)
        lt = idx[:, None] < idx[None, :]
        le = idx[:, None] <= idx[None, :]
        gt = idx[:, None] > idx[None, :]
        masks[:, gi * 3 + 0, :] = same & lt
        masks[:, gi * 3 + 1, :] = same & le
        masks[:, gi * 3 + 2, :] = same & gt
        tri[:, gi, 0, :] = same & le
        tri[:, gi, 1, :] = same & lt
        tri[:, gi, 2, :] = same & gt
    m["c_masks"] = masks
    m["c_tri"] = tri
    m["c_bd2"] = ((idx[:, None] // 64) == (idx[None, :] // 64)).astype(np.float32)
    oh = np.zeros((128, 2, 128), np.float32)
    oh[:, 0, 0:64] = 1.0
    oh[:, 1, 64:128] = 1.0
    m["c_ones_half"] = _bf(oh)
    m["c_caus"] = (idx[:, None] <= idx[None, :]).astype(np.float32)
    m["c_sub16"] = (idx % 16).astype(np.float32).reshape(128, 1)
    pos = np.zeros((NT, 128), np.float32)
    for i in range(NPT):
        pos[i] = i * 128 + idx
    pos[NPT] = 8192 + (idx % 8)
    inv = (np.float32(10000.0) ** (-np.arange(32, dtype=np.float32) / np.float32(32))).astype(np.float32)
    ang = (pos[:, :, None] * inv[None, None, :]).astype(np.float32).astype(np.float64)
    cs, sn = np.cos(ang), np.sin(ang)
    C = np.concatenate([cs, cs], -1)
    Sg = np.concatenate([-sn, sn], -1)
    rtok = np.stack([C, Sg], 2).astype(np.float32)
    m["c_rope_tok"] = rtok
    m["c_rope_T"] = _bf(rtok.transpose(3, 0, 2, 1))
    m["c_seqmask"] = np.concatenate([((idx[:, None] // 8) == np.arange(16)[None, :]), ((idx[:, None] // 64) == np.arange(2)[None, :])], 1).astype(np.float32)
    return m


def kernel(**inputs):
    n_cores = 8
    n_phys = int(np.asarray(inputs["cache_ckv"]).shape[0])
    nc, g = build_program(n_phys, mode="full")
    in_maps = []
    for c in range(n_cores):
        sl = slice(16 * c, 16 * c + 16)
        a = dict(inputs)
        a["x_prompt"] = np.asarray(inputs["x_prompt"])[c:c + 1]
        a["mem_prompt"] = np.asarray(inputs["mem_prompt"])[c:c + 1]
        a["x_sample"] = np.asarray(inputs["x_sample"])[sl]
        a["cache_mem_k"] = np.asarray(inputs["cache_mem_k"])[:, sl]
        a["cache_mem_v"] = np.asarray(inputs["cache_mem_v"])[:, sl]
        a["state_wkv"] = np.asarray(inputs["state_wkv"])[:, sl]
        a["state_shift"] = np.asarray(inputs["state_shift"])[:, sl]
        a["state_conv"] = np.asarray(inputs["state_conv"])[:, sl]
        a["page_table"] = np.asarray(inputs["page_table"])[sl]
        in_maps.append(core_inputs(a, c, n_phys))
    res = run_bass_kernel_spmd(nc, in_maps, core_ids=list(range(n_cores)))
    R = res.results
    cat = lambda key, f: np.stack([f(r["o_" + key]) for r in R], 0)
    y_p = cat("y", lambda v: v[:NPT].reshape(2048, D))
    y_s = np.concatenate([r["o_y"][NPT].reshape(16, 8, D) for r in R], 0)
    ckv_p = cat("ckv", lambda v: v[:NPT].reshape(2048, 256))
    kpe_p = cat("kpe", lambda v: v[:NPT].reshape(2048, 64))
    mem_k_p = np.stack([r["o_memk"].reshape(4, NMEM, 4, 64) for r in R], 1)
    mem_v_p = np.stack([r["o_memv"].reshape(4, NMEM, 4, 64) for r in R], 1)
    wkv_p = np.stack([r["o_wkv_p"] for r in R], 1)
    shift_p = np.stack([r["o_shift_p"] for r in R], 1)
    conv_p = np.stack([r["o_conv_p"] for r in R], 1)
    ckv_s = np.concatenate([r["o_ckv"][NPT].reshape(16, 8, 256) for r in R], 0)
    kpe_s = np.concatenate([r["o_kpe"][NPT].reshape(16, 8, 64) for r in R], 0)
    wkv_s = np.concatenate([r["o_wkv_s"] for r in R], 1)
    shift_s = np.concatenate([r["o_shift_s"] for r in R], 1)
    conv_s = np.concatenate([r["o_conv_s"] for r in R], 1)
    outs = (y_p, y_s, ckv_p, kpe_p, mem_k_p, mem_v_p, wkv_p, shift_p, conv_p, ckv_s, kpe_s, wkv_s, shift_s, conv_s)
    return tuple(np.ascontiguousarray(o, dtype=np.float32) for o in outs)
```

```python
import contextlib
import numpy as np
import ml_dtypes
import concourse.bass as bass
import concourse.mybir as mybir
from concourse.bass_utils import run_bass_kernel_spmd

F32 = mybir.dt.float32
BF16 = mybir.dt.bfloat16
I32 = mybir.dt.int32
AF = mybir.ActivationFunctionType
ALU = mybir.AluOpType
AX = mybir.AxisListType

D = 1024
NT = 17
NPT = 16
DFF = 2816
NMEM = 256
H_A = 12
C_RWKV = 2592
C_A = 2848
ATTN_SCALE = 192 ** -0.5
MEM_SCALE = 0.125
RMS_EPS = 1e-6
GN_EPS = 64e-5
DECAY_C = float(np.exp(-0.5))


class Buf:
    __slots__ = ("name", "w", "r", "excl")

    def __init__(self, name, excl=False):
        self.name = name
        self.w = None
        self.r = []
        self.excl = excl


class V:
    __slots__ = ("ap", "buf")

    def __init__(self, ap, buf):
        self.ap = ap
        self.buf = buf

    def __getitem__(self, k):
        return V(self.ap[k], self.buf)

    def re(self, s, **kw):
        return V(self.ap.rearrange(s, **kw), self.buf)

    def bc(self, shape):
        return V(self.ap.broadcast_to(shape), self.buf)


class Op:
    __slots__ = ("eng", "fn", "deps", "signal", "cnt", "dma", "sem", "semval", "idx")


ENGS = ["pe", "act", "dve", "pool", "sp"]
NDMA_SEMS = 24
NSEM_Q = {"sp": 24, "pool": 4, "act": 4}


class KB:
    def __init__(self, nc):
        self.nc = nc
        self.ops = []
        self.last = {e: None for e in ENGS}
        self.dmas_since = []

    def op(self, eng, fn, reads=(), writes=(), dma=False, extra_deps=(), pseudo=False):
        o = Op()
        o.eng = eng
        o.fn = fn
        o.dma = dma
        o.signal = False
        o.cnt = None
        o.sem = None
        o.semval = None
        o.idx = len(self.ops)
        deps = set()
        rb = []
        wb = []
        for v in reads:
            b = v.buf if isinstance(v, V) else v
            if b is not None and b not in rb:
                rb.append(b)
        for v in writes:
            b = v.buf if isinstance(v, V) else v
            if b is not None and b not in wb:
                wb.append(b)
        for b in rb:
            if b.w is not None:
                deps.add((b.w, "raw"))
            if b.excl:
                for r in b.r:
                    if self.ops[r].eng != eng:
                        deps.add((r, "raw"))
        for b in wb:
            if b.w is not None:
                deps.add((b.w, "waw"))
            for r in b.r:
                deps.add((r, "war"))
        for j in extra_deps:
            deps.add((j, "raw"))
        final = {}
        for (j, kind) in deps:
            if j == o.idx:
                continue
            pj = self.ops[j]
            need = True
            if pj.eng == eng and not pj.dma and not dma:
                if eng == "pe":
                    need = False
                elif kind != "raw":
                    need = False
            if need:
                final[j] = True
        o.deps = sorted(final.keys())
        for j in o.deps:
            self.ops[j].signal = True
        for b in rb:
            b.r.append(o.idx)
        for b in wb:
            b.w = o.idx
            b.r = []
        self.ops.append(o)
        if not pseudo:
            self.last[eng] = o.idx
        if dma:
            self.dmas_since.append(o.idx)
        return o

    def barrier(self):
        lasts = [j for j in self.last.values() if j is not None]
        dm = list(self.dmas_since)
        self.dmas_since = []
        for e in ENGS:
            self.op(e, lambda eng: None, extra_deps=lasts + dm, pseudo=True)

    def finalize(self):
        nc = self.nc
        ops = self.ops
        per = {e: [o for o in ops if o.eng == e] for e in ENGS}
        for e in ENGS:
            c = 0
            for o in per[e]:
                if o.dma:
                    continue
                if o.signal:
                    c += 1
                    o.cnt = c
        with contextlib.ExitStack() as st:
            esem = {e: st.enter_context(nc.semaphore("s_" + e)) for e in ENGS}
            dsem = {e: [st.enter_context(nc.semaphore("d_%s_%d" % (e, i))) for i in range(NSEM_Q[e])]
                    for e in ("sp", "pool", "act")}
            dstate = {e: {"i": 0, "tot": [0] * NSEM_Q[e]} for e in dsem}
            for o in ops:
                if o.dma:
                    s = dstate[o.eng]
                    k = s["i"] % NSEM_Q[o.eng]
                    s["i"] += 1
                    o.sem = (o.eng, k)
                    o.semval = (s["tot"][k], s["tot"][k] + 16)
                    s["tot"][k] += 16
            block = st.enter_context(nc.Block())

            def mk(e):
                def body(engobj):
                    seen = {}
                    for o in per[e]:
                        waits = {}
                        if o.dma:
                            key = ("d",) + o.sem
                            prev = o.semval[0]
                            if prev > 0:
                                waits[key] = (dsem[o.sem[0]][o.sem[1]], prev)
                        for j in o.deps:
                            pj = ops[j]
                            if pj.dma:
                                key = ("d",) + pj.sem
                                val = pj.semval[1]
                                semh = dsem[pj.sem[0]][pj.sem[1]]
                            else:
                                key = ("e", pj.eng)
                                val = pj.cnt
                                semh = esem[pj.eng]
                            if key not in waits or waits[key][1] < val:
                                waits[key] = (semh, val)
                        for key, (semh, val) in waits.items():
                            if seen.get(key, 0) >= val:
                                continue
                            engobj.wait_ge(semh, val)
                            seen[key] = val
                        ins = o.fn(engobj)
                        if ins is None:
                            continue
                        if o.dma:
                            ins.then_inc(dsem[o.sem[0]][o.sem[1]], 16)
                        elif o.signal:
                            ins.then_inc(esem[e], 1)
                    if e == "sp":
                        for qe in dsem:
                            for k in range(NSEM_Q[qe]):
                                tot = dstate[qe]["tot"][k]
                                if tot > 0:
                                    engobj.wait_ge(dsem[qe][k], tot)
                return body

            block.tensor(mk("pe"))
            block.scalar(mk("act"))
            block.vector(mk("dve"))
            block.gpsimd(mk("pool"))
            block.sync(mk("sp"))


class G:
    pass


def _aps(*vs):
    return [v for v in vs if isinstance(v, V)]


def mm(g, out, lhsT, rhs, start=True, stop=True):
    g.kb.op("pe", lambda e: e.matmul(out.ap, lhsT=lhsT.ap, rhs=rhs.ap, start=start, stop=stop),
            reads=[lhsT, rhs], writes=[out])


def tr(g, out, in_, ident):
    g.kb.op("pe", lambda e: e.transpose(out=out.ap, in_=in_.ap, identity=ident.ap),
            reads=[in_, ident], writes=[out])


def act(g, out, in_, func, scale=1.0, bias=0.0, accum=None):
    sc = scale.ap if isinstance(scale, V) else scale
    bi = bias.ap if isinstance(bias, V) else bias
    kw = {}
    if accum is not None:
        kw["accum_out"] = accum.ap
    g.kb.op("act", lambda e: e.activation(out=out.ap, in_=in_.ap, func=func, bias=bi, scale=sc, **kw),
            reads=_aps(in_, scale, bias), writes=_aps(out, accum))


def ts(g, eng, out, in0, s1, op0, s2=None, op1=None):
    a1 = s1.ap if isinstance(s1, V) else s1
    a2 = s2.ap if isinstance(s2, V) else s2
    if op1 is None:
        g.kb.op(eng, lambda e: e.tensor_scalar(out=out.ap, in0=in0.ap, scalar1=a1, scalar2=None, op0=op0),
                reads=_aps(in0, s1), writes=[out])
    else:
        g.kb.op(eng, lambda e: e.tensor_scalar(out=out.ap, in0=in0.ap, scalar1=a1, scalar2=a2, op0=op0, op1=op1),
                reads=_aps(in0, s1, s2), writes=[out])


def tt(g, eng, out, in0, in1, op):
    g.kb.op(eng, lambda e: e.tensor_tensor(out=out.ap, in0=in0.ap, in1=in1.ap, op=op),
            reads=[in0, in1], writes=[out])


def stt(g, out, in0, scalar, in1, op0, op1):
    sc = scalar.ap if isinstance(scalar, V) else scalar
    g.kb.op("dve", lambda e: e.scalar_tensor_tensor(out=out.ap, in0=in0.ap, scalar=sc, in1=in1.ap, op0=op0, op1=op1),
            reads=_aps(in0, scalar, in1), writes=[out])


def cp(g, eng, out, in_):
    if eng == "act":
        g.kb.op("act", lambda e: e.activation(out=out.ap, in_=in_.ap, func=AF.Copy), reads=[in_], writes=[out])
    else:
        g.kb.op(eng, lambda e: e.tensor_copy(out=out.ap, in_=in_.ap), reads=[in_], writes=[out])


def red(g, out, in_, op=ALU.add):
    g.kb.op("dve", lambda e: e.tensor_reduce(out=out.ap, in_=in_.ap, axis=AX.X, op=op), reads=[in_], writes=[out])


def recip(g, out, in_):
    g.kb.op("dve", lambda e: e.reciprocal(out=out.ap, in_=in_.ap), reads=[in_], writes=[out])


def memset(g, eng, out, val):
    g.kb.op(eng, lambda e: e.memset(out.ap, val), writes=[out])


def dma(g, out, in_, eng="sp", slow=False):
    kw = {"allow_slow_non_contiguous": True} if slow else {}
    g.kb.op(eng, lambda e: e.dma_start(out=out.ap, in_=in_.ap, **kw), reads=[in_], writes=[out], dma=True)


def dv(ap):
    return V(ap, None)


class Scope:
    def __init__(self, g, tag):
        self.g = g
        self.tag = tag
        self.st = contextlib.ExitStack()
        self.n = 0

    def __enter__(self):
        self.st.__enter__()
        return self

    def __exit__(self, *a):
        return self.st.__exit__(*a)

    def sb(self, name, shape, dt):
        self.n += 1
        h = self.st.enter_context(self.g.nc.sbuf_tensor("%s_%s_%d" % (self.tag, name, self.n), list(shape), dt))
        return V(h[:], Buf(name))


def psum_hold(g, k, shape, dt=F32):
    base = g.ps_f[6 + k] if dt == F32 else g.ps_b[6 + k]
    n = int(np.prod(shape[1:]))
    v = V(base.ap[0:shape[0], 0:n], base.buf)
    if len(shape) == 3:
        v = v.re("p (a b) -> p a b", a=shape[1])
    elif len(shape) == 4:
        v = v.re("p (a b c) -> p a b c", a=shape[1], b=shape[2])
    return v


def psum(g, shape, dt=F32):
    k = g.ps_i % 6
    g.ps_i += 1
    base = g.ps_f[k] if dt == F32 else g.ps_b[k]
    n = int(np.prod(shape[1:]))
    v = V(base.ap[0:shape[0], 0:n], base.buf)
    if len(shape) == 3:
        v = v.re("p (a b) -> p a b", a=shape[1])
    elif len(shape) == 4:
        v = v.re("p (a b c) -> p a b c", a=shape[1], b=shape[2])
    return v


def rmsnorm_T(g, S, x_tile, gT, out_T, tag):
    sq = S.tmp_sq
    ss = S.tmp_ss
    xn = S.tmp_xn
    act(g, sq, x_tile, AF.Square, accum=ss)
    ts(g, "dve", ss, ss, 1.0 / D, ALU.mult, RMS_EPS, ALU.add)
    act(g, ss, ss, AF.Sqrt)
    recip(g, ss, ss)
    ts(g, "dve", xn, x_tile, ss, ALU.mult)
    for half in range(2):
        pt = psum(g, [128, 4, 128], BF16)
        for c in range(4):
            tr(g, pt[:, c, :], xn[:, (half * 4 + c) * 128:(half * 4 + c + 1) * 128], g.ident_b)
        for c in range(4):
            cc = half * 4 + c
            ts(g, "dve", out_T[:, cc, :], pt[:, c, :], gT[:, cc:cc + 1], ALU.mult)
    return xn, ss


def load_vec_T(g, S, name, src_ap, n):
    k = n // 128
    t = S.sb(name, [128, k], F32)
    dma(g, t, dv(src_ap.rearrange("(c p) -> p c", p=128)), slow=True)
    return t


def load_w_into(g, S, t, src_ap, kdim, ncols, col_off=0, cast_eng="pool"):
    kc = (kdim + 127) // 128
    if not hasattr(S, "stg"):
        S.stg = [S.sb("stg%d" % i, [128, 1440], F32) for i in range(getattr(S, "nstg", 2))]
        S.stg_i = 0
    SW = S.stg[0].ap.shape[1]
    for c in range(kc):
        r = min(128, kdim - c * 128)
        for n0 in range(0, ncols, SW):
            n = min(SW, ncols - n0)
            stg = S.stg[S.stg_i % len(S.stg)]
            S.stg_i += 1
            dma(g, stg[0:r, 0:n], dv(src_ap[c * 128:c * 128 + r, n0:n0 + n]))
            ce = ("pool", "act", "dve")[S.stg_i % 3] if n >= 512 else cast_eng
            cp(g, ce, t[0:r, c, col_off + n0:col_off + n0 + n], stg[0:r, 0:n])


def load_w_bf16(g, S, name, src_ap, kdim, ncols, cast_eng="pool"):
    kc = (kdim + 127) // 128
    t = S.sb(name, [128, kc, ncols], BF16)
    load_w_into(g, S, t, src_ap, kdim, ncols, 0, cast_eng)
    return t


def load_bcast(g, S, name, src_ap, n):
    t = S.sb(name, [128, n], F32)
    dma(g, t, dv(src_ap.partition_broadcast(128)))
    return t


def ffn_phase(g, l, half):
    I = g.I
    nch = 11
    c0 = half * nch
    with Scope(g, "f%d%d" % (l, half)) as S:
        wup_g = load_w_bf16(g, S, "wupg", I["w_ffn_up"][l][:, c0 * 128:(c0 + nch) * 128], D, nch * 128)
        wup_v = load_w_bf16(g, S, "wupv", I["w_ffn_up"][l][:, DFF + c0 * 128:DFF + (c0 + nch) * 128], D, nch * 128)
        wdn = load_w_bf16(g, S, "wdn", I["w_ffn_down"][l][c0 * 128:(c0 + nch) * 128, :], nch * 128, D)
        cw = S.sb("cw", [128, 3, nch], F32)
        for j in range(3):
            dma(g, cw[:, j, :], dv(I["conv_w"][l][j, c0 * 128:(c0 + nch) * 128].rearrange("(c p) -> p c", p=128)), slow=True)
        cb = S.sb("cb", [128, nch], F32)
        dma(g, cb, dv(I["conv_b"][l][c0 * 128:(c0 + nch) * 128].rearrange("(c p) -> p c", p=128)), slow=True)
        gT = load_vec_T(g, S, "gffn", I["norm_ffn"][l], D)
        S.tmp_sq = S.sb("sq", [128, D], BF16)
        S.tmp_ss = S.sb("ss", [128, 1], F32)
        S.tmp_xn = S.sb("xn", [128, D], BF16)
        nset = 2 if l < 2 else 1
        xnT2 = [S.sb("xnT%d" % k, [128, 8, 512], BF16) for k in range(nset)] * (3 - nset)
        gext = S.sb("gext", [128, nch, 514], F32)
        conv = S.sb("conv", [128, nch, 512], F32)
        hT = S.sb("hT", [128, nch, 512], BF16)
        cst = S.sb("cst", [32, nch, 128], F32)
        cstT = S.sb("cstT", [128, nch, 32], F32)
        xb2 = [[S.sb("xb%d_%d" % (s_, k), [128, D], F32) for k in range(4)] for s_ in range(nset)] * (3 - nset)
        memset(g, "dve", gext[:, :, 0:2], 0.0)
        blocks = [(0, 4), (4, 4), (8, 4), (12, 4), (NPT, 1)]

        def pro_f(bi):
            t0_, nt_ = blocks[bi]
            xb_, xnT_ = xb2[bi % 2], xnT2[bi % 2]
            for q in range(nt_):
                i = t0_ + q
                sl = slice(q * 128, (q + 1) * 128)
                dma(g, xb_[q], g.x_src[i])
                if half == 0:
                    rmsnorm_T(g, S, xb_[q], gT, xnT_[:, :, sl], "f")
                    dma(g, g.xn2_scr[i], xnT_[:, :, sl])
                else:
                    dma(g, xnT_[:, :, sl], g.xn2_scr[i])

        if nset == 2:
            pro_f(0)
        for bi, (t0, nt) in enumerate(blocks):
            if nset == 1:
                pro_f(bi)
            samp = (t0 == NPT)
            W = nt * 128
            xb, xnT = xb2[bi % 2], xnT2[bi % 2]
            gev = gext[:, :, 0:160].re("p c (j t) -> p c j t", t=10)
            if samp:
                dma(g, cst.re("p c f -> p (c f)"), dv(I["state_conv"][l].rearrange("b j f -> (b j) f")[:, c0 * 128:(c0 + nch) * 128]))
                for c in range(nch):
                    pt = psum(g, [128, 32], F32)
                    tr(g, pt, cst[:, c, :], g.ident_f[0:32, 0:32])
                    cp(g, "act", cstT[:, c, :], pt)
                cp(g, "dve", gev[:, :, :, 0:2], cstT.re("p c (j t) -> p c j t", t=2))
            for c in range(nch):
                pg = psum(g, [128, 512])
                for k in range(8):
                    mm(g, pg[:, 0:W], wup_g[:, k, c * 128:(c + 1) * 128], xnT[:, k, 0:W], start=(k == 0), stop=(k == 7))
                if samp:
                    gcur = gev[:, c, :, 2:10]
                    src = pg[:, 0:128].re("p (j t) -> p j t", t=8)
                    cv = conv[:, c, 0:128].re("p (j t) -> p j t", t=8)
                else:
                    gcur = gext[:, c, 2:2 + W]
                    src = pg[:, 0:W]
                    cv = conv[:, c, 0:W]
                cp(g, "act", gcur, src)
                act(g, cv, src, AF.Identity, scale=cw[:, 2, c:c + 1], bias=cb[:, c:c + 1])
            for c in range(nch):
                if samp:
                    g1 = gev[:, c, :, 1:9]
                    g0 = gev[:, c, :, 0:8]
                    cv = conv[:, c, 0:128].re("p (j t) -> p j t", t=8)
                else:
                    g1 = gext[:, c, 1:1 + W]
                    g0 = gext[:, c, 0:W]
                    cv = conv[:, c, 0:W]
                stt(g, cv, g1, cw[:, 1, c:c + 1], cv, ALU.mult, ALU.add)
                stt(g, cv, g0, cw[:, 0, c:c + 1], cv, ALU.mult, ALU.add)
            if samp or t0 == NPT - 4:
                nrow = 32 if samp else 2
                for c in range(nch):
                    pt = psum(g, [32, 128], F32)
                    if samp:
                        cp(g, "pool", cstT[:, c, :].re("p (j t) -> p j t", t=2), gev[:, c, :, 8:10])
                    else:
                        cp(g, "pool", cstT[:, c, 0:2], gext[:, c, W:W + 2])
                    tr(g, pt[0:nrow, :], cstT[:, c, 0:nrow], g.ident_f)
                    cp(g, "act", cst[0:nrow, c, :], pt[0:nrow, :])
                if samp:
                    dma(g, dv(g.O["conv_s"][l].rearrange("b j f -> (b j) f")[:, c0 * 128:(c0 + nch) * 128]), cst.re("p c f -> p (c f)"))
                else:
                    dma(g, dv(g.O["conv_p"][l][:, c0 * 128:(c0 + nch) * 128]), cst[0:2].re("p c f -> p (c f)"))
            if not samp and t0 < NPT - 4:
                cp(g, "pool", gext[:, :, 0:2], gext[:, :, W:W + 2])
            act(g, conv[:, :, 0:W], conv[:, :, 0:W], AF.Silu)
            for c in range(nch):
                pv = psum(g, [128, 512])
                for k in range(8):
                    mm(g, pv[:, 0:W], wup_v[:, k, c * 128:(c + 1) * 128], xnT[:, k, 0:W], start=(k == 0), stop=(k == 7))
                tt(g, "dve", hT[:, c, 0:W], conv[:, c, 0:W], pv[:, 0:W], ALU.mult)
            if nset == 2 and bi + 1 < len(blocks):
                pro_f(bi + 1)
            for q in range(nt):
                xt = xb[q]
                for nb in range(2):
                    po = psum(g, [128, 512], F32)
                    for c in range(nch):
                        mm(g, po, hT[:, c, q * 128:(q + 1) * 128], wdn[:, c, nb * 512:(nb + 1) * 512], start=(c == 0), stop=(c == nch - 1))
                    tt(g, "dve", xt[:, nb * 512:(nb + 1) * 512], xt[:, nb * 512:(nb + 1) * 512], po, ALU.add)
                x_store(g, t0 + q, xt)
    g.kb.barrier()


def x_load(g, S, i):
    if not hasattr(S, "xbufs"):
        S.xbufs = [S.sb("xb%d" % k, [128, D], F32) for k in range(getattr(S, "nxb", 2))]
        S.xb_i = 0
    t = S.xbufs[S.xb_i % len(S.xbufs)]
    S.xb_i += 1
    dma(g, t, g.x_src[i])
    return t


def x_store(g, i, t):
    dma(g, g.x_scr[i], t)
    g.x_src[i] = g.x_scr[i]


def final_phase(g):
    I = g.I
    with Scope(g, "fin") as S:
        gb = S.sb("gfin", [128, D], F32)
        dma(g, gb, dv(I["final_norm"].partition_broadcast(128)))
        sq = S.sb("sq", [128, D], F32)
        ss = S.sb("ss", [128, 1], F32)
        yo = [S.sb("yo%d" % i, [128, D], F32) for i in range(2)]
        for i in range(NT):
            xt = x_load(g, S, i)
            y = yo[i % 2]
            act(g, sq, xt, AF.Square, accum=ss)
            ts(g, "dve", ss, ss, 1.0 / D, ALU.mult, RMS_EPS, ALU.add)
            act(g, ss, ss, AF.Sqrt)
            recip(g, ss, ss)
            stt(g, y, xt, ss, gb, ALU.mult, ALU.mult)
            dma(g, dv(g.O["y"][i]), y)
    g.kb.barrier()


def mem_prep(g, S, l):
    I, O = g.I, g.O
    kTm = S.sb("kTm", [128, 2, 2, 256], BF16)
    memset(g, "pool", kTm, 0.0)
    Vpm = S.sb("Vpm", [128, 2, 4, 128], BF16)
    memset(g, "pool", Vpm, 0.0)
    with Scope(g, "mp%d" % l) as T:
        T.nstg = 4
        wm = load_w_bf16(g, T, "wmem", I["w_mem_kv"][l], D, 512)
        gT = load_vec_T(g, T, "gmem", I["norm_mem"][l], D)
        T.tmp_sq = T.sb("sq", [128, D], F32)
        T.tmp_ss = T.sb("ss", [128, 1], F32)
        T.tmp_xn = T.sb("xn", [128, D], BF16)
        mT = T.sb("mT", [128, 8, 256], BF16)
        mi = [T.sb("min%d" % k, [128, D], F32) for k in range(2)]
        kv = [T.sb("kv%d" % k, [128, 512], F32) for k in range(2)]
        for mt in range(2):
            dma(g, mi[mt], dv(I["mem_prompt"][mt * 128:(mt + 1) * 128, :]))
            rmsnorm_T(g, T, mi[mt], gT, mT[:, :, mt * 128:(mt + 1) * 128], "m")
        for mt in range(2):
            po = psum(g, [128, 512])
            for k in range(8):
                mm(g, po, mT[:, k, mt * 128:(mt + 1) * 128], wm[:, k, :], start=(k == 0), stop=(k == 7))
            cp(g, "act", kv[mt], po)
            dma(g, dv(O["memk"][l][mt * 128:(mt + 1) * 128, :]), kv[mt][:, 0:256])
            dma(g, dv(O["memv"][l][mt * 128:(mt + 1) * 128, :]), kv[mt][:, 256:512])
            for h in range(4):
                hb = (h % 2) * 64
                cp(g, "dve", Vpm[:, mt, h, hb:hb + 64], kv[mt][:, 256 + h * 64:256 + (h + 1) * 64])
        for pair in range(2):
            pk = psum(g, [128, 256])
            for k in range(8):
                mm(g, pk, wm[:, k, pair * 128:(pair + 1) * 128], mT[:, k, :], start=(k == 0), stop=(k == 7))
            for hh in range(2):
                cp(g, "act", kTm[hh * 64:hh * 64 + 64, pair, hh, :], pk[hh * 64:hh * 64 + 64, :])
    g.kb.barrier()
    return kTm, Vpm


def mem_attn_alloc(g, S):
    S.mem_PT = S.sb("memPT", [128, 4, 128], BF16)
    S.mem_rs = S.sb("memrs", [128, 128], F32)
    S.smk = [S.sb("smk%d" % k, [128, 2, 256], F32) for k in range(1)] * 2
    S.smv = [S.sb("smv%d" % k, [128, 2, 256], F32) for k in range(1)] * 2
    S.smkb = [S.sb("smkb%d" % k, [128, 2, 256], BF16) for k in range(1)] * 2
    S.kTs = [S.sb("kTs%d" % k, [128, 2, 2, 256], BF16) for k in range(1)] * 2
    memset(g, "pool", S.kTs[0], 0.0)
    S.Vps = [S.sb("Vps%d" % k, [128, 2, 4, 128], BF16) for k in range(1)] * 2
    S.sPT = [S.sb("sPT%d" % k, [128, 2, 4, 8], BF16) for k in range(2)]
    memset(g, "pool", S.Vps[0], 0.0)


def mem_attn_prompt(g, S, qmT, kTm, Vpm, omT):
    for pair in range(2):
        ps_s = psum(g, [128, 4, 128])
        for hh in range(2):
            pb = hh * 64
            for mt in range(2):
                mm(g, ps_s[:, hh * 2 + mt, :], kTm[:, pair, hh, mt * 128:(mt + 1) * 128], qmT[:, pair, :])
        PT = S.mem_PT
        act(g, PT, ps_s, AF.Exp, scale=MEM_SCALE)
        po = psum(g, [128, 2, 128])
        n = 0
        for hh in range(2):
            h = pair * 2 + hh
            for mt in range(2):
                mm(g, po[:, 0, :], Vpm[:, mt, h, :], PT[:, hh * 2 + mt, :], start=(n == 0), stop=(n == 3))
                n += 1
        n = 0
        for hh in range(2):
            for mt in range(2):
                mm(g, po[:, 1, :], g.ones_half[:, hh, :], PT[:, hh * 2 + mt, :], start=(n == 0), stop=(n == 3))
                n += 1
        recip(g, S.mem_rs, po[:, 1, :])
        tt(g, "dve", omT[:, pair, :], po[:, 0, :], S.mem_rs, ALU.mult)


def mem_attn_sample(g, S, l, qmT, omT):
    I = g.I
    acc = psum_hold(g, 0, [128, 4, 128])
    for j in range(16):
        kin, vin, kb16, kTs, Vps, sPT = S.smk[j % 2], S.smv[j % 2], S.smkb[j % 2], S.kTs[j % 2], S.Vps[j % 2], S.sPT[j % 2]
        dma(g, kin, dv(I["cache_mem_k"][l, j].rearrange("(mt p) f -> p mt f", p=128)))
        dma(g, vin, dv(I["cache_mem_v"][l, j].rearrange("(mt p) f -> p mt f", p=128)))
        cp(g, "pool", kb16, kin)
        ptk = psum(g, [128, 4, 128], BF16)
        for pair in range(2):
            for mt in range(2):
                tr(g, ptk[:, pair * 2 + mt, :], kb16[:, mt, pair * 128:(pair + 1) * 128], g.ident_b)
        for pair in range(2):
            for hh in range(2):
                cp(g, "act", kTs[hh * 64:hh * 64 + 64, pair, hh, :].re("p (m k) -> p m k", m=2), ptk[hh * 64:hh * 64 + 64, pair * 2:pair * 2 + 2, :])
        vv = vin.re("p m (h d) -> p m h d", d=64)
        for hh in range(2):
            cp(g, "pool", Vps[:, :, hh::2, hh * 64:hh * 64 + 64], vv[:, :, hh::2, :])
        ps_s = psum(g, [128, 2, 4, 8])
        for pair in range(2):
            for hh in range(2):
                pb = hh * 64
                for mt in range(2):
                    mm(g, ps_s[:, pair, hh * 2 + mt, :], kTs[:, pair, hh, mt * 128:(mt + 1) * 128], qmT[:, pair, 8 * j:8 * j + 8])
        act(g, sPT, ps_s, AF.Exp, scale=MEM_SCALE)
        for pair in range(2):
            n = 0
            for hh in range(2):
                h = pair * 2 + hh
                for mt in range(2):
                    mm(g, acc[:, pair * 2, 8 * j:8 * j + 8], Vps[:, mt, h, :], sPT[:, pair, hh * 2 + mt, :], start=(n == 0), stop=(n == 3))
                    n += 1
            n = 0
            for hh in range(2):
                for mt in range(2):
                    mm(g, acc[:, pair * 2 + 1, 8 * j:8 * j + 8], g.ones_half[:, hh, :], sPT[:, pair, hh * 2 + mt, :], start=(n == 0), stop=(n == 3))
                    n += 1
    for pair in range(2):
        recip(g, S.mem_rs, acc[:, pair * 2 + 1, :])
        tt(g, "dve", omT[:, pair, :], acc[:, pair * 2, :], S.mem_rs, ALU.mult)

def a_chunks(l):
    ch = []
    for i in range(6):
        ch.append(("r", i, 128 * i, 128))
    ch.append(("wl", 0, 768, 64))
    for i in range(6):
        ch.append(("k", i, 832 + 128 * i, 128))
    for i in range(6):
        ch.append(("v", i, 1600 + 128 * i, 128))
    ch.append(("al", 0, 2368, 64))
    ch.append(("gl", 0, 2432, 128))
    ch.append(("gl", 1, 2560, 32))
    if l == 1:
        ch.append(("vl", 0, 2848, 32))
    return ch


def mixer_a_phase(g, l):
    I, O = g.I, g.O
    import os
    tiles = [int(t) for t in os.environ["DBG_TILES"].split(",") if t] if "DBG_TILES" in os.environ else list(range(NT))
    LIM = int(os.environ.get("DBG_STAGE", "99"))
    SLIM = int(os.environ.get("DBG_SCAN", "99"))
    with Scope(g, "a%d" % l) as S:
        kTm, Vpm = mem_prep(g, S, l)
        S.nxb = 2
        T1 = S.sb("T1", [128, 6, 128], F32)
        Al = S.sb("Al", [128, 6, 128], F32)
        Aa = S.sb("Aa", [128, 6, 128], F32)
        Kd = S.sb("Kd", [128, 6, 128], F32)
        S.stg = [T1.re("p c k -> p (c k)"), Al.re("p c k -> p (c k)"), Aa.re("p c k -> p (c k)"), Kd.re("p c k -> p (c k)")]
        S.stg_i = 0
        w_in = S.sb("w_in", [128, 8, 2880], BF16)
        load_w_into(g, S, w_in, I["w_in_a"][l], D, C_A, 0)
        if l == 1:
            load_w_into(g, S, w_in, I["w_vres_in"][0], D, 32, C_A)
        w_o = load_w_bf16(g, S, "w_o", I["w_o"][l], D, D)
        wd_up = load_w_bf16(g, S, "wd_up", I["w_decay_up"][l], 64, 768)
        wa_up = load_w_bf16(g, S, "wa_up", I["w_a_up"][l], 64, 768)
        wg_up = load_w_bf16(g, S, "wg_up", I["w_g_up"][l], 160, 768)
        if l == 1:
            wv_up = load_w_bf16(g, S, "wv_up", I["w_vres_up"][0], 32, 768)
            v0T = load_vec_T(g, S, "v0T", I["v0"][0], 768)
        gT = load_vec_T(g, S, "gmix", I["norm_mix"][l], D)
        a0T = load_vec_T(g, S, "a0T", I["a0"][l], 768)
        kkT = load_vec_T(g, S, "kkT", I["k_k"][l], 768)
        kaT = load_vec_T(g, S, "kaT", I["k_a"][l], 768)
        rkT = load_vec_T(g, S, "rkT", I["r_k"][l], 768)
        omka = S.sb("omka", [128, 6], F32)
        ts(g, "dve", omka, kaT, -1.0, ALU.mult, 1.0, ALU.add)
        w0b = load_bcast(g, S, "w0b", I["w0"][l], 768)
        lnw = S.sb("lnw", [128, 768], BF16)
        lnb = S.sb("lnb", [128, 768], BF16)
        for (dst_, nm_) in ((lnw, "lnx_w"), (lnb, "lnx_b")):
            stg_ = S.stg[S.stg_i % len(S.stg)]
            S.stg_i += 1
            dma(g, stg_, dv(I[nm_][l].partition_broadcast(128)))
            cp(g, "pool", dst_, stg_)
        shf = S.sb("shf", [128, 8, 16], F32)
        chunks = a_chunks(l)
        nsh = len(chunks)
        MU = S.sb("MU", [128, nsh], F32)
        memset(g, "dve", MU, 0.0)
        for ci, (nm, idx, col0, w) in enumerate(chunks):
            src = I["mu_vres"][0] if nm == "vl" else I["mu_a"][l][col0:col0 + w]
            dma(g, MU[0:w, ci:ci + 1], dv(src.rearrange("(p o) -> p o", o=1)), slow=True)
        OMU = S.sb("OMU", [128, nsh], F32)
        ts(g, "dve", OMU, MU, -1.0, ALU.mult, 1.0, ALU.add)
        CARRY = S.sb("CARRY", [128, nsh], F32)
        memset(g, "dve", CARRY, 0.0)
        S.tmp_sq = S.sb("sq", [128, D], BF16)
        S.tmp_ss = S.sb("ss", [128, 1], F32)
        S.tmp_xn = S.sb("xn", [128, D], BF16)
        xnT2 = [S.sb("xnT%d" % k, [128, 8, 128], BF16) for k in range(2)]
        E = [S.sb("E%d" % k, [128, 4, 144], F32) for k in range(1)] * 2
        tmpm = [S.sb("tmpm%d" % k, [128, 128], F32) for k in range(1)] * 2
        tmpb = S.sb("tmpb", [128, 128], F32)
        Rm = S.sb("Rm", [128, 6, 128], F32)
        Km = S.sb("Km", [128, 6, 128], F32)
        Vm = S.sb("Vm", [128, 6, 128], F32)
        WL = S.sb("WL", [64, 128], F32)
        AL = S.sb("AL", [64, 128], F32)
        GL = S.sb("GL", [128, 2, 128], F32)
        VL = S.sb("VL", [32, 128], F32)
        WLb = S.sb("WLb", [64, 128], BF16)
        ALb = S.sb("ALb", [64, 128], BF16)
        GLb = S.sb("GLb", [128, 2, 128], BF16)
        VLb = S.sb("VLb", [32, 128], BF16)
        qmT = S.sb("qmT", [128, 2, 128], BF16)
        LW = S.sb("LW", [128, 768], F32)
        Gt = S.sb("Gt", [128, 768], BF16)
        T1b = S.sb("T1b", [128, 6, 128], BF16)
        vfb = T1b
        EX = [S.sb("EX%d" % k, [128, 4, 128], F32) for k in range(1)] * 2
        PEND = S.sb("PEND", [128, 6, 16], F32)
        ART = S.sb("ART", [128, 6, 2, 128], BF16)
        BtT = S.sb("BtT", [128, 6, 128], BF16)
        BtT2 = S.sb("BtT2", [128, 6, 2, 128], BF16)
        KtT2 = S.sb("KtT2", [128, 6, 2, 128], BF16)
        AtT2 = S.sb("AtT2", [128, 6, 2, 128], BF16)
        for t_ in (BtT2, KtT2, AtT2):
            memset(g, "pool", t_, 0.0)
        BhT = S.sb("BhT", [128, 6, 128], BF16)
        KhT = S.sb("KhT", [128, 6, 128], BF16)
        BoT = S.sb("BoT", [128, 6, 128], BF16)
        Atk = S.sb("Atk", [128, 12, 64], BF16)
        Vtk = S.sb("Vtk", [128, 12, 64], BF16)
        Vpad = S.sb("Vpad", [128, 12, 128], BF16)
        Bhk = S.sb("Bhk", [128, 12, 64], BF16)
        Khk = S.sb("Khk", [128, 12, 64], BF16)
        Botk = T1b.re("p c k -> p (c k)")
        memset(g, "pool", Vpad, 0.0)
        CDT = BF16 if os.environ.get("DBG_CHAIN_BF16", "0") == "1" else F32
        PT_ = [S.sb("P%d" % k, [128, 2, 128], CDT) for k in range(2)]
        PTT = [S.sb("PT%d" % k, [128, 2, 128], CDT) for k in range(2)]
        Gm = [S.sb("G%d" % k, [128, 2, 128], CDT) for k in range(2)]
        nDrb = S.sb("nDrb", [128, 2, 128], BF16)
        LkT = S.sb("LkT", [128, 2, 128], BF16)
        DrkT = S.sb("DrkT", [128, 2, 128], BF16)
        AW = S.sb("AW", [128, 2, 128], F32)
        Ab = S.sb("Ab", [128, 2, 64], BF16)
        Apad = S.sb("Apad", [128, 2, 128], BF16)
        nUb = S.sb("nUb", [128, 2, 64], BF16)
        nUpad = S.sb("nUpad", [128, 2, 128], BF16)
        memset(g, "pool", Apad, 0.0)
        memset(g, "pool", nUpad, 0.0)
        RbT = S.sb("RbT", [128, 128], F32)
        YbT = S.sb("YbT", [128, 128], F32)
        YT = S.sb("YT", [128, 6, 128], F32)
        Nst = S.sb("Nst", [128, 6, 128], F32)
        memset(g, "dve", Nst, 0.0)
        PhT = S.sb("PhT", [128, 128], F32)
        Gam = S.sb("Gam", [128, 128], F32)
        SG = 2
        shin = T1.re("p c k -> p (c k)")[0:16, :]
        shT = S.sb("shT", [128, 8, 16], BF16)
        Bbd = S.sb("Bbd", [128, SG, 128], BF16)
        Kbd = S.sb("Kbd", [128, SG, 128], BF16)
        PhT2 = S.sb("PhT2", [128, 2, 128], F32)
        PhTs = S.sb("PhTs", [128, SG, 128], F32)
        GamTs = S.sb("GamTs", [128, SG, 128], F32)
        S0in = S.sb("S0in", [128, SG, 64], F32)
        S0bd = S.sb("S0bd", [128, SG, 128], F32)
        N0bd = S.sb("N0bd", [128, SG, 128], F32)
        RS = S.sb("RS", [128, SG, 128], F32)
        memset(g, "pool", S0bd, 0.0)
        Ytk = LW
        st1 = S.sb("st1", [128, 12], F32)
        st2 = S.sb("st2", [128, 12], F32)
        otk = T1b.re("p c k -> p (c k)")
        oT = S.sb("oT", [128, 8, 128], BF16)
        mem_attn_alloc(g, S)
        print("A-phase SBUF remaining:", g.nc.sbuf_bytes_remaining)

        def pro_a(i_, k_):
            xt_ = x_load(g, S, i_)
            rmsnorm_T(g, S, xt_, gT, xnT2[k_], "a")
            return xt_

        xt_next = pro_a(tiles[0], 0)
        for n_, i in enumerate(tiles):
            samp = (i == NPT)
            nblk = 16 if samp else 2
            blk = 8 if samp else 64
            MS = g.ms_s if samp else g.ms_p
            MI = g.mi_s if samp else g.mi_p
            MST = g.mst_s if samp else g.mst_p
            TRI = g.tri_s if samp else g.tri_p
            xt = xt_next
            xnT = xnT2[n_ % 2]
            if samp:
                cp(g, "dve", shf, V(xnT.ap[:, :, 7:128:8], xnT.buf))
                for c in range(8):
                    dma(g, dv(O["shift_s"][l][:, c * 128:(c + 1) * 128].rearrange("j p -> p j")), shf[:, c, :], slow=True)
            elif i == NPT - 1:
                cp(g, "dve", shf[:, :, 0:1], xnT[:, :, 127:128])
                dma(g, dv(O["shift_p"][l].rearrange("(c p) -> p c", p=128)), shf[:, :, 0], slow=True)
            if samp:
                pts = psum(g, [128, 8, 16], F32)
                dma(g, shin[:, 0:768], dv(I["state_shift"][l][:, 0:768]))
                for k in range(6):
                    tr(g, pts[:, k, :], shin[:, k * 128:(k + 1) * 128], g.ident_f[0:16, 0:16])
                dma(g, shin[:, 0:256], dv(I["state_shift"][l][:, 768:1024]))
                for k in range(2):
                    tr(g, pts[:, 6 + k, :], shin[:, k * 128:(k + 1) * 128], g.ident_f[0:16, 0:16])
                cp(g, "act", shT, pts)
            dest = {"r": Rm, "k": Km, "v": Vm, "gl": GL}
            for gi, g0 in enumerate(range(0, nsh, 4)):
                grp = chunks[g0:g0 + 4]
                n = len(grp)
                Eb = E[gi % 2]
                pg = psum(g, [128, 4, 128])
                for c, (nm, idx, col0, w) in enumerate(grp):
                    for k in range(8):
                        mm(g, pg[0:w, c, :], w_in[:, k, col0:col0 + w], xnT[:, k, :], start=(k == 0), stop=(k == 7))
                if samp:
                    pg2 = psum(g, [128, 4, 16])
                    for c, (nm, idx, col0, w) in enumerate(grp):
                        for k in range(8):
                            mm(g, pg2[0:w, c, :], w_in[:, k, col0:col0 + w], shT[:, k, :], start=(k == 0), stop=(k == 7))
                    Ev = Eb.re("p c (j t) -> p c j t", t=9)
                    for c, (nm, idx, col0, w) in enumerate(grp):
                        cp(g, "act", Ev[0:w, c, :, 1:9], pg[0:w, c, :].re("p (j t) -> p j t", t=8))
                        cp(g, "dve", Ev[0:w, c, :, 0:1], pg2[0:w, c, :].re("p (j o) -> p j o", o=1))
                else:
                    cp(g, "dve", Eb[:, 0:n, 0:1], CARRY[:, g0:g0 + n].re("p (c o) -> p c o", o=1))
                    for c, (nm, idx, col0, w) in enumerate(grp):
                        cp(g, "act", Eb[0:w, c, 1:129], pg[0:w, c, :])
                        cp(g, "dve", CARRY[0:w, g0 + c:g0 + c + 1], Eb[0:w, c, 128:129])
                for c, (nm, idx, col0, w) in enumerate(grp):
                    ci = g0 + c
                    if nm in dest:
                        dst = dest[nm][0:w, idx, :]
                    else:
                        dst = {"wl": WL, "al": AL, "vl": VL}[nm][0:w, :]
                    tm = tmpm[ci % 2]
                    if samp:
                        Ev = Eb.re("p c (j t) -> p c j t", t=9)
                        cur = Ev[0:w, c, :, 1:9]
                        prev = Ev[0:w, c, :, 0:8]
                        dst = dst.re("p (j t) -> p j t", t=8)
                        tmv = tm[0:w, :].re("p (j t) -> p j t", t=8)
                    else:
                        cur = Eb[0:w, c, 1:129]
                        prev = Eb[0:w, c, 0:128]
                        tmv = tm[0:w, :]
                    act(g, tmv, prev, AF.Identity, scale=MU[0:w, ci:ci + 1])
                    stt(g, dst, cur, OMU[0:w, ci:ci + 1], tmv, ALU.mult, ALU.add)
            pg = psum(g, [128, 2, 128])
            for c in range(2):
                for k in range(8):
                    mm(g, pg[:, c, :], w_in[:, k, C_RWKV + c * 128:C_RWKV + (c + 1) * 128], xnT[:, k, :], start=(k == 0), stop=(k == 7))
            cp(g, "act", qmT, pg)
            if LIM < 1:
                x_store(g, i, xt)
                continue
            act(g, WLb, WL, AF.Tanh)
            cp(g, "act", ALb, AL)
            act(g, GLb[:, 0, :], GL[:, 0, :], AF.Sigmoid)
            act(g, GLb[0:32, 1, :], GL[0:32, 1, :], AF.Sigmoid)
            for nb, (c0, cn) in enumerate([(0, 512), (512, 256)]):
                pz = psum(g, [128, cn])
                mm(g, pz, WLb, wd_up[0:64, 0, c0:c0 + cn])
                tt(g, "dve", LW[:, c0:c0 + cn], pz, w0b[:, c0:c0 + cn], ALU.add)
            act(g, LW, LW, AF.Sigmoid)
            ts(g, "dve", LW, LW, -DECAY_C, ALU.mult)
            for nb, (c0, cn) in enumerate([(0, 512), (512, 256)]):
                pz = psum(g, [128, cn])
                mm(g, pz, GLb[:, 0, :], wg_up[:, 0, c0:c0 + cn], start=True, stop=False)
                mm(g, pz, GLb[0:32, 1, :], wg_up[0:32, 1, c0:c0 + cn], start=False, stop=True)
                cp(g, "act", Gt[:, c0:c0 + cn], pz)
            for half in range(2):
                pa = psum(g, [128, 3, 128])
                for c in range(3):
                    cc = half * 3 + c
                    mm(g, pa[:, c, :], wa_up[0:64, 0, cc * 128:(cc + 1) * 128], ALb)
                for c in range(3):
                    cc = half * 3 + c
                    act(g, Aa[:, cc, :], pa[:, c, :], AF.Sigmoid, bias=a0T[:, cc:cc + 1])
            if l == 0:
                cp(g, "act", vfb, Vm)
                dma(g, g.vf_scr[i], vfb)
            else:
                cp(g, "act", VLb, VL)
                dma(g, vfb, g.vf_scr[i])
                for half in range(2):
                    pa = psum(g, [128, 3, 128])
                    for c in range(3):
                        cc = half * 3 + c
                        mm(g, pa[:, c, :], wv_up[0:32, 0, cc * 128:(cc + 1) * 128], VLb)
                    for c in range(3):
                        cc = half * 3 + c
                        act(g, T1[:, cc, :], pa[:, c, :], AF.Sigmoid, bias=v0T[:, cc:cc + 1])
                tt(g, "dve", Al, vfb, Vm, ALU.subtract)
                tt(g, "dve", Al, Al, T1, ALU.mult)
                tt(g, "dve", Vm, Vm, Al, ALU.add)
            if LIM < 2:
                x_store(g, i, xt)
                continue
            for cc in range(6):
                act(g, Al[:, cc, :], Km[:, cc, :], AF.Identity, scale=kkT[:, cc:cc + 1])
                ts(g, "dve", Kd[:, cc, :], Aa[:, cc, :], kaT[:, cc:cc + 1], ALU.mult, omka[:, cc:cc + 1], ALU.add)
            tt(g, "dve", T1b, Al, Al, ALU.mult)
            for nb, (c0, cn) in enumerate([(0, 4), (4, 2)]):
                pss = psum(g, [128, cn, 128])
                for c in range(cn):
                    mm(g, pss[:, c, :], g.bd2_b, T1b[:, c0 + c, :])
                ts(g, "dve", T1[:, c0:c0 + cn, :], pss, 1e-24, ALU.max)
            act(g, T1, T1, AF.Sqrt)
            recip(g, T1, T1)
            tt(g, "dve", Al, Al, T1, ALU.mult)
            tt(g, "dve", Kd, Kd, Km, ALU.mult)
            tt(g, "dve", T1, Rm, Kd, ALU.mult)
            for cc in range(6):
                ts(g, "dve", T1b[:, cc, :], T1[:, cc, :], rkT[:, cc:cc + 1], ALU.mult)
            for nb, (c0, cn) in enumerate([(0, 4), (4, 2)]):
                pss = psum(g, [128, cn, 128])
                for c in range(cn):
                    mm(g, pss[:, c, :], g.bd2_b, T1b[:, c0 + c, :])
                tt(g, "dve", BoT[:, c0:c0 + cn, :], pss, Vm[:, c0:c0 + cn, :], ALU.mult)
            if LIM < 3:
                x_store(g, i, xt)
                continue
            for cc in range(6):
                pc = psum(g, [128, 3, 128])
                mm(g, pc.re("p a t -> p (a t)"), LW[:, cc * 128:(cc + 1) * 128], TRI.re("p a t -> p (a t)"))
                ex = EX[cc % 2]
                act(g, ex[:, 0:2, :], pc[:, 0:2, :], AF.Exp)
                act(g, ex[:, 2, :], pc[:, 0, :], AF.Exp, scale=-1.0)
                act(g, ex[:, 3, :], pc[:, 2, :], AF.Exp)
                tt(g, "dve", ART[:, cc, 0, :], Al[:, cc, :], ex[:, 1, :], ALU.mult)
                tt(g, "dve", ART[:, cc, 1, :], Rm[:, cc, :], ex[:, 0, :], ALU.mult)
                tt(g, "dve", tmpb, Al[:, cc, :], Aa[:, cc, :], ALU.mult)
                tt(g, "dve", BtT[:, cc, :], tmpb, ex[:, 2, :], ALU.mult)
                tt(g, "pool", BhT[:, cc, :], tmpb, ex[:, 3, :], ALU.mult)
                tt(g, "dve", KhT[:, cc, :], Kd[:, cc, :], ex[:, 3, :], ALU.mult)
                for hh in range(2):
                    pb = hh * 64
                    cp(g, "act", AtT2[pb:pb + 64, cc, hh, :], ART[pb:pb + 64, cc, 0, :])
                    cp(g, "pool", BtT2[pb:pb + 64, cc, hh, :], BtT[pb:pb + 64, cc, :])
                    tt(g, "dve", KtT2[pb:pb + 64, cc, hh, :], Kd[pb:pb + 64, cc, :], ex[pb:pb + 64, 2, :], ALU.mult)
                cp(g, "act", PEND[:, cc, 0:nblk], V(ex.ap[:, 0, blk - 1:128:blk], ex.buf))
            if LIM < 4:
                x_store(g, i, xt)
                continue
            cp(g, "act", T1b, Vm)
            for (srcT, sel, dstk) in [(ART, 0, Atk), (T1b, None, Vtk), (BhT, None, Bhk), (KhT, None, Khk)]:
                for half in range(2):
                    ptt = psum(g, [128, 3, 128], BF16)
                    for c in range(3):
                        cc = half * 3 + c
                        src = srcT[:, cc, sel, :] if sel is not None else srcT[:, cc, :]
                        tr(g, ptt[:, c, :], src, g.ident_b)
                    cp(g, "act", dstk[:, half * 6:half * 6 + 6, :].re("p h k -> p (h k)"), ptt.re("p c k -> p (c k)"))
            for hh in range(2):
                cp(g, "pool", Vpad[:, hh::2, hh * 64:hh * 64 + 64], Vtk[:, hh::2, :])
            for half in range(2):
                ptt = psum(g, [128, 3, 128], BF16)
                for c in range(3):
                    tr(g, ptt[:, c, :], BoT[:, half * 3 + c, :], g.ident_b)
                cp(g, "act", Botk[:, half * 384:(half + 1) * 384], ptt.re("p c k -> p (c k)"))
            if LIM < 5:
                x_store(g, i, xt)
                continue
            for cc in range(6):
                h0 = 2 * cc
                if cc == 2 and n_ + 1 < len(tiles):
                    xt_next = pro_a(tiles[n_ + 1], (n_ + 1) % 2)
                pg1 = psum(g, [128, 2, 256])
                pg2 = psum(g, [128, 2, 256])
                ppt = psum(g, [128, 2, 128])
                ar = ART[:, cc, :, :].re("p a t -> p (a t)")
                for hh in range(2):
                    mm(g, pg1[:, hh, :], BtT2[:, cc, hh, :], ar)
                    mm(g, pg2[:, hh, :], KtT2[:, cc, hh, :], ar)
                    mm(g, ppt[:, hh, :], AtT2[:, cc, hh, :], BtT[:, cc, :])
                P, PT = PT_[0], PTT[0]
                msb = MS.re("p (o t) -> p o t", o=1).bc([128, 2, 128])
                mib = MI.re("p (o t) -> p o t", o=1).bc([128, 2, 128])
                mstb = MST.re("p (o t) -> p o t", o=1).bc([128, 2, 128])
                stt(g, P, pg1[:, :, 0:128], -1.0, msb, ALU.mult, ALU.mult)
                stt(g, nDrb, pg1[:, :, 128:256], -1.0, mib, ALU.mult, ALU.mult)
                tt(g, "dve", LkT, pg2[:, :, 0:128], msb, ALU.mult)
                tt(g, "dve", DrkT, pg2[:, :, 128:256], mib, ALU.mult)
                stt(g, PT, ppt, -1.0, mstb, ALU.mult, ALU.mult)
                if SLIM < 1:
                    continue
                G0 = Gm[0]
                tt(g, "dve", G0, P, g.ident_f.re("p (o t) -> p o t", o=1).bc([128, 2, 128]), ALU.add)
                nit = 2 if samp else 5
                nit = min(nit, int(os.environ.get("DBG_NIT", "9")))
                cur = 0
                for it in range(nit):
                    Q, QT, Gc = PT_[cur], PTT[cur], Gm[cur]
                    Qn, QTn, Gn = PT_[1 - cur], PTT[1 - cur], Gm[1 - cur]
                    last = (it == nit - 1)
                    pq2t = psum(g, [128, 2, 128])
                    for hh in range(2):
                        mm(g, pq2t[:, hh, :], Q[:, hh, :], QT[:, hh, :])
                    if os.environ.get("DBG_SQ", "0") != "1":
                        cp(g, "act", QTn, pq2t)
                    if not last:
                        pq2 = psum(g, [128, 2, 128])
                        for hh in range(2):
                            mm(g, pq2[:, hh, :], QT[:, hh, :], Q[:, hh, :])
                        cp(g, "act", Qn, pq2)
                    if os.environ.get("DBG_SKIPG", "0") == "1":
                        continue
                    pgn = psum(g, [128, 2, 128])
                    for hh in range(2):
                        mm(g, pgn[:, hh, :], QTn[:, hh, :], Gc[:, hh, :])
                    tt(g, "dve", Gn, pgn, Gc, ALU.add)
                    cur = 1 - cur
                TTf = Gm[cur]
                if SLIM < 2:
                    continue
                pw = psum(g, [128, 2, 64])
                for hh in range(2):
                    mm(g, pw[:, hh, :], LkT[:, hh, :], Vtk[:, h0 + hh, :])
                cp(g, "act", AW[:, :, 64:128], pw)
                cp(g, "act", AW[:, :, 0:64], Atk[:, h0:h0 + 2, :])
                pau = psum(g, [128, 2, 128])
                for hh in range(2):
                    mm(g, pau[:, hh, :], TTf[:, hh, :], AW[:, hh, :])
                cp(g, "dve", Ab, pau[:, :, 0:64])
                ts(g, "dve", nUb, pau[:, :, 64:128], -1.0, ALU.mult)
                for hh in range(2):
                    cp(g, "dve", Apad[:, hh, hh * 64:hh * 64 + 64], pau[:, hh, 0:64])
                    cp(g, "dve", nUpad[:, hh, hh * 64:hh * 64 + 64], pau[:, hh, 64:128])
                if SLIM < 3:
                    continue
                pr = psum(g, [128, 2, 128])
                for hh in range(2):
                    mm(g, pr[:, 0, :], Apad[:, hh, :], nDrb[:, hh, :], start=(hh == 0), stop=(hh == 1))
                n = 0
                for hh in range(2):
                    mm(g, pr[:, 1, :], Vpad[:, h0 + hh, :], DrkT[:, hh, :], start=(n == 0), stop=False)
                    n += 1
                for hh in range(2):
                    mm(g, pr[:, 1, :], nUpad[:, hh, :], nDrb[:, hh, :], start=False, stop=(hh == 1))
                tt(g, "dve", RbT, pr[:, 0, :], ART[:, cc, 1, :], ALU.add)
                cp(g, "act", YbT, pr[:, 1, :])
                if SLIM < 4:
                    continue
                if not samp:
                    bpair = Bhk[:, h0:h0 + 2, :].re("p h k -> p (h k)").re("p (o k) -> p o k", o=1).bc([128, 2, 128])
                    kpair = Khk[:, h0:h0 + 2, :].re("p h k -> p (h k)").re("p (o k) -> p o k", o=1).bc([128, 2, 128])
                    bmsk = g.blkmask.re("p (j o) -> p j o", o=1).bc([128, 2, 128])
                    tt(g, "dve", Bbd, bpair, bmsk, ALU.mult)
                    tt(g, "pool", Kbd, kpair, bmsk, ALU.mult)
                    pe1 = psum(g, [128, 2, 128])
                    mm(g, pe1.re("p j k -> p (j k)"), Ab.re("p h k -> p (h k)"), Bbd.re("p j k -> p (j k)"))
                    tt(g, "dve", PhT2, pe1, g.bd2_f.re("p (o k) -> p o k", o=1).bc([128, 2, 128]), ALU.mult)
                    for c in range(2):
                        r0 = c * 64
                        stt(g, PhT, g.ident_f, PEND[:, cc, c:c + 1], PhT2[:, c, :], ALU.mult, ALU.subtract)
                        pe2 = psum(g, [128, 128])
                        mm(g, pe2, Kbd[:, c, :], Vtk[:, h0:h0 + 2, :].re("p h k -> p (h k)"), start=True, stop=False)
                        mm(g, pe2, Bbd[:, c, :], nUb.re("p h k -> p (h k)"), start=False, stop=True)
                        tt(g, "dve", Gam, pe2, g.bd2_f, ALU.mult)
                        py = psum(g, [128, 64])
                        mm(g, py, Nst[:, cc, :], RbT[:, r0:r0 + 64])
                        tt(g, "dve", YT[:, cc, r0:r0 + 64], py, YbT[:, r0:r0 + 64], ALU.add)
                        pn = psum(g, [128, 128])
                        mm(g, pn, PhT, Nst[:, cc, :])
                        tt(g, "dve", Nst[:, cc, :], pn, Gam, ALU.add)
                    if i == NPT - 1:
                        pst = psum(g, [128, 128])
                        tr(g, pst, Nst[:, cc, :], g.ident_f)
                        cp(g, "act", PhT, pst)
                        for hh in range(2):
                            dma(g, dv(O["wkv_p"][l][h0 + hh]), PhT[hh * 64:hh * 64 + 64, hh * 64:hh * 64 + 64])
                else:
                    py = psum_hold(g, 1, [128, 128])
                    smb = g.seqmask.re("p (j o) -> p j o", o=1)
                    bpair = Bhk[:, h0:h0 + 2, :].re("p h k -> p (h k)").re("p (o k) -> p o k", o=1).bc([128, SG, 128])
                    kpair = Khk[:, h0:h0 + 2, :].re("p h k -> p (h k)").re("p (o k) -> p o k", o=1).bc([128, SG, 128])
                    bd2b = g.bd2_f.re("p (o k) -> p o k", o=1).bc([128, SG, 128])
                    for sg in range(16 // SG):
                        j0 = sg * SG
                        dma(g, S0in, dv(I["state_wkv"][l][j0:j0 + SG, h0:h0 + 2].rearrange("j h v k -> (h v) j k")))
                        for hh in range(2):
                            cp(g, "pool", S0bd[hh * 64:hh * 64 + 64, :, hh * 64:hh * 64 + 64], S0in[hh * 64:hh * 64 + 64, :, :])
                        pst = psum(g, [128, SG, 128])
                        for jj in range(SG):
                            tr(g, pst[:, jj, :], S0bd[:, jj, :], g.ident_f)
                        cp(g, "act", N0bd, pst)
                        msk = smb[:, j0:j0 + SG, :].bc([128, SG, 128])
                        tt(g, "dve", Bbd, bpair, msk, ALU.mult)
                        tt(g, "dve", Kbd, kpair, msk, ALU.mult)
                        pe1 = psum(g, [128, SG, 128])
                        mm(g, pe1.re("p j k -> p (j k)"), Ab.re("p h k -> p (h k)"), Bbd.re("p j k -> p (j k)"))
                        tt(g, "dve", PhTs, pe1, bd2b, ALU.mult)
                        for jj in range(SG):
                            stt(g, PhTs[:, jj, :], g.ident_f, PEND[:, cc, j0 + jj:j0 + jj + 1], PhTs[:, jj, :], ALU.mult, ALU.subtract)
                        pe2 = psum(g, [128, SG, 128])
                        mm(g, pe2.re("p j k -> p (j k)"), Vtk[:, h0:h0 + 2, :].re("p h k -> p (h k)"), Kbd.re("p j k -> p (j k)"), start=True, stop=False)
                        mm(g, pe2.re("p j k -> p (j k)"), nUb.re("p h k -> p (h k)"), Bbd.re("p j k -> p (j k)"), start=False, stop=True)
                        tt(g, "dve", GamTs, pe2, bd2b, ALU.mult)
                        for jj in range(SG):
                            j = j0 + jj
                            mm(g, py[:, 8 * j:8 * j + 8], N0bd[:, jj, :], RbT[:, 8 * j:8 * j + 8])
                        pn = psum(g, [128, SG, 128])
                        for jj in range(SG):
                            mm(g, pn[:, jj, :], N0bd[:, jj, :], PhTs[:, jj, :])
                        tt(g, "dve", RS, pn, GamTs, ALU.add)
                        for hh in range(2):
                            dma(g, dv(O["wkv_s"][l][j0:j0 + SG, h0 + hh].rearrange("j v k -> v j k")), RS[hh * 64:hh * 64 + 64, :, hh * 64:hh * 64 + 64])
                    tt(g, "dve", YT[:, cc, :], py, YbT, ALU.add)
            if LIM < 6:
                x_store(g, i, xt)
                continue
            for half in range(2):
                pty = psum(g, [128, 3, 128])
                for c in range(3):
                    tr(g, pty[:, c, :], YT[:, half * 3 + c, :], g.ident_f)
                cp(g, "act", Ytk[:, half * 384:(half + 1) * 384], pty.re("p c k -> p (c k)"))
            y3 = Ytk.re("p (h k) -> p h k", k=64)
            red(g, st1, y3)
            ts(g, "dve", st1, st1, 1.0 / 64, ALU.mult)
            tt(g, "dve", y3, y3, st1.re("p (h o) -> p h o", o=1).bc([128, 12, 64]), ALU.subtract)
            t13 = T1.re("p c k -> p (c k)").re("p (h k) -> p h k", k=64)
            tt(g, "dve", t13, y3, y3, ALU.mult)
            red(g, st2, t13)
            ts(g, "dve", st2, st2, 1.0 / 64, ALU.mult, GN_EPS, ALU.add)
            act(g, st2, st2, AF.Sqrt)
            recip(g, st2, st2)
            tt(g, "dve", y3, y3, st2.re("p (h o) -> p h o", o=1).bc([128, 12, 64]), ALU.mult)
            tt(g, "dve", Ytk, Ytk, lnw, ALU.mult)
            tt(g, "dve", Ytk, Ytk, lnb, ALU.add)
            tt(g, "dve", Ytk, Ytk, Botk, ALU.add)
            tt(g, "dve", otk, Ytk, Gt, ALU.mult)
            if g.taps is not None and "otok" in g.taps:
                dma(g, dv(O["dbg_otok"][l, i]), otk_f(g, S, otk))
            for half in range(2):
                pto = psum(g, [128, 3, 128], BF16)
                for c in range(3):
                    tr(g, pto[:, c, :], otk[:, (half * 3 + c) * 128:(half * 3 + c + 1) * 128], g.ident_b)
                cp(g, "act", oT[:, half * 3:half * 3 + 3, :], pto)
            if LIM < 7:
                x_store(g, i, xt)
                continue
            if samp:
                mem_attn_sample(g, S, l, qmT, oT[:, 6:8, :])
            else:
                mem_attn_prompt(g, S, qmT, kTm, Vpm, oT[:, 6:8, :])
            if LIM < 8:
                x_store(g, i, xt)
                continue
            for nb in range(2):
                po = psum(g, [128, 512])
                for k in range(8):
                    mm(g, po, oT[:, k, :], w_o[:, k, nb * 512:(nb + 1) * 512], start=(k == 0), stop=(k == 7))
                tt(g, "dve", xt[:, nb * 512:(nb + 1) * 512], xt[:, nb * 512:(nb + 1) * 512], po, ALU.add)
            x_store(g, i, xt)
    g.kb.barrier()


def otk_f(g, S, otk):
    if not hasattr(S, "otkf"):
        S.otkf = S.sb("otkf", [128, 768], F32)
    cp(g, "dve", S.otkf, otk)
    return S.otkf

def gather(g, out, rows_ap, idx_view):
    g.kb.op("pool", lambda e: e.indirect_dma_start(out=out.ap, out_offset=None, in_=rows_ap,
                                                    in_offset=bass.IndirectOffsetOnAxis(ap=idx_view.ap, axis=0)),
            reads=[idx_view], writes=[out], dma=True)


def latent_phase(g, SB):
    I, O = g.I, g.O
    SB.ckv_tok = SB.sb("ckv_tok", [128, NT, 256], BF16)
    SB.ckvT = SB.sb("ckvT", [128, 2, NT * 128], BF16)
    SB.kpeT = SB.sb("kpeT", [64, NT * 128], BF16)
    with Scope(g, "lat") as S:
        S.nstg = 4
        wkva = load_w_bf16(g, S, "wkva", I["w_kv_a"], D, 384)
        gT = load_vec_T(g, S, "gkv", I["norm_kv"], D)
        gck = load_bcast(g, S, "gck", I["norm_ckv"], 256)
        rt = S.sb("rt", [128, NT, 2, 64], F32)
        dma(g, rt.re("p i a d -> p i (a d)"), dv(I["c_rope_tok"].rearrange("i p a d -> p i (a d)")))
        S.tmp_sq = S.sb("sq", [128, D], BF16)
        S.tmp_ss = S.sb("ss", [128, 1], F32)
        S.tmp_xn = S.sb("xn", [128, D], BF16)
        xkT = S.sb("xkT", [128, 8, 128], BF16)
        hh_ = S.sb("h", [128, 384], F32)
        ck = [S.sb("ck%d" % k, [128, 256], F32) for k in range(2)]
        kp = [S.sb("kp%d" % k, [128, 64], F32) for k in range(2)]
        kpb = S.sb("kpb", [128, 64], BF16)
        t64 = S.sb("t64", [128, 64], F32)
        ss2 = S.sb("ss2", [128, 1], F32)
        junk = S.sb("junk", [128, 256], BF16)
        for i in range(NT):
            xt = x_load(g, S, i)
            rmsnorm_T(g, S, xt, gT, xkT, "k")
            ph = psum(g, [128, 384])
            for k in range(8):
                mm(g, ph, xkT[:, k, :], wkva[:, k, :], start=(k == 0), stop=(k == 7))
            cp(g, "act", hh_, ph)
            act(g, junk, hh_[:, 0:256], AF.Square, accum=ss2)
            ts(g, "dve", ss2, ss2, 1.0 / 256, ALU.mult, RMS_EPS, ALU.add)
            act(g, ss2, ss2, AF.Sqrt)
            recip(g, ss2, ss2)
            c_ = ck[i % 2]
            stt(g, c_, hh_[:, 0:256], ss2, gck, ALU.mult, ALU.mult)
            dma(g, dv(O["ckv"][i]), c_)
            cp(g, "pool", SB.ckv_tok[:, i, :], c_)
            k_ = kp[i % 2]
            tt(g, "dve", k_, hh_[:, 256:320], rt[:, i, 0, :], ALU.mult)
            tt(g, "dve", t64, hh_[:, 320:384], rt[:, i, 1, :], ALU.mult)
            tt(g, "dve", k_, k_, t64, ALU.add)
            dma(g, dv(O["kpe"][i]), k_)
            cp(g, "pool", kpb, k_)
            pt = psum(g, [128, 3, 128], BF16)
            for cc in range(2):
                tr(g, pt[:, cc, :], SB.ckv_tok[:, i, cc * 128:(cc + 1) * 128], g.ident_b)
            tr(g, pt[0:64, 2, :], kpb, g.ident_b)
            cp(g, "act", SB.ckvT[:, :, i * 128:(i + 1) * 128], pt[:, 0:2, :])
            cp(g, "act", SB.kpeT[:, i * 128:(i + 1) * 128], pt[0:64, 2, :])
    g.kb.barrier()


def mixer_b_phase(g, l, SB):
    I, O = g.I, g.O
    j_ = l - 2
    import os
    tiles = [int(t) for t in os.environ["DBG_TILES"].split(",") if t] if "DBG_TILES" in os.environ else list(range(NT))
    with Scope(g, "b%d" % l) as S:
        kTm, Vpm = mem_prep(g, S, l)
        S.nxb = 2
        S.nstg = 4
        w_in = load_w_bf16(g, S, "w_inb", I["w_in_b"][j_], D, 512)
        wq = load_w_bf16(g, S, "wq", I["w_q_b"][j_], 256, 1536)
        wuk = load_w_bf16(g, S, "wuk", I["wukT"].rearrange("d h c -> d (h c)"), 128, 1536)
        wuv = load_w_bf16(g, S, "wuv", I["wuv"].rearrange("c h v -> c (h v)"), 256, 768)
        w_o = load_w_bf16(g, S, "w_o", I["w_o"][l], D, D)
        gT = load_vec_T(g, S, "gmix", I["norm_mix"][l], D)
        gq = load_vec_T(g, S, "gq", I["norm_q"][j_], 256)
        rT = S.sb("rT", [64, NT, 2, 128], BF16)
        dma(g, rT.re("p i a t -> p i (a t)"), dv(I["c_rope_T"].rearrange("p i a t -> p i (a t)")))
        S.tmp_sq = S.sb("sq", [128, D], BF16)
        S.tmp_ss = S.sb("ss", [128, 1], F32)
        S.tmp_xn = S.sb("xn", [128, D], BF16)
        t1 = S.sb("t1", [64, 128], F32)
        t2 = S.sb("t2", [64, 128], F32)
        PTb = [S.sb("PTb%d" % k, [128, 4, 128], BF16) for k in range(2)]
        rs2 = [S.sb("rs%d" % k, [128, 128], F32) for k in range(2)]
        olat2 = [S.sb("olat%d" % k, [128, 2, 128], BF16) for k in range(2)]
        oT = S.sb("oT", [128, 8, 128], BF16)
        ss2 = S.sb("ss2", [128, 1], F32)
        junk = S.sb("junk", [128, 256], BF16)
        ones_b = S.sb("ones_b", [128, 128], BF16)
        memset(g, "pool", ones_b, 1.0)
        qlat_s = S.sb("qlat_s", [128, 2, 16, 48], BF16)
        qpe_s = S.sb("qpe_s", [64, 16, 48], BF16)
        GK = [S.sb("GK%d" % k, [128, 8, 256], BF16) for k in range(2)]
        GP = [S.sb("GP%d" % k, [128, 8, 64], BF16) for k in range(2)]
        KT = [S.sb("KT%d" % k, [128, 2, 3, 128], BF16) for k in range(2)]
        PTs = [S.sb("PTs%d" % k, [128, 8, 48], BF16) for k in range(2)]
        OL = S.sb("OL", [128, 3, 16, 48], F32)
        rsS = S.sb("rsS", [128, 16, 48], F32)
        olat_s = S.sb("olat_s", [128, 2, 16, 48], BF16)
        idx = S.sb("idx", [128, 16, 8], I32)
        ptb = S.sb("ptb", [128, 16, 8], I32)
        sub16 = S.sb("sub16", [128, 1], F32)
        dma(g, ptb, dv(I["ptb"]))
        dma(g, sub16, dv(I["c_sub16"]))
        ts(g, "dve", idx, ptb, 16.0, ALU.mult, sub16[:, 0:1], ALU.add)
        mem_attn_alloc(g, S)
        rows_ckv = I["cache_ckv"].rearrange("n (s t) d -> (n s) (t d)", s=16)
        rows_kpe = I["cache_kpe"].rearrange("n (s t) d -> (n s) (t d)", s=16)

        class QS:
            pass
        Qs = []
        for k_ in range(2):
            Q_ = QS()
            Q_.xnT = S.sb("xnT%d" % k_, [128, 8, 128], BF16)
            Q_.qa = S.sb("qa%d" % k_, [128, 256], F32)
            Q_.qn = S.sb("qn%d" % k_, [128, 256], BF16)
            Q_.qnT = S.sb("qnT%d" % k_, [128, 2, 128], BF16)
            Q_.qmT = S.sb("qmT%d" % k_, [128, 2, 128], BF16)
            Q_.qnope = S.sb("qnope%d" % k_, [128, 6, 128], BF16)
            Q_.qpe = S.sb("qpe%d" % k_, [64, 6, 128], BF16)
            Q_.qlat = S.sb("qlat%d" % k_, [128, 6, 2, 128], BF16)
            Qs.append(Q_)

        def pro(i, Q):
            xnT, qa, qn, qnT, qmT, qnope, qpe, qlat = Q.xnT, Q.qa, Q.qn, Q.qnT, Q.qmT, Q.qnope, Q.qpe, Q.qlat
            Q.xt = x_load(g, S, i)
            xt = Q.xt
            rmsnorm_T(g, S, xt, gT, xnT, "b")
            yield
            pq = psum(g, [128, 256])
            for k in range(8):
                mm(g, pq, xnT[:, k, :], w_in[:, k, 0:256], start=(k == 0), stop=(k == 7))
            cp(g, "act", qa, pq)
            act(g, junk, qa, AF.Square, accum=ss2)
            ts(g, "dve", ss2, ss2, 1.0 / 256, ALU.mult, RMS_EPS, ALU.add)
            act(g, ss2, ss2, AF.Sqrt)
            recip(g, ss2, ss2)
            ts(g, "dve", qn, qa, ss2, ALU.mult)
            pt = psum(g, [128, 2, 128], BF16)
            for c in range(2):
                tr(g, pt[:, c, :], qn[:, c * 128:(c + 1) * 128], g.ident_b)
            for c in range(2):
                ts(g, "dve", qnT[:, c, :], pt[:, c, :], gq[:, c:c + 1], ALU.mult)
            pm = psum(g, [128, 2, 128])
            for c in range(2):
                for k in range(8):
                    mm(g, pm[:, c, :], w_in[:, k, 256 + c * 128:256 + (c + 1) * 128], xnT[:, k, :], start=(k == 0), stop=(k == 7))
            cp(g, "act", qmT, pm)
            yield
            for h in range(6):
                ph = psum(g, [128, 3, 128])
                for kc in range(2):
                    mm(g, ph[:, 0, :], wq[:, kc, h * 192:h * 192 + 128], qnT[:, kc, :], start=(kc == 0), stop=(kc == 1))
                for kc in range(2):
                    mm(g, ph[0:64, 1, :], wq[:, kc, h * 192 + 128:h * 192 + 192], qnT[:, kc, :], start=(kc == 0), stop=(kc == 1))
                for kc in range(2):
                    mm(g, ph[0:64, 2, :], wq[:, kc, 1152 + h * 64:1152 + (h + 1) * 64], qnT[:, kc, :], start=(kc == 0), stop=(kc == 1))
                cp(g, "act", qnope[:, h, :], ph[:, 0, :])
                tt(g, "dve", t1, ph[0:64, 1, :], rT[:, i, 0, :], ALU.mult)
                tt(g, "dve", t2, ph[0:64, 2, :], rT[:, i, 1, :], ALU.mult)
                tt(g, "dve", qpe[:, h, :], t1, t2, ALU.add)
                if h % 2 == 1:
                    yield
            for h in range(6):
                pl = psum(g, [128, 2, 128])
                for cc in range(2):
                    mm(g, pl[:, cc, :], wuk[:, 0, h * 256 + cc * 128:h * 256 + (cc + 1) * 128], qnope[:, h, :])
                cp(g, "act", qlat[:, h, :, :], pl)
            yield

        def drain(gen):
            if gen is not None:
                for _ in gen:
                    pass

        def step(gen):
            if gen is not None:
                next(gen, None)

        drain(pro(tiles[0], Qs[0]))
        for n_, i in enumerate(tiles):
            samp = (i == NPT)
            Q = Qs[n_ % 2]
            xt, qmT, qpe, qlat = Q.xt, Q.qmT, Q.qpe, Q.qlat
            nxt = pro(tiles[n_ + 1], Qs[(n_ + 1) % 2]) if n_ + 1 < len(tiles) else None
            if not samp:
                for h in range(6):
                    acc = psum_hold(g, h % 2, [128, 3, 128])
                    rs, olat = rs2[h % 2], olat2[h % 2]
                    nkt = i + 1
                    for k0 in range(0, nkt, 4):
                        kn = min(4, nkt - k0)
                        pss = psum(g, [128, 4, 128])
                        for kk in range(kn):
                            kt = k0 + kk
                            ks = slice(kt * 128, (kt + 1) * 128)
                            mm(g, pss[:, kk, :], SB.ckvT[:, 0, ks], qlat[:, h, 0, :], start=True, stop=False)
                            mm(g, pss[:, kk, :], SB.ckvT[:, 1, ks], qlat[:, h, 1, :], start=False, stop=False)
                            mm(g, pss[:, kk, :], SB.kpeT[:, ks], qpe[:, h, :], start=False, stop=True)
                        P_ = PTb[(k0 // 4) % 2]
                        act(g, P_[:, 0:kn, :], pss[:, 0:kn, :], AF.Exp, scale=ATTN_SCALE)
                        if k0 + kn == nkt:
                            tt(g, "dve", P_[:, kn - 1, :], P_[:, kn - 1, :], g.caus, ALU.mult)
                        for a in range(3):
                            for kk in range(kn):
                                kt = k0 + kk
                                first = (kt == 0)
                                lastk = (kt == nkt - 1)
                                st_ = first and a == 0
                                if a < 2:
                                    mm(g, acc[:, a, :], SB.ckv_tok[:, kt, a * 128:(a + 1) * 128], P_[:, kk, :], start=st_, stop=lastk)
                                else:
                                    mm(g, acc[:, 2, :], ones_b, P_[:, kk, :], start=st_, stop=lastk)
                    recip(g, rs, acc[:, 2, :])
                    tt(g, "dve", olat, acc[:, 0:2, :], rs.re("p (o q) -> p o q", o=1).bc([128, 2, 128]), ALU.mult)
                    po_ = psum(g, [128, 128])
                    for cc in range(2):
                        mm(g, po_, wuv[:, cc, h * 128:(h + 1) * 128], olat[:, cc, :], start=(cc == 0), stop=(cc == 1))
                    cp(g, "act", oT[:, h, :], po_)
                    step(nxt)
            else:
                for h in range(6):
                    cp(g, "pool", qlat_s[:, :, :, h * 8:(h + 1) * 8], qlat[:, h, :, :].re("p c (j t) -> p c j t", t=8))
                    cp(g, "pool", qpe_s[:, :, h * 8:(h + 1) * 8], qpe[:, h, :].re("p (j t) -> p j t", t=8))
                for j in range(16):
                    acc = psum_hold(g, j % 2, [128, 3, 48])
                    for gi in range(8):
                        gk, gp = GK[gi % 2], GP[gi % 2]
                        gather(g, gk.re("p u d -> p (u d)"), rows_ckv, idx[:, j, gi:gi + 1])
                        gather(g, gp.re("p u d -> p (u d)"), rows_kpe, idx[:, j, gi:gi + 1])
                        pss = psum(g, [128, 8, 48])
                        for u2 in range(4):
                            kt_ = KT[u2 % 2]
                            ptk = psum(g, [128, 2, 3, 128], BF16)
                            for uu in range(2):
                                u = u2 * 2 + uu
                                for cc in range(2):
                                    tr(g, ptk[:, uu, cc, :], gk[:, u, cc * 128:(cc + 1) * 128], g.ident_b)
                                tr(g, ptk[0:64, uu, 2, :], gp[:, u, :], g.ident_b)
                            cp(g, "act", kt_[:, :, 0:2, :], ptk[:, :, 0:2, :])
                            cp(g, "act", kt_[0:64, :, 2, :], ptk[0:64, :, 2, :])
                            for uu in range(2):
                                u = u2 * 2 + uu
                                mm(g, pss[:, u, :], kt_[:, uu, 0, :], qlat_s[:, 0, j, :], start=True, stop=False)
                                mm(g, pss[:, u, :], kt_[:, uu, 1, :], qlat_s[:, 1, j, :], start=False, stop=False)
                                mm(g, pss[:, u, :], kt_[0:64, uu, 2, :], qpe_s[:, j, :], start=False, stop=True)
                        P_ = PTs[gi % 2]
                        act(g, P_, pss, AF.Exp, scale=ATTN_SCALE)
                        for u in range(8):
                            first = (gi == 0 and u == 0)
                            lastk = (gi == 7 and u == 7)
                            for cc in range(2):
                                mm(g, acc[:, cc, :], gk[:, u, cc * 128:(cc + 1) * 128], P_[:, u, :], start=(first and cc == 0), stop=lastk)
                            mm(g, acc[:, 2, :], ones_b, P_[:, u, :], start=False, stop=lastk)
                    cp(g, "act", OL[:, :, j, :], acc)
                ks = slice(NPT * 128, NT * 128)
                for h in range(6):
                    pss = psum(g, [128, 128])
                    mm(g, pss, SB.ckvT[:, 0, ks], qlat[:, h, 0, :], start=True, stop=False)
                    mm(g, pss, SB.ckvT[:, 1, ks], qlat[:, h, 1, :], start=False, stop=False)
                    mm(g, pss, SB.kpeT[:, ks], qpe[:, h, :], start=False, stop=True)
                    P_ = PTb[h % 2]
                    act(g, P_[:, 0, :], pss, AF.Exp, scale=ATTN_SCALE)
                    tt(g, "dve", P_[:, 0, :], P_[:, 0, :], g.mi_s, ALU.mult)
                    pn = psum(g, [128, 3, 128])
                    for cc in range(2):
                        mm(g, pn[:, cc, :], SB.ckv_tok[:, NPT, cc * 128:(cc + 1) * 128], P_[:, 0, :])
                    mm(g, pn[:, 2, :], ones_b, P_[:, 0, :])
                    olv = OL[:, :, :, h * 8:(h + 1) * 8]
                    for a in range(3):
                        tt(g, "dve", OL[:, a, :, h * 8:(h + 1) * 8], OL[:, a, :, h * 8:(h + 1) * 8], pn[:, a, :].re("p (j t) -> p j t", t=8), ALU.add)
                recip(g, rsS, OL[:, 2, :, :])
                for cc in range(2):
                    tt(g, "dve", olat_s[:, cc, :, :], OL[:, cc, :, :], rsS, ALU.mult)
                for h in range(6):
                    po_ = psum(g, [128, 16, 8])
                    for cc in range(2):
                        mm(g, po_, wuv[:, cc, h * 128:(h + 1) * 128], olat_s[:, cc, :, h * 8:(h + 1) * 8], start=(cc == 0), stop=(cc == 1))
                    cp(g, "act", oT[:, h, :], po_.re("p j t -> p (j t)"))
            if samp:
                mem_attn_sample(g, S, l, qmT, oT[:, 6:8, :])
            else:
                mem_attn_prompt(g, S, qmT, kTm, Vpm, oT[:, 6:8, :])
            for nb in range(2):
                po = psum(g, [128, 512])
                for k in range(8):
                    mm(g, po, oT[:, k, :], w_o[:, k, nb * 512:(nb + 1) * 512], start=(k == 0), stop=(k == 7))
                tt(g, "dve", xt[:, nb * 512:(nb + 1) * 512], xt[:, nb * 512:(nb + 1) * 512], po, ALU.add)
            x_store(g, i, xt)
            drain(nxt)
    g.kb.barrier()


IN_SPECS = None
USE_CACHE = True


def build_program(n_phys, mode="full", dbg=None):
    nc = bass.Bass("TRN2", target_bir_lowering=False)
    g = G()
    g.nc = nc
    g.kb = KB(nc)
    g.dbg = dbg or {}
    I = {}
    O = {}

    def din(name, shape, dt=F32):
        I[name] = nc.dram_tensor(name, list(shape), dt, kind="ExternalInput").ap()

    def dout(name, shape, dt=F32):
        O[name] = nc.dram_tensor("o_" + name, list(shape), dt, kind="ExternalOutput").ap()

    din("xin", [NT, 128, D])
    if USE_CACHE:
        din("cache_ckv", [n_phys, 128, 256])
        din("cache_kpe", [n_phys, 128, 64])
        din("ptb", [128, 16, 8], I32)
    din("cache_mem_k", [4, 16, NMEM, 256])
    din("cache_mem_v", [4, 16, NMEM, 256])
    din("state_wkv", [2, 16, 12, 64, 64])
    din("state_shift", [2, 16, D])
    din("state_conv", [4, 16, 2, DFF])
    din("mem_prompt", [NMEM, D])
    for nm, shp in [("norm_mix", [4, D]), ("norm_ffn", [4, D]), ("norm_mem", [4, D]), ("w_mem_kv", [4, D, 512]),
                    ("w_o", [4, D, D]), ("w_in_a", [2, D, C_A]), ("mu_a", [2, C_RWKV]), ("w_vres_in", [1, D, 32]),
                    ("mu_vres", [1, 32]), ("w_decay_up", [2, 64, 768]), ("w0", [2, 768]), ("w_a_up", [2, 64, 768]),
                    ("a0", [2, 768]), ("w_g_up", [2, 160, 768]), ("w_vres_up", [1, 32, 768]), ("v0", [1, 768]),
                    ("k_k", [2, 768]), ("k_a", [2, 768]), ("r_k", [2, 768]), ("lnx_w", [2, 768]), ("lnx_b", [2, 768]),
                    ("norm_kv", [D]), ("w_kv_a", [D, 384]), ("norm_ckv", [256]), ("wukT", [128, 6, 256]),
                    ("wuv", [256, 6, 128]), ("w_in_b", [2, D, 512]), ("norm_q", [2, 256]), ("w_q_b", [2, 256, 1536]),
                    ("w_ffn_up", [4, D, 2 * DFF]), ("conv_w", [4, 3, DFF]), ("conv_b", [4, DFF]),
                    ("w_ffn_down", [4, DFF, D]), ("final_norm", [D])]:
        din(nm, shp)
    din("c_ident_b", [128, 128], BF16)
    din("c_ident_f", [128, 128])
    din("c_masks", [128, 6, 128])
    din("c_tri", [128, 2, 3, 128])
    din("c_bd2", [128, 128])
    din("c_ones_half", [128, 2, 128], BF16)
    din("c_seqmask", [128, 18])
    din("c_caus", [128, 128])
    din("c_sub16", [128, 1])
    din("c_rope_tok", [NT, 128, 2, 64])
    din("c_rope_T", [64, NT, 2, 128], BF16)
    dout("y", [NT, 128, D])
    dout("ckv", [NT, 128, 256])
    dout("kpe", [NT, 128, 64])
    dout("memk", [4, NMEM, 256])
    dout("memv", [4, NMEM, 256])
    dout("wkv_p", [2, 12, 64, 64])
    dout("shift_p", [2, D])
    dout("conv_p", [4, 2, DFF])
    dout("wkv_s", [2, 16, 12, 64, 64])
    dout("shift_s", [2, 16, D])
    dout("conv_s", [4, 16, 2, DFF])
    g.taps = g.dbg
    for k, shp in g.dbg.items():
        dout("dbg_" + k, shp)
    vf = nc.dram_tensor("scr_vf", [NT, 128, 6, 128], BF16, kind="Internal").ap()
    g.vf_scr = [V(vf[i], Buf("vf_%d" % i)) for i in range(NT)]
    g.I = I
    g.O = O
    xn2 = nc.dram_tensor("scr_xn2", [NT, 128, 8, 128], BF16, kind="Internal").ap()
    g.xn2_scr = [V(xn2[i], Buf("xn2_%d" % i)) for i in range(NT)]

    with contextlib.ExitStack() as st:
        xs_d = nc.dram_tensor("scr_x", [NT, 128, D], F32, kind="Internal").ap()
        g.x_scr = [V(xs_d[i], Buf("x%d" % i)) for i in range(NT)]
        g.x_src = [V(I["xin"][i], None) for i in range(NT)]
        ib = st.enter_context(nc.sbuf_tensor("identb", [128, 128], BF16))
        g.ident_b = V(ib[:], Buf("identb"))
        idf = st.enter_context(nc.sbuf_tensor("identf", [128, 128], F32))
        g.ident_f = V(idf[:], Buf("identf"))
        g.ps_f = []
        g.ps_b = []
        for k in range(8):
            ph = st.enter_context(nc.psum_tensor("psf%d" % k, [128, 512], F32))
            b = Buf("ps%d" % k, excl=True)
            g.ps_f.append(V(ph[:], b))
            g.ps_b.append(V(ph[:].bitcast(BF16), b))
        g.ps_i = 0
        dma(g, g.ident_b, dv(I["c_ident_b"]))
        dma(g, g.ident_f, dv(I["c_ident_f"]))

        def const(name, shape, dt, src):
            h = st.enter_context(nc.sbuf_tensor("k_" + name, list(shape), dt))
            v = V(h[:], Buf(name))
            dma(g, v, dv(src))
            return v
        mk = const("masks", [128, 6, 128], F32, I["c_masks"])
        g.ms_p, g.mi_p, g.mst_p, g.ms_s, g.mi_s, g.mst_s = [mk[:, k, :] for k in range(6)]
        tri = const("tri", [128, 2, 3, 128], F32, I["c_tri"])
        g.tri_p, g.tri_s = tri[:, 0], tri[:, 1]
        g.bd2_f = const("bd2f", [128, 128], F32, I["c_bd2"])
        g.bd2_b = const("bd2b", [128, 128], BF16, I["c_ident_b"])
        cp(g, "pool", g.bd2_b, g.bd2_f)
        g.ones_half = const("onesh", [128, 2, 128], BF16, I["c_ones_half"])
        sm_ = const("seqm", [128, 18], F32, I["c_seqmask"])
        g.seqmask = sm_[:, 0:16]
        g.blkmask = sm_[:, 16:18]
        g.caus = const("caus", [128, 128], F32, I["c_caus"])
        stages = g.dbg.get("_stages", None)
        import os
        if mode == "ffn":
            for l in range(int(os.environ.get("DBG_NL", "4"))):
                ffn_phase(g, l, 0)
                ffn_phase(g, l, 1)
        elif mode == "a0":
            mixer_a_phase(g, 0)
        elif mode in ("full", "b2"):
            nl = int(os.environ.get("DBG_NL", "4"))
            if mode == "full":
                for l in range(min(nl, 2)):
                    mixer_a_phase(g, l)
                    ffn_phase(g, l, 0)
                    ffn_phase(g, l, 1)
            if nl > 2 or mode == "b2":
                with Scope(g, "B") as SB:
                    latent_phase(g, SB)
                    for l in range(2, nl if mode == "full" else 3):
                        mixer_b_phase(g, l, SB)
                        if mode == "full":
                            ffn_phase(g, l, 0)
                            ffn_phase(g, l, 1)
        final_phase(g)
        g.kb.finalize()
    return nc, g


def _bf(a):
    return np.ascontiguousarray(a).astype(ml_dtypes.bfloat16)


def core_inputs(a, c_unused, n_phys):
    f = lambda v: np.ascontiguousarray(v, dtype=np.float32)
    m = {}
    xin = np.concatenate([a["x_prompt"].reshape(NPT, 128, D), a["x_sample"].reshape(1, 128, D)], 0)
    m["xin"] = f(xin)
    if USE_CACHE:
        m["cache_ckv"] = f(a["cache_ckv"])
        m["cache_kpe"] = f(a["cache_kpe"])
        pt = np.asarray(a["page_table"]).astype(np.int32)
        ptb = np.zeros((128, 16, 8), np.int32)
        for p in range(128):
            ptb[p] = pt.reshape(16, 8, 8)[:, :, p // 16]
        m["ptb"] = ptb
    m["cache_mem_k"] = f(a["cache_mem_k"]).reshape(4, 16, NMEM, 256)
    m["cache_mem_v"] = f(a["cache_mem_v"]).reshape(4, 16, NMEM, 256)
    m["state_wkv"] = f(a["state_wkv"])
    m["state_shift"] = f(a["state_shift"])
    m["state_conv"] = f(a["state_conv"])
    m["mem_prompt"] = f(a["mem_prompt"]).reshape(NMEM, D)
    for nm in ["norm_mix", "norm_ffn", "norm_mem", "w_mem_kv", "w_o", "w_in_a", "mu_a", "w_vres_in", "mu_vres",
               "w_decay_up", "w0", "w_a_up", "a0", "w_g_up", "w_vres_up", "v0", "k_k", "k_a", "lnx_w", "lnx_b",
               "norm_kv", "norm_ckv", "w_in_b", "norm_q", "w_ffn_up", "conv_w", "conv_b", "w_ffn_down", "final_norm"]:
        m[nm] = f(a[nm])
    m["r_k"] = f(a["r_k"]).reshape(2, 768)
    wkvb = f(a["w_kv_b"])
    m["wukT"] = np.ascontiguousarray(wkvb[:, :, :128].transpose(2, 1, 0))
    m["wuv"] = np.ascontiguousarray(wkvb[:, :, 128:])
    wkva = f(a["w_kv_a"])
    sw = np.concatenate([wkva[:, 288:320], wkva[:, 256:288]], 1)
    m["w_kv_a"] = np.ascontiguousarray(np.concatenate([wkva, sw], 1))
    wq = f(a["w_q_b"]).reshape(2, 256, 6, 192)
    wq_sw = np.concatenate([wq[..., 160:192], wq[..., 128:160]], -1)
    m["w_q_b"] = np.ascontiguousarray(np.concatenate([wq.reshape(2, 256, 1152), wq_sw.reshape(2, 256, 384)], -1))
    m["c_ident_b"] = _bf(np.eye(128, dtype=np.float32))
    m["c_ident_f"] = np.eye(128, dtype=np.float32)
    idx = np.arange(128)
    masks = np.zeros((128, 6, 128), np.float32)
    tri = np.zeros((128, 2, 3, 128), np.float32)
    for gi, blk in enumerate([64, 8]):
        same = (idx[:, None] // blk) == (idx[None, :] // blk)
        lt = idx[:, None] < idx[None, :]
        le = idx[:, None] <= idx[None, :]
        gt = idx[:, None] > idx[None, :]
        masks[:, gi * 3 + 0, :] = same & lt
        masks[:, gi * 3 + 1, :] = same & le
        masks[:, gi * 3 + 2, :] = same & gt
        tri[:, gi, 0, :] = same & le
        tri[:, gi, 1, :] = same & lt
        tri[:, gi, 2, :] = same & gt
    m["c_masks"] = masks
    m["c_tri"] = tri
    m["c_bd2"] = ((idx[:, None] // 64) == (idx[None, :] // 64)).astype(np.float32)
    oh = np.zeros((128, 2, 128), np.float32)
    oh[:, 0, 0:64] = 1.0
    oh[:, 1, 64:128] = 1.0
    m["c_ones_half"] = _bf(oh)
    m["c_caus"] = (idx[:, None] <= idx[None, :]).astype(np.float32)
    m["c_sub16"] = (idx % 16).astype(np.float32).reshape(128, 1)
    pos = np.zeros((NT, 128), np.float32)
    for i in range(NPT):
        pos[i] = i * 128 + idx
    pos[NPT] = 8192 + (idx % 8)
    inv = (np.float32(10000.0) ** (-np.arange(32, dtype=np.float32) / np.float32(32))).astype(np.float32)
    ang = (pos[:, :, None] * inv[None, None, :]).astype(np.float32).astype(np.float64)
    cs, sn = np.cos(ang), np.sin(ang)
    C = np.concatenate([cs, cs], -1)
    Sg = np.concatenate([-sn, sn], -1)
    rtok = np.stack([C, Sg], 2).astype(np.float32)
    m["c_rope_tok"] = rtok
    m["c_rope_T"] = _bf(rtok.transpose(3, 0, 2, 1))
    m["c_seqmask"] = np.concatenate([((idx[:, None] // 8) == np.arange(16)[None, :]), ((idx[:, None] // 64) == np.arange(2)[None, :])], 1).astype(np.float32)
    return m


def kernel(**inputs):
    n_cores = 8
    n_phys = int(np.asarray(inputs["cache_ckv"]).shape[0])
    nc, g = build_program(n_phys, mode="full")
    in_maps = []
    for c in range(n_cores):
        sl = slice(16 * c, 16 * c + 16)
        a = dict(inputs)
        a["x_prompt"] = np.asarray(inputs["x_prompt"])[c:c + 1]
        a["mem_prompt"] = np.asarray(inputs["mem_prompt"])[c:c + 1]
        a["x_sample"] = np.asarray(inputs["x_sample"])[sl]
        a["cache_mem_k"] = np.asarray(inputs["cache_mem_k"])[:, sl]
        a["cache_mem_v"] = np.asarray(inputs["cache_mem_v"])[:, sl]
        a["state_wkv"] = np.asarray(inputs["state_wkv"])[:, sl]
        a["state_shift"] = np.asarray(inputs["state_shift"])[:, sl]
        a["state_conv"] = np.asarray(inputs["state_conv"])[:, sl]
        a["page_table"] = np.asarray(inputs["page_table"])[sl]
        in_maps.append(core_inputs(a, c, n_phys))
    res = run_bass_kernel_spmd(nc, in_maps, core_ids=list(range(n_cores)))
    R = res.results
    cat = lambda key, f: np.stack([f(r["o_" + key]) for r in R], 0)
    y_p = cat("y", lambda v: v[:NPT].reshape(2048, D))
    y_s = np.concatenate([r["o_y"][NPT].reshape(16, 8, D) for r in R], 0)
    ckv_p = cat("ckv", lambda v: v[:NPT].reshape(2048, 256))
    kpe_p = cat("kpe", lambda v: v[:NPT].reshape(2048, 64))
    mem_k_p = np.stack([r["o_memk"].reshape(4, NMEM, 4, 64) for r in R], 1)
    mem_v_p = np.stack([r["o_memv"].reshape(4, NMEM, 4, 64) for r in R], 1)
    wkv_p = np.stack([r["o_wkv_p"] for r in R], 1)
    shift_p = np.stack([r["o_shift_p"] for r in R], 1)
    conv_p = np.stack([r["o_conv_p"] for r in R], 1)
    ckv_s = np.concatenate([r["o_ckv"][NPT].reshape(16, 8, 256) for r in R], 0)
    kpe_s = np.concatenate([r["o_kpe"][NPT].reshape(16, 8, 64) for r in R], 0)
    wkv_s = np.concatenate([r["o_wkv_s"] for r in R], 1)
    shift_s = np.concatenate([r["o_shift_s"] for r in R], 1)
    conv_s = np.concatenate([r["o_conv_s"] for r in R], 1)
    outs = (y_p, y_s, ckv_p, kpe_p, mem_k_p, mem_v_p, wkv_p, shift_p, conv_p, ckv_s, kpe_s, wkv_s, shift_s, conv_s)
    return tuple(np.ascontiguousarray(o, dtype=np.float32) for o in outs)
```
